# Optimizing a Trainium2 kernel written in Bass

```python
import jax, jax.numpy as jnp
from jax import lax
import numpy as np

D_MODEL = 1024
BATCH = 16
SEQ = 2048
DEPTH = 1
DEC_BATCH = 8
DEC_SEQ = 4096
PAST_LEN = 128

HEAD_DIM = 64
N_HEADS_A = D_MODEL // (2 * HEAD_DIM)
N_HEADS_B = D_MODEL // (2 * HEAD_DIM)
N_KV_B = 2
GQA_GROUP = N_HEADS_B // N_KV_B
WIDTH_A = N_HEADS_A * HEAD_DIM
WIDTH_B = N_HEADS_B * HEAD_DIM
MIX_WIDTH = WIDTH_A + WIDTH_B
KV_WIDTH_B = N_KV_B * HEAD_DIM
IN_COLS = 3 * WIDTH_A + WIDTH_B + 2 * KV_WIDTH_B
DILATED_BRANCHES = ((128, 1), (512, 4), (2048, 16))
WINDOW_B = 128
BLOCK_B = 128
ROPE_THETA = 500000.0
ROPE_DIM = HEAD_DIM // 4
D_FF = 4 * D_MODEL
EPS = 1e-6
NEG_BIG = -1e30

kernel_name = "hymba_dilated_swa_encoder"


def rmsnorm(x, g):
    xf = x.astype(jnp.float32)
    y = xf * lax.rsqrt(jnp.mean(xf * xf, axis=-1, keepdims=True) + EPS)
    return (y * g.astype(jnp.float32)).astype(x.dtype)


def partial_rope(x, pos):
    half = ROPE_DIM // 2
    inv = ROPE_THETA ** (-jnp.arange(half, dtype=jnp.float32) * 2.0 / ROPE_DIM)
    ang = pos[:, None] * inv[None, :]
    cos = jnp.cos(ang)[None, :, None, :]
    sin = jnp.sin(ang)[None, :, None, :]
    xf = x.astype(jnp.float32)
    x1, x2, xp = xf[..., :half], xf[..., half:ROPE_DIM], xf[..., ROPE_DIM:]
    out = jnp.concatenate([x1 * cos - x2 * sin, x2 * cos + x1 * sin, xp], axis=-1)
    return out.astype(x.dtype)


def banded_attention(q, k, v, half_window, block, sink=None):
    N, L, Hq, dh = q.shape
    Hk = k.shape[2]
    G = Hq // Hk
    nb = -(-L // block)
    Lp = nb * block
    pad = Lp - L
    qb = jnp.pad(q, ((0, 0), (0, pad), (0, 0), (0, 0))).reshape(N, nb, block, Hk, G, dh)
    kp = jnp.pad(k, ((0, 0), (block, pad + block), (0, 0), (0, 0))).reshape(N, nb + 2, block, Hk, dh)
    vp = jnp.pad(v, ((0, 0), (block, pad + block), (0, 0), (0, 0))).reshape(N, nb + 2, block, Hk, dh)
    kw = jnp.concatenate([kp[:, :-2], kp[:, 1:-1], kp[:, 2:]], axis=2)
    vw = jnp.concatenate([vp[:, :-2], vp[:, 1:-1], vp[:, 2:]], axis=2)
    a = jnp.arange(block)[None, :, None]
    c = jnp.arange(3 * block)[None, None, :]
    j = jnp.arange(nb)[:, None, None]
    qpos = j * block + a
    kpos = j * block + c - block
    mask = (jnp.abs(qpos - kpos) <= half_window) & (kpos >= 0) & (kpos < L)
    s = jnp.einsum('nbqhgd,nbkhd->nbhgqk', qb, kw).astype(jnp.float32) * (dh ** -0.5)
    s = jnp.where(mask[None, :, None, None, :, :], s, NEG_BIG)
    m = jnp.max(s, axis=-1, keepdims=True)
    if sink is not None:
        sk = sink.astype(jnp.float32).reshape(1, 1, Hk, G, 1, 1)
        m = jnp.maximum(m, sk)
    p = jnp.exp(s - m)
    l = jnp.sum(p, axis=-1, keepdims=True)
    if sink is not None:
        l = l + jnp.exp(sk - m)
    o = jnp.einsum('nbhgqk,nbkhd->nbqhgd', p, vw.astype(jnp.float32))
    l_t = jnp.moveaxis(l[..., 0], 4, 2)
    lse = jnp.moveaxis((m + jnp.log(l))[..., 0], 4, 2)
    o = (o / l_t[..., None]).reshape(N, Lp, Hq, dh)[:, :L].astype(q.dtype)
    lse = lse.reshape(N, Lp, Hq)[:, :L]
    return o, lse


def to_dilated(t, d):
    B, S = t.shape[:2]
    t = jnp.moveaxis(t.reshape(B, S // d, d, *t.shape[2:]), 2, 1)
    return t.reshape(B * d, S // d, *t.shape[3:])


def from_dilated(t, d, B):
    Nd, M = t.shape[:2]
    t = jnp.moveaxis(t.reshape(B, d, M, *t.shape[2:]), 1, 2)
    return t.reshape(B, M * d, *t.shape[3:])


def dilated_attention(q, k, v):
    B = q.shape[0]
    outs, lses = [], []
    for window, dil in DILATED_BRANCHES:
        hw = window // (2 * dil)
        o, lse = banded_attention(to_dilated(q, dil), to_dilated(k, dil), to_dilated(v, dil), hw, hw)
        outs.append(from_dilated(o, dil, B))
        lses.append(from_dilated(lse, dil, B))
    wts = jax.nn.softmax(jnp.stack(lses, axis=0), axis=0)
    o = jnp.sum(wts[..., None] * jnp.stack(outs, axis=0).astype(jnp.float32), axis=0)
    return o.astype(q.dtype)


def encoder_layer(x, norm_attn, w_in, q_norm_a, k_norm_a, q_norm_b, k_norm_b, sink_b,
                  out_norm_a, out_norm_b, w_o, norm_mlp, w_up, w_down):
    B, S, _ = x.shape
    pos = jnp.arange(S, dtype=jnp.float32)
    xn = rmsnorm(x, norm_attn)
    h = xn @ w_in
    o0 = 0
    def take(width, heads):
        nonlocal o0
        t = h[..., o0:o0 + width].reshape(B, S, heads, HEAD_DIM)
        o0 += width
        return t
    q_a = take(WIDTH_A, N_HEADS_A)
    k_a = take(WIDTH_A, N_HEADS_A)
    v_a = take(WIDTH_A, N_HEADS_A)
    q_b = take(WIDTH_B, N_HEADS_B)
    k_b = take(KV_WIDTH_B, N_KV_B)
    v_b = take(KV_WIDTH_B, N_KV_B)
    q_a = partial_rope(rmsnorm(q_a, q_norm_a), pos)
    k_a = partial_rope(rmsnorm(k_a, k_norm_a), pos)
    q_b = partial_rope(rmsnorm(q_b, q_norm_b), pos)
    k_b = partial_rope(rmsnorm(k_b, k_norm_b), pos)
    o_a = dilated_attention(q_a, k_a, v_a).reshape(B, S, WIDTH_A)
    o_b, _ = banded_attention(q_b, k_b, v_b, WINDOW_B, BLOCK_B, sink=sink_b)
    o_b = o_b.reshape(B, S, WIDTH_B)
    mix = jnp.concatenate([rmsnorm(o_a, out_norm_a), rmsnorm(o_b, out_norm_b)], axis=-1)
    x = x + mix @ w_o
    hn = rmsnorm(x, norm_mlp)
    u = jax.nn.relu(hn @ w_up)
    return x + (u * u) @ w_down


def setup_inputs(seed: int = 0) -> dict:
    key = jax.random.key(seed)
    ks = jax.random.split(key, 16)
    f32 = jnp.float32
    def gain(k, n):
        return 1.0 + 0.01 * jax.random.normal(k, (DEPTH, n), f32)
    return {
        "x_prompt": jax.random.normal(ks[0], (BATCH, SEQ, D_MODEL), f32),
        "x_sample": jax.random.normal(ks[1], (DEC_BATCH, DEC_SEQ, D_MODEL), f32),
        "norm_attn": gain(ks[2], D_MODEL),
        "w_in": jax.random.normal(ks[3], (DEPTH, D_MODEL, IN_COLS), f32) * D_MODEL ** -0.5,
        "q_norm_a": gain(ks[4], HEAD_DIM),
        "k_norm_a": gain(ks[5], HEAD_DIM),
        "q_norm_b": gain(ks[6], HEAD_DIM),
        "k_norm_b": gain(ks[7], HEAD_DIM),
        "sink_b": 0.5 * jax.random.normal(ks[8], (DEPTH, N_HEADS_B), f32),
        "out_norm_a": gain(ks[9], WIDTH_A),
        "out_norm_b": gain(ks[10], WIDTH_B),
        "w_o": jax.random.normal(ks[11], (DEPTH, MIX_WIDTH, D_MODEL), f32) * MIX_WIDTH ** -0.5,
        "norm_mlp": gain(ks[12], D_MODEL),
        "w_up": jax.random.normal(ks[13], (DEPTH, D_MODEL, D_FF), f32) * D_MODEL ** -0.5,
        "w_down": jax.random.normal(ks[14], (DEPTH, D_FF, D_MODEL), f32) * D_FF ** -0.5,
    }


def reference(x_prompt, x_sample, norm_attn, w_in, q_norm_a, k_norm_a, q_norm_b, k_norm_b,
              sink_b, out_norm_a, out_norm_b, w_o, norm_mlp, w_up, w_down):
    yp = x_prompt
    ys = x_sample
    for l in range(DEPTH):
        params = (norm_attn[l], w_in[l], q_norm_a[l], k_norm_a[l], q_norm_b[l], k_norm_b[l],
                  sink_b[l], out_norm_a[l], out_norm_b[l], w_o[l], norm_mlp[l], w_up[l], w_down[l])
        yp = encoder_layer(yp, *params)
        ys = encoder_layer(ys, *params)
    return (yp, ys)
```

```python
import numpy as np
import ml_dtypes
import concourse.bass as bass
import concourse.mybir as mybir
from concourse.bass_utils import run_bass_kernel_spmd

F32 = mybir.dt.float32
BF16 = mybir.dt.bfloat16
AF = mybir.ActivationFunctionType
ALU = mybir.AluOpType
AX = mybir.AxisListType

D = 1024
HD = 64
IN_COLS = 2304
DFF = 4096
EPS = 1e-6
NQ = 16
PERM_HEADS = [0, 4, 1, 5, 2, 6, 3, 7]


class Buf:
    __slots__ = ("name", "w", "r", "dsem", "dcount", "excl")

    def __init__(self, name, excl=False):
        self.name = name
        self.excl = excl
        self.w = None
        self.r = {}
        self.dsem = None
        self.dcount = 0


class Eng:
    def __init__(self, key, sem):
        self.key = key
        self.sem = sem
        self.count = 0
        self.prog = []
        self.waited = {}


class Prog:
    def __init__(self, nc):
        self.nc = nc
        self.eng = {}
        for key in ("pe", "act", "dve", "pool", "sp"):
            self.eng[key] = Eng(key, nc.alloc_semaphore("s_" + key))
        self.bsem = nc.alloc_semaphore("s_bar")
        self.bcount = 0
        self.bufs = []
        self.dma_bufs = []
        self.fresh_dma_sems = False
        import os
        for j in range(int(os.environ.get("DUMMYSEM", "0"))):
            nc.alloc_semaphore("dummy%d" % j)

    def buf(self, name, excl=False):
        b = Buf(name, excl)
        self.bufs.append(b)
        return b

    def _wait(self, E, tick):
        sem, val = tick
        if E.key == "pe" and sem is E.sem:
            return
        if E.waited.get(sem, 0) >= val:
            return
        E.waited[sem] = val
        E.prog.append(("wait", sem, val))

    @staticmethod
    def _split(reads, writes):
        writes = list(writes)
        r2 = []
        for b in reads:
            if b.excl:
                if b not in writes:
                    writes.append(b)
            else:
                r2.append(b)
        return r2, writes

    def _deps(self, E, reads, writes):
        for b in reads:
            if b.w is not None:
                self._wait(E, b.w)
        for b in writes:
            if b.w is not None:
                self._wait(E, b.w)
            for s, v in b.r.items():
                self._wait(E, (s, v))

    def _commit(self, tick, reads, writes):
        for b in reads:
            b.r[tick[0]] = tick[1]
        for b in writes:
            b.w = tick
            b.r = {}

    def op(self, ek, name, kw, reads=(), writes=()):
        reads, writes = self._split(reads, writes)
        E = self.eng[ek]
        self._deps(E, reads, writes)
        E.count += 1
        tick = (E.sem, E.count)
        E.prog.append(("op", (name, kw), E.sem, 1))
        self._commit(tick, reads, writes)

    def group(self, ek, fns, reads=(), writes=()):
        reads, writes = self._split(reads, writes)
        E = self.eng[ek]
        self._deps(E, reads, writes)
        for fn in fns[:-1]:
            E.prog.append(("op", fn, None, 0))
        E.count += 1
        tick = (E.sem, E.count)
        E.prog.append(("op", fns[-1], E.sem, 1))
        self._commit(tick, reads, writes)

    def dma(self, ek, kw, primary, reads=(), writes=()):
        fn = ("dma_start", kw)
        E = self.eng[ek]
        self._deps(E, reads, writes)
        if primary.dsem is None:
            self.nsem = getattr(self, "nsem", 0) + 1
            primary.dsem = self.nc.alloc_semaphore("d%d_%s" % (self.nsem, primary.name))
            self.dma_bufs.append(primary)
        primary.dcount += 1
        tick = (primary.dsem, 16 * primary.dcount)
        E.prog.append(("op", fn, primary.dsem, 16))
        self._commit(tick, reads, writes)

    def barrier(self):
        sp = self.eng["sp"]
        for k in ("pe", "act", "dve", "pool"):
            E = self.eng[k]
            if E.count:
                self._wait(sp, (E.sem, E.count))
        for b in self.dma_bufs:
            self._wait(sp, (b.dsem, 16 * b.dcount))
        if self.fresh_dma_sems:
            for b in self.dma_bufs:
                b.dsem = None
                b.dcount = 0
            self.dma_bufs = []
        self.bcount += 1
        sp.prog.append(("inc", self.bsem, 1))
        for k in ("pe", "act", "dve", "pool"):
            self.eng[k].prog.append(("wait", self.bsem, self.bcount))
        for b in self.bufs:
            b.w = None
            b.r = {}

    def finish(self):
        sp = self.eng["sp"]
        for k in ("pe", "act", "dve", "pool"):
            E = self.eng[k]
            if E.count:
                self._wait(sp, (E.sem, E.count))
        for b in self.dma_bufs:
            self._wait(sp, (b.dsem, 16 * b.dcount))

    def emit(self):
        nc = self.nc

        def replay(E):
            def f(eng):
                for item in E.prog:
                    if item[0] == "wait":
                        eng.wait_ge(item[1], item[2])
                    elif item[0] == "inc":
                        eng.sem_inc(item[1], item[2])
                    elif item[0] == "clear":
                        eng.sem_clear(item[1])
                    else:
                        ins = getattr(eng, item[1][0])(**item[1][1])
                        if item[2] is not None:
                            ins.then_inc(item[2], item[3])
            return f

        with nc.Block() as block:
            block.sync(replay(self.eng["sp"]))
            block.gpsimd(replay(self.eng["pool"]))
            block.scalar(replay(self.eng["act"]))
            block.vector(replay(self.eng["dve"]))
            block.tensor(replay(self.eng["pe"]))


def _mask_tables():
    a = np.arange(128)[:, None]
    b = np.arange(128)[None, :]
    tabs = []
    idxA = {}
    for dl in range(-2, 3):
        diff = 128 * dl + a - b
        m = (np.abs(diff) <= 64).astype(np.float32)
        m += ((diff % 4 == 0) & (np.abs(diff) <= 256)).astype(np.float32)
        idxA[dl] = len(tabs)
        tabs.append(m)
    idxB = {}
    for dl in (-1, 1):
        diff = 128 * dl + a - b
        m = (np.abs(diff) <= 128).astype(np.float32)
        idxB[dl] = len(tabs)
        tabs.append(m)
    idxD = {}
    for off in (0, 128, -64, 64):
        m = (np.abs(off + a - b) <= 64).astype(np.float32)
        idxD[off] = len(tabs)
        tabs.append(m)
    return tabs, idxA, idxB, idxD


_MASKS, _IDXA, _IDXB, _IDXD = _mask_tables()
NM = len(_MASKS)


def _rope_tables():
    half = 8
    inv = 500000.0 ** (-(np.arange(half, dtype=np.float64) * 2.0 / 16.0))
    pos = np.arange(4096, dtype=np.float64)
    ang = pos[:, None] * inv[None, :]
    cos = np.cos(ang).astype(np.float32).reshape(32, 128, half).transpose(1, 0, 2).reshape(128, 32 * half)
    sin = np.sin(ang).astype(np.float32).reshape(32, 128, half).transpose(1, 0, 2).reshape(128, 32 * half)
    return np.ascontiguousarray(cos), np.ascontiguousarray(sin)


def build_program(groups, n_p, len_p, n_s, len_s, stop=None, n1a=None, n1b=None):
    nc = bass.Bass("TRN2", target_bir_lowering=False)
    P = Prog(nc)

    def din(name, shape, dt=F32):
        return nc.dram_tensor(name, list(shape), dt, kind="ExternalInput").ap()

    xsrc = {}
    ydst = {}
    if n_p:
        xsrc["p"] = din("xp", [n_p, len_p, D])
        ydst["p"] = nc.dram_tensor("yp", [n_p, len_p, D], F32, kind="ExternalOutput").ap()
    if n_s:
        xsrc["s"] = din("xs", [n_s, len_s, D])
        ydst["s"] = nc.dram_tensor("ys", [n_s, len_s, D], F32, kind="ExternalOutput").ap()
    w_in = din("w_in", [D, IN_COLS])
    w_o = din("w_o", [D, D])
    w_up = din("w_up", [D, DFF])
    w_dn = din("w_down", [DFF, D])
    d_gin = din("gin", [128, 8])
    d_gmlp = din("gmlp", [128, 8])
    d_gout = din("gout", [128, 8])
    d_qkg = din("qkg", [1, 256])
    d_sink = din("sinkp", [1, 8])
    d_cos = din("cost", [128, 256])
    d_sin = din("sint", [128, 256])
    d_masks = din("masks", [128, NM * 128], BF16)
    d_ident = din("ident", [128, 128], BF16)

    def sb(name, shape, dt):
        return nc.alloc_sbuf_tensor("sb_" + name, shape, dt)
    WIN_t = sb("WIN", [128, 8 * IN_COLS], BF16)
    WIN = WIN_t.ap().rearrange("p (k c) -> p k c", k=8)
    R1 = sb("R1", [128, 16384], BF16)
    QT = R1.ap().rearrange("p (m t) -> p m t", m=8)
    WUP = [R1.ap()[:, s * 8192:s * 8192 + 4096].rearrange("p (k f) -> p k f", k=8) for s in range(2)]
    WDN = [R1.ap()[:, s * 8192 + 4096:(s + 1) * 8192].rearrange("p (c n) -> p c n", c=4) for s in range(2)]
    R2 = sb("R2", [128, 32768], BF16)
    KT = R2.ap()[:, 0:15360].rearrange("p (m t) -> p m t", m=5)
    V = R2.ap()[:, 15360:15360 + 15600].rearrange("p (c h d) -> p c h d", c=24, h=10)
    H = R2.ap().bitcast(F32).rearrange("p (t f) -> p t f", t=16)
    R3 = sb("R3", [128, 16384], BF16)
    MT = R3.ap().rearrange("p (m t) -> p m t", m=8)
    r3f = R3.ap().bitcast(F32)
    STG = r3f[:, 0:1664]
    SQ = r3f[:, 1664:3328]
    ROPEIN = r3f[:, 3328:3328 + 416]
    R4 = sb("R4", [128, 8192], BF16)
    r4f = R4.ap().bitcast(F32)
    QN32 = r4f[:, 0:1664]
    QKB = R4.ap()[:, 3328:3328 + 1664]
    PTB = [R4.ap()[:, 4992 + s * 512:4992 + (s + 1) * 512] for s in range(2)]
    O32 = r4f[:, 3008:3008 + 1024]
    PT4 = [PTB[0], PTB[1], R4.ap()[:, 0:512], R4.ap()[:, 512:1024], R4.ap()[:, 4112:4624]]
    QZ2 = [None, R4.ap()[:, 1024:3072].rearrange("p (m v t) -> p m v t", m=8, v=2)]
    WO = R1.ap()[:, 8192:16384].rearrange("p (k n) -> p k n", k=8)
    UT = [R4.ap()[:, s * 2048:(s + 1) * 2048].rearrange("p (c t) -> p c t", c=4) for s in range(2)]
    R32 = [r4f[:, 2048 + s * 512:2048 + (s + 1) * 512] for s in range(2)]

    QZ = sb("qz", [128, 8, 2, 128], BF16).ap()
    QZ2[0] = QZ
    XT = [sb("xt%d" % s, [128, D], F32).ap() for s in range(2)]
    XS = sb("xsb", [128, D], BF16).ap()
    JUNK = sb("junk", [128, D], BF16).ap()
    XNT = sb("xnT", [128, 8, 128], BF16).ap()
    xnt_flat = XNT.rearrange("p k t -> p (k t)")
    PT4 += [xnt_flat[:, 0:512], xnt_flat[:, 512:1024]]
    MASKS = sb("masks", [128, NM, 128], BF16).ap()
    COS = sb("cos", [128, 32, 8], F32).ap()
    SIN = sb("sin", [128, 32, 8], F32).ap()
    IDENT = sb("ident", [128, 128], BF16).ap()
    GIN = sb("gin", [128, 8], F32).ap()
    GMLP = sb("gmlp", [128, 8], F32).ap()
    GOUT = sb("gout", [128, 8], F32).ap()
    QKG = sb("qkg", [128, 4, 64], F32).ap()
    SINK = sb("sink", [128, 8], F32).ap()
    ESINK = sb("esink", [128, 8], F32).ap()
    EPST = sb("epst", [128, 1], F32).ap()
    SS = sb("ss", [128, 2], F32).ap()
    LNV = sb("lnv", [128, 2], F32).ap()
    RSTD = sb("rstd", [128, 2], F32).ap()
    SSQ = sb("ssq", [128, 26], F32).ap()
    LNQ = sb("lnq", [128, 26], F32).ap()
    RQ = sb("rq", [128, 26], F32).ap()
    RT = [r3f[:, 3744 + j * 208:3744 + (j + 1) * 208].rearrange("p (h d) -> p h d", d=8) for j in range(4)]
    ODS = r4f[:, 1536:1536 + 520]
    VDS = [XT[s_].bitcast(BF16)[:, 0:1040].rearrange("p (c f) -> p c f", c=2) for s_ in range(2)]
    ODT = [XT[s_][:, 0:520] for s_ in range(2)]
    vd_dram = nc.dram_tensor("vd_scratch", [24 * 128, 520], F32, kind="Internal").ap()
    od_dram = nc.dram_tensor("od_scratch", [NQ * 128, 520], F32, kind="Internal").ap()
    kt_scr = nc.dram_tensor("kt_scratch", [128, 5 * 1024], BF16, kind="Internal").ap()
    v_scr = nc.dram_tensor("v_scratch", [128, 8 * 650], BF16, kind="Internal").ap()
    DEN = sb("den", [128, 16], F32).ap()
    RDEN = sb("rden", [128, 16], F32).ap()

    PB = [nc.alloc_psum_tensor("pb%d" % j, [128, 512], F32).ap() for j in range(8)]
    PBh = [p.bitcast(BF16) for p in PB]

    WIN_COLS = [(0, 512), (512, 1024), (1024, 1536), (1536, 2048), (2048, 2304)]
    bWIN = [P.buf("win%d" % k) for k in range(5)]
    bXT = [P.buf("xt%d" % s) for s in range(2)]
    bXS, bJUNK, bXNT = P.buf("xs"), P.buf("junk"), P.buf("xnt")
    bMASKS, bCOS, bSIN, bIDENT = P.buf("masks"), P.buf("cos"), P.buf("sin"), P.buf("ident")
    bGIN, bGMLP, bGOUT, bQKG = P.buf("gin"), P.buf("gmlp"), P.buf("gout"), P.buf("qkg")
    bSINK, bESINK, bEPS = P.buf("sink"), P.buf("esink"), P.buf("eps")
    bSS, bLNV, bRSTD = P.buf("ss"), P.buf("lnv"), P.buf("rstd")
    bSSQ, bLNQ, bRQ = P.buf("ssq"), P.buf("lnq"), P.buf("rq")
    bRT = [P.buf("rt%d" % j) for j in range(4)]
    bDEN, bRDEN = P.buf("den"), P.buf("rden")
    bPB = [P.buf("pb%d" % j, excl=True) for j in range(8)]
    bQN32, bQKB, bO32 = P.buf("qn32"), P.buf("qkb"), P.buf("o32")
    bSTG, bSQ, bRIN = P.buf("stg"), P.buf("sq"), P.buf("rin")
    bXNTb = P.buf("xntb")
    bPT = [P.buf("pt%d" % s) for s in range(2)]
    bKT = [P.buf("kt%d" % c) for c in range(24)]
    bV = [P.buf("v%d" % c) for c in range(24)]
    bQT = [P.buf("qt%d" % i) for i in range(NQ)]
    bM = [P.buf("m%d" % i) for i in range(NQ)]
    bH = [P.buf("h%d" % i) for i in range(NQ)]
    bWO = [P.buf("wo%d" % hh) for hh in range(2)]
    bQZ = P.buf("qz")
    bQZ2 = [bQZ, P.buf("qz1")]
    bQZlo = [P.buf("qzlo0"), P.buf("qzlo1")]
    bQZhi = [P.buf("qzhi0"), P.buf("qzhi1")]
    bPT4 = [bPT[0], bPT[1], P.buf("pt2"), P.buf("pt3"), P.buf("pt4"), P.buf("pt5"), P.buf("pt6")]
    NPT = len(bPT4)
    bYST = P.buf("yst")
    bVDd, bVDW, bODd, bODS = P.buf("vdd"), P.buf("vdw"), P.buf("odd"), P.buf("ods")
    bVDS = [P.buf("vds%d" % b) for b in range(4)]
    bKTS, bVSS, bKTL = P.buf("kts"), P.buf("vss"), P.buf("ktl")
    bVL = [P.buf("vl0"), P.buf("vl1")]
    bWUP = [P.buf("wup%d" % s) for s in range(2)]
    bWDN = [[P.buf("wdn%d_%d" % (s, hh)) for hh in range(2)] for s in range(2)]
    bUT = [P.buf("ut%d" % s) for s in range(2)]
    bR32 = [P.buf("r32%d" % s) for s in range(2)]

    P.dma("sp", dict(out=IDENT, in_=d_ident), bIDENT, writes=[bIDENT])
    P.dma("sp", dict(out=MASKS, in_=d_masks.rearrange("p (m t) -> p m t", m=NM)), bMASKS, writes=[bMASKS])
    P.dma("sp", dict(out=COS, in_=d_cos.rearrange("p (t f) -> p t f", t=32)), bCOS, writes=[bCOS])
    P.dma("sp", dict(out=SIN, in_=d_sin.rearrange("p (t f) -> p t f", t=32)), bSIN, writes=[bSIN])
    P.dma("sp", dict(out=GIN, in_=d_gin), bGIN, writes=[bGIN])
    P.dma("sp", dict(out=GMLP, in_=d_gmlp), bGMLP, writes=[bGMLP])
    P.dma("sp", dict(out=GOUT, in_=d_gout), bGOUT, writes=[bGOUT])
    P.dma("sp", dict(out=QKG.rearrange("p a b -> p (a b)"), in_=d_qkg.partition_broadcast(128)), bQKG, writes=[bQKG])
    P.dma("sp", dict(out=SINK, in_=d_sink.partition_broadcast(128)), bSINK, writes=[bSINK])
    P.op("pool", "memset", dict(ap=EPST, constant=EPS), writes=[bEPS])
    P.op("pool", "memset", dict(ap=QZ, constant=0.0), writes=[bQZlo[0], bQZhi[0]])
    for j, (a0, a1) in enumerate(WIN_COLS):
        P.dma("pool", dict(out=WIN[:, :, a0:a1], in_=w_in[:, a0:a1].rearrange("(k p) f -> p k f", p=128)),
              bWIN[j], writes=[bWIN[j]])
    P.op("act", "activation", dict(out=ESINK, in_=SINK, func=AF.Exp), reads=[bSINK], writes=[bESINK])

    def rstd_chain(ss_ap, ln_ap, r_ap, n, bss, bln, br):
        P.op("act", "activation", dict(out=ln_ap, in_=ss_ap, func=AF.Ln, scale=1.0 / n, bias=EPST),
             reads=[bss, bEPS], writes=[bln])
        P.op("act", "activation", dict(out=r_ap, in_=ln_ap, func=AF.Exp, scale=-0.5), reads=[bln], writes=[br])

    import os
    CUT = int(os.environ.get("CUT", "99"))
    state = {"x": 0, "s": 0, "u": 0, "y": 0, "mm": 0}

    def load_x(xrows):
        slot = state["x"] % 2
        state["x"] += 1
        P.dma("sp", dict(out=XT[slot], in_=xrows), bXT[slot], writes=[bXT[slot]])
        return slot

    def transposes_xs(ps_idx):
        fns = [("transpose", dict(out=PBh[ps_idx][:, k * 128:(k + 1) * 128], in_=XS[:, k * 128:(k + 1) * 128],
                                  identity=IDENT)) for k in range(8)]
        P.group("pe", fns, reads=[bXS, bIDENT], writes=[bPB[ps_idx]])

    def norm_transpose(src_ap, bsrc, ps_idx):
        P.op("act", "activation", dict(out=JUNK, in_=src_ap, func=AF.Square, accum_out=SS[:, 0:1]),
             reads=[bsrc], writes=[bJUNK, bSS])
        rstd_chain(SS[:, 0:1], LNV[:, 0:1], RSTD[:, 0:1], D, bSS, bLNV, bRSTD)
        P.op("act", "activation", dict(out=XS, in_=src_ap, func=AF.Copy, scale=RSTD[:, 0:1]),
             reads=[bsrc, bRSTD], writes=[bXS])
        transposes_xs(ps_idx)

    def evac_T(ps_idx, out_ap, gain_ap, bgain, bout):
        P.op("dve", "tensor_tensor", dict(out=out_ap, in0=PBh[ps_idx].rearrange("p (k t) -> p k t", k=8),
                                          in1=gain_ap.unsqueeze(2).to_broadcast([128, 8, 128]), op=ALU.mult),
             reads=[bPB[ps_idx], bgain], writes=[bout])

    def hv(ap):
        return ap.rearrange("p (h d) -> p h d", d=64)

    for gspec in groups:
        (src, seq, q0, c0, nctx) = gspec[:5]
        halo_mode = gspec[5] if len(gspec) > 5 else None
        if stop == "setup":
            break
        xseq = xsrc[src][seq]
        yseq = ydst[src][seq]
        L = xseq.shape[0]
        nts = L // 128

        P.op("pool", "memset", dict(ap=V[:, :, :, 64:65], constant=1.0), writes=bV[:nctx])

        n_c = nctx if n1a is None else n1a
        c_start = 0
        if halo_mode == "load":
            c_start = 8
            P.dma("sp", dict(out=KT[:, :, 0:1024], in_=kt_scr.rearrange("p (m t) -> p m t", m=5)), bKTL,
                  reads=[bKTS], writes=bKT[0:8] + [bKTL])
            for half_ in range(2):
                P.dma("sp", dict(out=V[:, half_ * 4:(half_ + 1) * 4, :, :].rearrange("p c h d -> p (c h d)"),
                                 in_=v_scr[:, half_ * 2600:(half_ + 1) * 2600]), bVL[half_],
                      reads=[bVSS], writes=bV[half_ * 4:(half_ + 1) * 4] + [bVL[half_]])
            for c_ in range(8):
                for hf in range(2):
                    P.dma("pool", dict(out=vd_dram[c_ * 128:(c_ + 1) * 128, hf * 260:(hf + 1) * 260],
                                       in_=V[:, c_, hf * 4:(hf + 1) * 4, :].rearrange("p h d -> p (h d)")),
                          bVDW, reads=[bV[c_]], writes=[bVDd])
        BANKS = [(0, 0, 512), (1, 512, 1024), (2, 1024, 1536), (3, 1536, 2048), (4, 2048, 2304)]
        SEGS = [(0, 0, 8, 0), (1, 8, 8, 1), (2, 16, 8, 2), (4, 24, 2, 3)]
        stv = hv(STG)
        kb = hv(QKB)
        rin = ROPEIN.rearrange("p (h d) -> p h d", d=16)

        def info(c):
            ts = c0 + c
            own = q0 <= ts < q0 + NQ
            return dict(ts=ts, own=own, qi=ts - q0, banks=BANKS if own else BANKS[2:],
                        segs=SEGS if own else SEGS[2:], h_lo=0 if own else 16)

        slots1a = {}

        def st_ldx(c):
            nf = info(c)
            slots1a[c] = load_x(xseq[nf["ts"] * 128:(nf["ts"] + 1) * 128, :])

        def st_A(c):
            nf = info(c)
            slot = slots1a[c]
            P.op("act", "activation", dict(out=JUNK, in_=XT[slot], func=AF.Square, accum_out=SS[:, 0:1]),
                 reads=[bXT[slot]], writes=[bJUNK, bSS])
            rstd_chain(SS[:, 0:1], LNV[:, 0:1], RSTD[:, 0:1], D, bSS, bLNV, bRSTD)
            P.op("act", "activation", dict(out=XS, in_=XT[slot], func=AF.Copy, scale=RSTD[:, 0:1]),
                 reads=[bXT[slot], bRSTD], writes=[bXS])

        XNT2 = [XNT, R3.ap()[:, 12288:13312].rearrange("p (k t) -> p k t", k=8)]
        bXNT2 = [bXNT, bXNTb]

        def st_MM(c, which):
            nf = info(c)
            xnt, bxnt = XNT2[c % 2], bXNT2[c % 2]
            part = [b for b in nf["banks"] if (b[0] < 2) == (which == 0)]
            if not part:
                return
            fns = []
            for k in range(8):
                for (bk, a0, a1) in part:
                    fns.append(("matmul", dict(out=PB[bk][:, 0:a1 - a0], lhsT=xnt[:, k, :], rhs=WIN[:, k, a0:a1],
                                               start=(k == 0), stop=(k == 7))))
            P.group("pe", fns, reads=[bxnt] + [bWIN[bk] for (bk, _, _) in part],
                    writes=[bPB[bk] for (bk, _, _) in part])

        def st_C(c, part):
            nf = info(c)
            for (bk, h0, nh, gt) in [sg_ for sg_ in nf["segs"] if (sg_[0] < 2) == (part == 0)]:
                P.op("act", "activation", dict(out=SQ[:, h0 * 64:(h0 + nh) * 64], in_=PB[bk][:, 0:nh * 64], func=AF.Square),
                     reads=[bPB[bk]], writes=[bSQ])
                P.op("dve", "tensor_tensor", dict(
                    out=hv(STG[:, h0 * 64:(h0 + nh) * 64]), in0=hv(PB[bk][:, 0:nh * 64]),
                    in1=QKG[:, gt, :].unsqueeze(1).to_broadcast([128, nh, 64]), op=ALU.mult),
                    reads=[bPB[bk], bQKG], writes=[bSTG])
            if part == 0:
                return
            P.op("act", "activation", dict(out=V[:, c, 0:8, 0:64], in_=hv(PB[3][:, :]), func=AF.Copy),
                 reads=[bPB[3]], writes=[bV[c]])
            P.op("act", "activation", dict(out=V[:, c, 8:10, 0:64], in_=hv(PB[4][:, 128:256]), func=AF.Copy),
                 reads=[bPB[4]], writes=[bV[c]])
            for hf in range(2):
                P.dma("pool", dict(out=vd_dram[c * 128:(c + 1) * 128, hf * 260:(hf + 1) * 260],
                                   in_=V[:, c, hf * 4:(hf + 1) * 4, :].rearrange("p h d -> p (h d)")),
                      bVDW, reads=[bV[c]], writes=[bVDd])

        def st_D(c):
            nf = info(c)
            h_lo, h_hi = nf["h_lo"], 26
            nh_all = h_hi - h_lo
            ts = nf["ts"]
            P.op("dve", "tensor_reduce", dict(out=SSQ[:, h_lo:h_hi], in_=hv(SQ[:, h_lo * 64:h_hi * 64]),
                                              axis=AX.X, op=ALU.add), reads=[bSQ], writes=[bSSQ])
            rstd_chain(SSQ[:, h_lo:h_hi], LNQ[:, h_lo:h_hi], RQ[:, h_lo:h_hi], HD, bSSQ, bLNQ, bRQ)
            P.op("dve", "tensor_tensor", dict(
                out=kb[:, h_lo:h_hi, 16:64], in0=stv[:, h_lo:h_hi, 16:64],
                in1=RQ[:, h_lo:h_hi].unsqueeze(2).to_broadcast([128, nh_all, 48]), op=ALU.mult),
                reads=[bSTG, bRQ], writes=[bQKB])
            P.op("dve", "tensor_tensor", dict(
                out=rin[:, h_lo:h_hi, :], in0=stv[:, h_lo:h_hi, 0:16],
                in1=RQ[:, h_lo:h_hi].unsqueeze(2).to_broadcast([128, nh_all, 16]), op=ALU.mult),
                reads=[bSTG, bRQ], writes=[bRIN])
            x1 = rin[:, h_lo:h_hi, 0:8]
            x2 = rin[:, h_lo:h_hi, 8:16]
            cosb = COS[:, ts, :].unsqueeze(1).to_broadcast([128, nh_all, 8])
            sinb = SIN[:, ts, :].unsqueeze(1).to_broadcast([128, nh_all, 8])
            for j, (xa, tb, bt) in enumerate([(x1, cosb, bCOS), (x2, sinb, bSIN), (x2, cosb, bCOS), (x1, sinb, bSIN)]):
                P.op("pool", "tensor_tensor", dict(out=RT[j][:, h_lo:h_hi, :], in0=xa, in1=tb, op=ALU.mult),
                     reads=[bRIN, bt], writes=[bRT[j]])
            P.op("dve", "tensor_tensor", dict(out=kb[:, h_lo:h_hi, 0:8], in0=RT[0][:, h_lo:h_hi, :],
                                              in1=RT[1][:, h_lo:h_hi, :], op=ALU.subtract),
                 reads=[bRT[0], bRT[1]], writes=[bQKB])
            P.op("dve", "tensor_tensor", dict(out=kb[:, h_lo:h_hi, 8:16], in0=RT[2][:, h_lo:h_hi, :],
                                              in1=RT[3][:, h_lo:h_hi, :], op=ALU.add),
                 reads=[bRT[2], bRT[3]], writes=[bQKB])

        def st_T2(c):
            nf = info(c)
            if nf["own"]:
                fns = [("transpose", dict(out=PBh[6][:, m * 128:(m + 1) * 128], in_=QKB[:, m * 128:(m + 1) * 128],
                                          identity=IDENT)) for m in range(8)]
                P.group("pe", fns, reads=[bQKB, bIDENT], writes=[bPB[6]])
            fns = [("transpose", dict(out=PBh[7][:, m * 128:(m + 1) * 128],
                                      in_=QKB[:, 1024 + m * 128:1024 + (m + 1) * 128], identity=IDENT)) for m in range(5)]
            P.group("pe", fns, reads=[bQKB, bIDENT], writes=[bPB[7]])

        def st_E(c):
            nf = info(c)
            if nf["own"]:
                qi = nf["qi"]
                P.op("dve", "tensor_copy", dict(out=QT[:, :, qi * 128:(qi + 1) * 128],
                                                in_=PBh[6].rearrange("p (m t) -> p m t", m=8)),
                     reads=[bPB[6]], writes=[bQT[qi]])
            P.op("act", "activation", dict(out=KT[:, :, c * 128:(c + 1) * 128],
                                           in_=PBh[7][:, 0:640].rearrange("p (m t) -> p m t", m=5), func=AF.Copy),
                 reads=[bPB[7]], writes=[bKT[c]])

        st_ldx(c_start)
        if n_c > c_start + 1:
            st_ldx(c_start + 1)
        st_A(c_start)
        for c in range(c_start, n_c + 2):
            if c_start <= c - 2 < n_c:
                st_D(c - 2)
            if c_start <= c - 1 < n_c:
                st_MM(c - 1, 0)
            if c < n_c:
                transposes_xs(5)
            if c_start <= c - 1 < n_c:
                st_C(c - 1, 0)
            if c < n_c:
                evac_T(5, XNT2[c % 2], GIN, bGIN, bXNT2[c % 2])
            if c_start <= c - 1 < n_c:
                st_MM(c - 1, 1)
            if c_start <= c - 2 < n_c:
                st_T2(c - 2)
            if c_start <= c - 1 < n_c:
                st_C(c - 1, 1)
            if c_start <= c - 2 < n_c:
                st_E(c - 2)
            if c + 2 < n_c:
                st_ldx(c + 2)
            if c + 1 < n_c:
                st_A(c + 1)

        if halo_mode == "save":
            P.dma("pool", dict(out=kt_scr.rearrange("p (m t) -> p m t", m=5), in_=KT[:, :, 1024:2048]), bKTS,
                  reads=bKT[8:16], writes=[bKTS])
            for half_ in range(2):
                P.dma("pool", dict(out=v_scr[:, half_ * 2600:(half_ + 1) * 2600],
                                   in_=V[:, 8 + half_ * 4:8 + (half_ + 1) * 4, :, :].rearrange("p c h d -> p (c h d)")),
                      bVSS, reads=bV[8 + half_ * 4:8 + (half_ + 1) * 4], writes=[bVSS])

        if stop == "1a":
            break
        P.barrier()
        P.op("pool", "memset", dict(ap=QZ2[1], constant=0.0), writes=[bQZlo[1], bQZhi[1]])
        n_q = NQ if n1b is None else n1b
        LA = 5
        SBANK = [0, 1, 7]

        def sl(st_, n_, step):
            return slice(st_, st_ + step * (n_ - 1) + 1, step)

        n_kpos = nctx * 8
        kchunks = [(kc, min(128, n_kpos - 128 * kc)) for kc in range((n_kpos + 127) // 128)]
        vd_cls = vd_dram.rearrange("(a s) f -> s a f", s=16)
        od_cls = od_dram.rearrange("(j s) f -> s j f", s=16)

        NVB = 4 if len(kchunks) == 1 else 2
        nkc = len(kchunks)
        VDSn = [XT[b % 2].bitcast(BF16)[:, (b // 2) * 520 * nkc:(b // 2 + 1) * 520 * nkc].rearrange("p (c f) -> p c f", c=nkc)
                for b in range(NVB)]

        def load_vd(r):
            b = r % NVB
            for kc, na in kchunks:
                for hf in range(2):
                    P.dma("pool", dict(out=VDSn[b][0:na, kc, hf * 260:(hf + 1) * 260],
                                       in_=vd_cls[r][128 * kc:128 * kc + na, hf * 260:(hf + 1) * 260]),
                          bVDS[b], reads=[bVDd], writes=[bVDS[b], bXT[b % 2]])

        def fill_qz(kind, j):
            b = j % 2
            qz = QZ2[b]
            if kind == "D":
                P.op("act", "activation", dict(out=qz[0:64, 0:4, 0, :], in_=QT[0:64, 0:4, sl(j, 128, 16)], func=AF.Copy),
                     reads=bQT, writes=[bQZlo[b]])
                P.op("dve", "tensor_copy", dict(out=qz[64:128, 0:4, 1, :], in_=QT[64:128, 0:4, sl(j, 128, 16)]),
                     reads=bQT, writes=[bQZhi[b]])
            else:
                P.op("act", "activation", dict(out=qz[0:64, :, 0, :], in_=QT[0:64, :, j * 128:(j + 1) * 128], func=AF.Copy),
                     reads=[bQT[j]], writes=[bQZlo[b]])
                P.op("dve", "tensor_copy", dict(out=qz[64:128, :, 1, :], in_=QT[64:128, :, j * 128:(j + 1) * 128]),
                     reads=[bQT[j]], writes=[bQZhi[b]])

        groups_order = ([("D", r) for r in range(16)] if n_q == NQ else []) + [("N", i) for i in range(n_q)]

        def load_wo():
            for hh in range(2):
                P.dma("pool", dict(out=WO[:, :, hh * 512:(hh + 1) * 512],
                                   in_=w_o[:, hh * 512:(hh + 1) * 512].rearrange("(k p) n -> p k n", p=128)),
                      bWO[hh], writes=[bWO[hh]] + bQT + [bWUP[1]] + bWDN[1])

        def first_any(gidx):
            if gidx + 1 < len(groups_order):
                fill_qz(*groups_order[gidx + 1])
                if gidx + 2 == len(groups_order):
                    load_wo()

        def first_dil(r):
            first_any(r)

        def last_dil(r):
            if r + NVB < 16:
                load_vd(r + NVB)
            b0 = 2 + 2 * (r % 2)
            P.op("act", "activation", dict(out=ODS[:, 0:260], in_=PB[b0][:, 0:260], func=AF.Copy),
                 reads=[bPB[b0]], writes=[bODS])
            P.op("dve", "tensor_copy", dict(out=ODS[:, 260:520], in_=PB[b0 + 1][:, 0:260]),
                 reads=[bPB[b0 + 1]], writes=[bODS])
            P.dma("sp", dict(out=od_cls[r], in_=ODS), bODS, reads=[bODS], writes=[bODd])

        def first_nat(i):
            first_any((16 if n_q == NQ else 0) + i)

        def first_back_nat(i):
            P.dma("sp", dict(out=ODT[i % 2], in_=od_dram[i * 128:(i + 1) * 128, :]), bXT[i % 2],
                  reads=[bODd], writes=[bXT[i % 2]] + [bVDS[b] for b in range(4) if b % 2 == i % 2])

        iters = []
        fill_qz(*groups_order[0])
        if len(groups_order) == 1:
            load_wo()
        if n_q == NQ:
            for r in range(NVB):
                load_vd(r)
            for r in range(16):
                cls = []
                for hg in range(2):
                    for idx, (kc, na) in enumerate(kchunks):
                        off = (c0 * 8 + 128 * kc) - q0 * 8
                        cls.append(dict(
                            kind="A", g=hg, idx=idx, n=len(kchunks), nk=na, mi=_IDXD[off], ob=2 + hg + 2 * (r % 2),
                            qz=(QZ2[r % 2], bQZ2[r % 2]), qb=r % 2,
                            kt=(lambda ch, kc=kc, na=na, r=r: KT[:, ch, sl(2048 * kc + r, na, 16)]),
                            ktb=bKT[:nctx],
                            vfn=(lambda h, kc=kc, na=na, r=r: VDSn[r % NVB][0:na, kc, h * 65:(h + 1) * 65]),
                            vb=[bVDS[r % NVB]]))
                cls[0]["first"] = (lambda r=r: first_dil(r))
                cls[-1]["last"] = (lambda n_now, r=r: last_dil(r))
                iters += cls
        for i in range(n_q):
            tq = q0 + i
            deltas = [dl for dl in range(-2, 3) if 0 <= tq + dl < nts]
            deltas_b = [dl for dl in (-1, 0, 1) if 0 <= tq + dl < nts]
            tl = []
            for hg in range(2):
                for idx, dl in enumerate(deltas):
                    ck = tq + dl - c0
                    assert 0 <= ck < nctx
                    tl.append(dict(kind="A", g=hg, idx=idx, n=len(deltas), nk=128, mi=_IDXA[dl],
                                   qz=(QZ2[i % 2], bQZ2[i % 2]), qb=i % 2,
                                   kt=(lambda ch, ck=ck: KT[:, ch, ck * 128:(ck + 1) * 128]), ktb=[bKT[ck]],
                                   vfn=(lambda h, ck=ck: V[:, ck, h, :]), vb=[bV[ck]]))
            for e_kv in range(2):
                for idx, dl in enumerate(deltas_b):
                    ck = tq + dl - c0
                    assert 0 <= ck < nctx
                    tl.append(dict(kind="B", g=e_kv, idx=idx, n=len(deltas_b), nk=128,
                                   mi=(_IDXB[dl] if dl != 0 else None),
                                   qz=(QZ2[i % 2], bQZ2[i % 2]), qb=i % 2,
                                   kt=(lambda ch, ck=ck: KT[:, ch, ck * 128:(ck + 1) * 128]), ktb=[bKT[ck]],
                                   vfn=(lambda h, ck=ck, e_kv=e_kv: V[:, ck, 8 + e_kv, :]), vb=[bV[ck]]))
            tl[0]["first"] = (lambda i=i: first_nat(i))
            if n_q == NQ:
                tl[0]["first_back"] = (lambda i=i: first_back_nat(i))
            tl[-1]["last"] = (lambda n_now, i=i: epilogue(i, n_now))
            iters += tl

        def front(n, it):
            g, nk = it["g"], it["nk"]
            qz, bqz = it["qz"]
            if it.get("first"):
                it["first"]()
            sb_ = SBANK[n % 3]
            pt, bpt = PT4[n % NPT], bPT4[n % NPT]
            if it["kind"] == "A":
                fns = []
                for j in range(2):
                    ch = g * 2 + j
                    fns.append(("matmul", dict(out=PB[sb_][0:nk, j * 256:(j + 1) * 256], lhsT=it["kt"](ch),
                                               rhs=qz[:, ch, :, :], start=True, stop=True)))
            else:
                fns = [("matmul", dict(out=PB[sb_][0:nk, :], lhsT=it["kt"](4), rhs=qz[:, 4:8, g, :],
                                       start=True, stop=True))]
            P.group("pe", fns, reads=list(it["ktb"]) + [bQZlo[it["qb"]], bQZhi[it["qb"]]], writes=[bPB[sb_]])
            P.op("act", "activation", dict(out=pt[0:nk, :], in_=PB[sb_][0:nk, :], func=AF.Exp, scale=0.125),
                 reads=[bPB[sb_]], writes=[bpt])
            if it["mi"] is not None:
                ptv = pt[0:nk, :].rearrange("p (h t) -> p h t", h=4)
                P.op("dve", "tensor_tensor", dict(out=ptv, in0=ptv,
                                                  in1=MASKS[0:nk, it["mi"], :].unsqueeze(1).to_broadcast([nk, 4, 128]),
                                                  op=ALU.mult),
                     reads=[bpt, bMASKS], writes=[bpt])

        def back(n, it):
            g, nk, idx = it["g"], it["nk"], it["idx"]
            if it.get("first_back"):
                it["first_back"]()
            pt, bpt = PT4[n % NPT], bPT4[n % NPT]
            ob = it.get("ob", (2 + g) if it["kind"] == "A" else (4 + g))
            fns = []
            for hl in range(4):
                fns.append(("matmul", dict(out=PB[ob][:, hl * 65:(hl + 1) * 65],
                                           lhsT=pt[0:nk, hl * 128:(hl + 1) * 128], rhs=it["vfn"](g * 4 + hl),
                                           start=(idx == 0 and hl == 0), stop=(idx == it["n"] - 1),
                                           skip_group_check=True)))
            P.group("pe", fns, reads=[bpt] + list(it["vb"]), writes=[bPB[ob]])
            if it.get("last"):
                it["last"](n + LA)

        deferred = []

        def epilogue(i, n_now):
            xb = bXT[i % 2]
            obv = O32[:, 512:1024].rearrange("p (m e d) -> p m e d", e=2, d=64)

            def e1():
                for hg in range(2):
                    odt = ODT[i % 2][:, hg * 260:(hg + 1) * 260]
                    if n_q == NQ:
                        P.op("dve", "tensor_tensor", dict(out=odt, in0=PB[2 + hg][:, 0:260], in1=odt, op=ALU.add),
                             reads=[bPB[2 + hg], xb], writes=[xb])
                    else:
                        P.op("dve", "tensor_copy", dict(out=odt, in_=PB[2 + hg][:, 0:260]), reads=[bPB[2 + hg]], writes=[xb])
                for e_kv in range(2):
                    ov = PB[4 + e_kv][:, 0:260].rearrange("p (h d) -> p h d", d=65)
                    esv = ESINK.rearrange("p (m e) -> p m e", e=2)[:, :, e_kv]
                    dsl = slice(8 + e_kv * 4, 8 + (e_kv + 1) * 4)
                    P.op("dve", "tensor_copy", dict(out=obv[:, :, e_kv, :], in_=ov[:, :, 0:64]),
                         reads=[bPB[4 + e_kv]], writes=[bO32])
                    P.op("dve", "tensor_tensor", dict(out=DEN[:, dsl], in0=ov[:, :, 64], in1=esv, op=ALU.add),
                         reads=[bPB[4 + e_kv], bESINK], writes=[bDEN])

            def e2():
                for hg in range(2):
                    ov = ODT[i % 2][:, hg * 260:(hg + 1) * 260].rearrange("p (h d) -> p h d", d=65)
                    P.op("dve", "reciprocal", dict(out=RDEN[:, hg * 4:(hg + 1) * 4], in_=ov[:, :, 64]),
                         reads=[xb], writes=[bRDEN])
                    P.op("dve", "tensor_tensor", dict(
                        out=hv(O32[:, hg * 256:(hg + 1) * 256]), in0=ov[:, :, 0:64],
                        in1=RDEN[:, hg * 4:(hg + 1) * 4].unsqueeze(2).to_broadcast([128, 4, 64]), op=ALU.mult),
                        reads=[xb, bRDEN], writes=[bO32])
                P.op("dve", "reciprocal", dict(out=RDEN[:, 8:16], in_=DEN[:, 8:16]), reads=[bDEN], writes=[bRDEN])
                for e_kv in range(2):
                    dsl = slice(8 + e_kv * 4, 8 + (e_kv + 1) * 4)
                    P.op("dve", "tensor_tensor", dict(
                        out=obv[:, :, e_kv, :], in0=obv[:, :, e_kv, :],
                        in1=RDEN[:, dsl].unsqueeze(2).to_broadcast([128, 4, 64]), op=ALU.mult),
                        reads=[bO32, bRDEN], writes=[bO32])
                if os.environ.get("DUMP_O32"):
                    P.dma("sp", dict(out=yseq[(q0 + i) * 128:(q0 + i + 1) * 128, :], in_=O32), bO32, reads=[bO32])

            def e3():
                for half in range(2):
                    P.op("act", "activation", dict(out=JUNK[:, half * 512:(half + 1) * 512],
                                                   in_=O32[:, half * 512:(half + 1) * 512],
                                                   func=AF.Square, accum_out=SS[:, half:half + 1]),
                         reads=[bO32], writes=[bJUNK, bSS])
                rstd_chain(SS[:, 0:2], LNV[:, 0:2], RSTD[:, 0:2], 512, bSS, bLNV, bRSTD)

            def e4():
                for half in range(2):
                    P.op("dve", "tensor_scalar", dict(out=XS[:, half * 512:(half + 1) * 512],
                                                      in0=O32[:, half * 512:(half + 1) * 512],
                                                      scalar1=RSTD[:, half:half + 1], scalar2=None, op0=ALU.mult),
                         reads=[bO32, bRSTD], writes=[bXS])
                transposes_xs(6)

            def e5():
                evac_T(6, MT[:, :, i * 128:(i + 1) * 128], GOUT, bGOUT, bM[i])

            while deferred:
                deferred.pop(0)[1]()
            e1()
            for dist, fn in zip((1, 7, 11, 14), [e2, e3, e4, e5]):
                deferred.append((n_now + dist, fn))

        n = 0
        while n < len(iters) + LA or deferred:
            if n < len(iters):
                front(n, iters[n])
            if 0 <= n - LA < len(iters):
                back(n - LA, iters[n - LA])
            due = [d for d in deferred if d[0] <= n]
            for d in due:
                deferred.remove(d)
                d[1]()
            n += 1

        P.barrier()
        if stop == "1b":
            break
        def load_w(fb):
            s = fb % 2
            extra = list(bWO) if s == 1 else []
            P.dma("pool", dict(out=WUP[s], in_=w_up[:, fb * 512:(fb + 1) * 512].rearrange("(k p) f -> p k f", p=128)),
                  bWUP[s], writes=[bWUP[s]] + extra)
            for hh in range(2):
                P.dma("pool", dict(out=WDN[s][:, :, hh * 512:(hh + 1) * 512],
                                   in_=w_dn[fb * 512:(fb + 1) * 512, hh * 512:(hh + 1) * 512].rearrange("(c p) n -> p c n", p=128)),
                      bWDN[s][hh], writes=[bWDN[s][hh]] + extra)

        load_w(0)
        slots2a = {}

        def st2a_mm(t):
            hb = (t % 3) * 2
            fns = []
            for k in range(8):
                for half in range(2):
                    fns.append(("matmul", dict(out=PB[hb + half][:, :], lhsT=MT[:, k, t * 128:(t + 1) * 128],
                                               rhs=WO[:, k, half * 512:(half + 1) * 512], start=(k == 0), stop=(k == 7))))
            P.group("pe", fns, reads=[bM[t]] + bWO, writes=[bPB[hb], bPB[hb + 1]])

        XS2 = [XS, JUNK]
        bXS2 = [bXS, bJUNK]
        junk2 = XNT.rearrange("p k t -> p (k t)")

        def st2a_adds(t):
            hb = (t % 3) * 2
            slot = slots2a[t]
            for half in range(2):
                P.op("dve", "tensor_tensor", dict(out=H[:, t, half * 512:(half + 1) * 512], in0=PB[hb + half][:, :],
                                                  in1=XT[slot][:, half * 512:(half + 1) * 512], op=ALU.add),
                     reads=[bPB[hb + half], bXT[slot]], writes=[bH[t]])

        def st2a_act(t):
            xs_, bxs_ = XS2[t % 2], bXS2[t % 2]
            P.op("act", "activation", dict(out=junk2, in_=H[:, t, :], func=AF.Square, accum_out=SS[:, 0:1]),
                 reads=[bH[t]], writes=[bXNT, bSS])
            rstd_chain(SS[:, 0:1], LNV[:, 0:1], RSTD[:, 0:1], D, bSS, bLNV, bRSTD)
            P.op("act", "activation", dict(out=xs_, in_=H[:, t, :], func=AF.Copy, scale=RSTD[:, 0:1]),
                 reads=[bH[t], bRSTD], writes=[bxs_])

        def st2a_T(t):
            xs_, bxs_ = XS2[t % 2], bXS2[t % 2]
            fns = [("transpose", dict(out=PBh[6][:, k * 128:(k + 1) * 128], in_=xs_[:, k * 128:(k + 1) * 128],
                                      identity=IDENT)) for k in range(8)]
            P.group("pe", fns, reads=[bxs_, bIDENT], writes=[bPB[6]])
            evac_T(6, MT[:, :, t * 128:(t + 1) * 128], GMLP, bGMLP, bM[t])

        def st2a_ldx(t):
            tq = q0 + t
            slots2a[t] = load_x(xseq[tq * 128:(tq + 1) * 128, :])

        st2a_ldx(0)
        st2a_ldx(1)
        st2a_mm(0)
        st2a_mm(1)
        st2a_adds(0)
        st2a_act(0)
        for t in range(NQ):
            if t + 2 < NQ:
                st2a_mm(t + 2)
            if t + 1 < NQ:
                st2a_adds(t + 1)
                st2a_act(t + 1)
            if t + 2 < NQ:
                st2a_ldx(t + 2)
            st2a_T(t)

        if stop == "2a":
            P.barrier()
        else:
            for b_ in bUT + bR32:
                for hh in range(2):
                    for sem_, v_ in bWO[hh].r.items():
                        b_.r[sem_] = max(b_.r.get(sem_, 0), v_)
        if stop == "2a":
            for t in range(NQ):
                tq = q0 + t
                P.dma("sp", dict(out=yseq[tq * 128:(tq + 1) * 128, :], in_=H[:, t, :]), bH[t], reads=[bH[t]])
            break
        NFB = DFF // 512
        pairs = [(fb, sg) for fb in range(NFB) for sg in range(4)]
        utb_of = {}

        def up_proj(fb, sg):
            s = fb % 2
            utb = state["u"] % 2
            state["u"] += 1
            utb_of[(fb, sg)] = utb
            for c in range(4):
                ub = state["y"] % 2
                state["y"] += 1
                fns = [("matmul", dict(out=PB[ub][:, :], lhsT=WUP[s][:, k, c * 128:(c + 1) * 128],
                                       rhs=MT[:, k, sg * 512:(sg + 1) * 512], start=(k == 0), stop=(k == 7)))
                       for k in range(8)]
                P.group("pe", fns, reads=[bWUP[s]] + bM[sg * 4:sg * 4 + 4], writes=[bPB[ub]])
                P.op("act", "activation", dict(out=R32[ub], in_=PB[ub][:, :], func=AF.Relu),
                     reads=[bPB[ub]], writes=[bR32[ub]])
                P.op("act", "activation", dict(out=UT[utb][:, c, :], in_=R32[ub], func=AF.Square),
                     reads=[bR32[ub]], writes=[bUT[utb]])

        def down_proj(fb, sg):
            s = fb % 2
            utb = utb_of[(fb, sg)]
            for tl in range(4):
                t = sg * 4 + tl
                for half in range(2):
                    yb = 2 + ((tl * 2 + half) % 4)
                    fns = [("matmul", dict(out=PB[yb][:, :], lhsT=UT[utb][:, c, tl * 128:(tl + 1) * 128],
                                           rhs=WDN[s][:, c, half * 512:(half + 1) * 512], start=(c == 0), stop=(c == 3)))
                           for c in range(4)]
                    P.group("pe", fns, reads=[bUT[utb], bWDN[s][half]], writes=[bPB[yb]])
                    P.op("dve", "tensor_tensor", dict(out=H[:, t, half * 512:(half + 1) * 512], in0=PB[yb][:, :],
                                                      in1=H[:, t, half * 512:(half + 1) * 512], op=ALU.add),
                         reads=[bPB[yb], bH[t]], writes=[bH[t]])
                if fb == NFB - 1:
                    tq = q0 + t
                    P.dma("sp", dict(out=yseq[tq * 128:(tq + 1) * 128, :], in_=H[:, t, :]), bYST, reads=[bH[t]])

        load_w(1)
        up_proj(*pairs[0])
        for j, (fb, sg) in enumerate(pairs):
            if j + 1 < len(pairs):
                up_proj(*pairs[j + 1])
            down_proj(fb, sg)
            if sg == 3 and fb + 2 < NFB:
                load_w(fb + 2)
        P.barrier()

    npad = int(os.environ.get("PADPE", "0"))
    if npad:
        fns = [("matmul", dict(out=PB[7][:, 0:128], lhsT=IDENT, rhs=IDENT, start=True, stop=True)) for _ in range(npad)]
        P.group("pe", fns, reads=[bIDENT], writes=[bPB[7]])
    npad = int(os.environ.get("PADPOOL", "0"))
    for _ in range(npad):
        P.op("pool", "memset", dict(ap=EPST, constant=EPS), writes=[bEPS])
    P.finish()
    P.emit()
    return nc


def prep_weights(norm_attn, w_in, q_norm_a, k_norm_a, q_norm_b, k_norm_b, sink_b,
                 out_norm_a, out_norm_b, w_o, norm_mlp, w_up, w_down):
    f = np.float32
    w_in = np.asarray(w_in, f)[0]
    w_o = np.asarray(w_o, f)[0]
    ar = np.arange
    qb_cols = np.concatenate([1536 + h * 64 + ar(64) for h in PERM_HEADS])
    cols = np.concatenate([ar(0, 512), qb_cols, 512 + ar(512), 1024 + ar(512), 2048 + ar(128), 2176 + ar(128)])
    w_in_p = np.ascontiguousarray(w_in[:, cols])
    rows_b = np.concatenate([512 + h * 64 + ar(64) for h in PERM_HEADS])
    w_o_p = np.ascontiguousarray(w_o[np.concatenate([ar(512), rows_b])])
    gout = np.concatenate([np.asarray(out_norm_a, f)[0], np.asarray(out_norm_b, f)[0][rows_b - 512]])

    def chunked(g):
        return np.ascontiguousarray(np.asarray(g, f).reshape(8, 128).T)

    cos, sin = _rope_tables()
    masks = np.ascontiguousarray(np.concatenate(_MASKS, axis=1)).astype(ml_dtypes.bfloat16)
    return {
        "w_in": w_in_p, "w_o": w_o_p,
        "w_up": np.ascontiguousarray(np.asarray(w_up, f)[0]),
        "w_down": np.ascontiguousarray(np.asarray(w_down, f)[0]),
        "gin": chunked(np.asarray(norm_attn, f)[0]), "gmlp": chunked(np.asarray(norm_mlp, f)[0]),
        "gout": chunked(gout),
        "qkg": np.ascontiguousarray(np.concatenate([np.asarray(q_norm_a, f)[0], np.asarray(q_norm_b, f)[0],
                                                    np.asarray(k_norm_a, f)[0], np.asarray(k_norm_b, f)[0]])[None, :]),
        "sinkp": np.ascontiguousarray(np.asarray(sink_b, f)[0][PERM_HEADS][None, :]),
        "cost": cos, "sint": sin, "masks": masks,
        "ident": np.eye(128, dtype=np.float32).astype(ml_dtypes.bfloat16),
    }


FULL_GROUPS = [("p", 0, 0, 0, 16), ("p", 1, 0, 0, 16), ("s", 0, 0, 0, 24), ("s", 0, 16, 8, 24)]

_NC_CACHE = {}


def kernel(x_prompt, x_sample, norm_attn, w_in, q_norm_a, k_norm_a, q_norm_b, k_norm_b,
           sink_b, out_norm_a, out_norm_b, w_o, norm_mlp, w_up, w_down):
    n = 8
    x_prompt = np.asarray(x_prompt, np.float32)
    x_sample = np.asarray(x_sample, np.float32)
    shared = prep_weights(norm_attn, w_in, q_norm_a, k_norm_a, q_norm_b, k_norm_b, sink_b,
                          out_norm_a, out_norm_b, w_o, norm_mlp, w_up, w_down)
    if "full" not in _NC_CACHE:
        _NC_CACHE["full"] = build_program(FULL_GROUPS, 2, 2048, 1, 4096)
    nc = _NC_CACHE["full"]
    in_maps = []
    for c in range(n):
        m = dict(shared)
        m["xp"] = np.ascontiguousarray(x_prompt[2 * c:2 * c + 2])
        m["xs"] = np.ascontiguousarray(x_sample[c:c + 1])
        in_maps.append(m)
    res = run_bass_kernel_spmd(nc, in_maps, core_ids=list(range(n)))
    yp = np.concatenate([r["yp"] for r in res.results], axis=0).astype(np.float32)
    ys = np.concatenate([r["ys"] for r in res.results], axis=0).astype(np.float32)
    return (yp, ys)
```

```python
import numpy as np
import ml_dtypes
import concourse.bass as bass
import concourse.mybir as mybir
from concourse.bass_utils import run_bass_kernel_spmd

F32 = mybir.dt.float32
BF16 = mybir.dt.bfloat16
AF = mybir.ActivationFunctionType
ALU = mybir.AluOpType
AX = mybir.AxisListType

D = 1024
HD = 64
IN_COLS = 2304
DFF = 4096
EPS = 1e-6
NQ = 16
PERM_HEADS = [0, 4, 1, 5, 2, 6, 3, 7]


class Buf:
    __slots__ = ("name", "w", "r", "dsem", "dcount", "excl")

    def __init__(self, name, excl=False):
        self.name = name
        self.excl = excl
        self.w = None
        self.r = {}
        self.dsem = None
        self.dcount = 0


class Eng:
    def __init__(self, key, sem):
        self.key = key
        self.sem = sem
        self.count = 0
        self.prog = []
        self.waited = {}


class Prog:
    def __init__(self, nc):
        self.nc = nc
        self.eng = {}
        for key in ("pe", "act", "dve", "pool", "sp"):
            self.eng[key] = Eng(key, nc.alloc_semaphore("s_" + key))
        self.bsem = nc.alloc_semaphore("s_bar")
        self.bcount = 0
        self.bufs = []
        self.dma_bufs = []
        self.fresh_dma_sems = False
        import os
        for j in range(int(os.environ.get("DUMMYSEM", "0"))):
            nc.alloc_semaphore("dummy%d" % j)

    def buf(self, name, excl=False):
        b = Buf(name, excl)
        self.bufs.append(b)
        return b

    def _wait(self, E, tick):
        sem, val = tick
        if E.key == "pe" and sem is E.sem:
            return
        if E.waited.get(sem, 0) >= val:
            return
        E.waited[sem] = val
        E.prog.append(("wait", sem, val))

    @staticmethod
    def _split(reads, writes):
        writes = list(writes)
        r2 = []
        for b in reads:
            if b.excl:
                if b not in writes:
                    writes.append(b)
            else:
                r2.append(b)
        return r2, writes

    def _deps(self, E, reads, writes):
        for b in reads:
            if b.w is not None:
                self._wait(E, b.w)
        for b in writes:
            if b.w is not None:
                self._wait(E, b.w)
            for s, v in b.r.items():
                self._wait(E, (s, v))

    def _commit(self, tick, reads, writes):
        for b in reads:
            b.r[tick[0]] = tick[1]
        for b in writes:
            b.w = tick
            b.r = {}

    def op(self, ek, name, kw, reads=(), writes=()):
        reads, writes = self._split(reads, writes)
        E = self.eng[ek]
        self._deps(E, reads, writes)
        E.count += 1
        tick = (E.sem, E.count)
        E.prog.append(("op", (name, kw), E.sem, 1))
        self._commit(tick, reads, writes)

    def group(self, ek, fns, reads=(), writes=()):
        reads, writes = self._split(reads, writes)
        E = self.eng[ek]
        self._deps(E, reads, writes)
        for fn in fns[:-1]:
            E.prog.append(("op", fn, None, 0))
        E.count += 1
        tick = (E.sem, E.count)
        E.prog.append(("op", fns[-1], E.sem, 1))
        self._commit(tick, reads, writes)

    def dma(self, ek, kw, primary, reads=(), writes=()):
        fn = ("dma_start", kw)
        E = self.eng[ek]
        self._deps(E, reads, writes)
        if primary.dsem is None:
            self.nsem = getattr(self, "nsem", 0) + 1
            primary.dsem = self.nc.alloc_semaphore("d%d_%s" % (self.nsem, primary.name))
            self.dma_bufs.append(primary)
        primary.dcount += 1
        tick = (primary.dsem, 16 * primary.dcount)
        E.prog.append(("op", fn, primary.dsem, 16))
        self._commit(tick, reads, writes)

    def barrier(self):
        sp = self.eng["sp"]
        for k in ("pe", "act", "dve", "pool"):
            E = self.eng[k]
            if E.count:
                self._wait(sp, (E.sem, E.count))
        for b in self.dma_bufs:
            self._wait(sp, (b.dsem, 16 * b.dcount))
        if self.fresh_dma_sems:
            for b in self.dma_bufs:
                b.dsem = None
                b.dcount = 0
            self.dma_bufs = []
        self.bcount += 1
        sp.prog.append(("inc", self.bsem, 1))
        for k in ("pe", "act", "dve", "pool"):
            self.eng[k].prog.append(("wait", self.bsem, self.bcount))
        for b in self.bufs:
            b.w = None
            b.r = {}

    def finish(self):
        sp = self.eng["sp"]
        for k in ("pe", "act", "dve", "pool"):
            E = self.eng[k]
            if E.count:
                self._wait(sp, (E.sem, E.count))
        for b in self.dma_bufs:
            self._wait(sp, (b.dsem, 16 * b.dcount))

    def emit(self):
        nc = self.nc

        def replay(E):
            def f(eng):
                for item in E.prog:
                    if item[0] == "wait":
                        eng.wait_ge(item[1], item[2])
                    elif item[0] == "inc":
                        eng.sem_inc(item[1], item[2])
                    elif item[0] == "clear":
                        eng.sem_clear(item[1])
                    else:
                        ins = getattr(eng, item[1][0])(**item[1][1])
                        if item[2] is not None:
                            ins.then_inc(item[2], item[3])
            return f

        with nc.Block() as block:
            block.sync(replay(self.eng["sp"]))
            block.gpsimd(replay(self.eng["pool"]))
            block.scalar(replay(self.eng["act"]))
            block.vector(replay(self.eng["dve"]))
            block.tensor(replay(self.eng["pe"]))


def _mask_tables():
    a = np.arange(128)[:, None]
    b = np.arange(128)[None, :]
    tabs = []
    idxA = {}
    for dl in range(-2, 3):
        diff = 128 * dl + a - b
        m = (np.abs(diff) <= 64).astype(np.float32)
        m += ((diff % 4 == 0) & (np.abs(diff) <= 256)).astype(np.float32)
        idxA[dl] = len(tabs)
        tabs.append(m)
    idxB = {}
    for dl in (-1, 1):
        diff = 128 * dl + a - b
        m = (np.abs(diff) <= 128).astype(np.float32)
        idxB[dl] = len(tabs)
        tabs.append(m)
    idxD = {}
    for off in (0, 128, -64, 64):
        m = (np.abs(off + a - b) <= 64).astype(np.float32)
        idxD[off] = len(tabs)
        tabs.append(m)
    return tabs, idxA, idxB, idxD


_MASKS, _IDXA, _IDXB, _IDXD = _mask_tables()
NM = len(_MASKS)


def _rope_tables():
    half = 8
    inv = 500000.0 ** (-(np.arange(half, dtype=np.float64) * 2.0 / 16.0))
    pos = np.arange(4096, dtype=np.float64)
    ang = pos[:, None] * inv[None, :]
    cos = np.cos(ang).astype(np.float32).reshape(32, 128, half).transpose(1, 0, 2).reshape(128, 32 * half)
    sin = np.sin(ang).astype(np.float32).reshape(32, 128, half).transpose(1, 0, 2).reshape(128, 32 * half)
    return np.ascontiguousarray(cos), np.ascontiguousarray(sin)


def build_program(groups, n_p, len_p, n_s, len_s, stop=None, n1a=None, n1b=None):
    nc = bass.Bass("TRN2", target_bir_lowering=False)
    P = Prog(nc)

    def din(name, shape, dt=F32):
        return nc.dram_tensor(name, list(shape), dt, kind="ExternalInput").ap()

    xsrc = {}
    ydst = {}
    if n_p:
        xsrc["p"] = din("xp", [n_p, len_p, D])
        ydst["p"] = nc.dram_tensor("yp", [n_p, len_p, D], F32, kind="ExternalOutput").ap()
    if n_s:
        xsrc["s"] = din("xs", [n_s, len_s, D])
        ydst["s"] = nc.dram_tensor("ys", [n_s, len_s, D], F32, kind="ExternalOutput").ap()
    w_in = din("w_in", [D, IN_COLS])
    w_o = din("w_o", [D, D])
    w_up = din("w_up", [D, DFF])
    w_dn = din("w_down", [DFF, D])
    d_gin = din("gin", [128, 8])
    d_gmlp = din("gmlp", [128, 8])
    d_gout = din("gout", [128, 8])
    d_qkg = din("qkg", [1, 256])
    d_sink = din("sinkp", [1, 8])
    d_cos = din("cost", [128, 256])
    d_sin = din("sint", [128, 256])
    d_masks = din("masks", [128, NM * 128], BF16)
    d_ident = din("ident", [128, 128], BF16)

    def sb(name, shape, dt):
        return nc.alloc_sbuf_tensor("sb_" + name, shape, dt)
    WIN_t = sb("WIN", [128, 8 * IN_COLS], BF16)
    WIN = WIN_t.ap().rearrange("p (k c) -> p k c", k=8)
    R1 = sb("R1", [128, 16384], BF16)
    QT = R1.ap().rearrange("p (m t) -> p m t", m=8)
    WUP = [R1.ap()[:, s * 8192:s * 8192 + 4096].rearrange("p (k f) -> p k f", k=8) for s in range(2)]
    WDN = [R1.ap()[:, s * 8192 + 4096:(s + 1) * 8192].rearrange("p (c n) -> p c n", c=4) for s in range(2)]
    R2 = sb("R2", [128, 32768], BF16)
    KT = R2.ap()[:, 0:15360].rearrange("p (m t) -> p m t", m=5)
    V = R2.ap()[:, 15360:15360 + 15600].rearrange("p (c h d) -> p c h d", c=24, h=10)
    H = R2.ap().bitcast(F32).rearrange("p (t f) -> p t f", t=16)
    R3 = sb("R3", [128, 16384], BF16)
    MT = R3.ap().rearrange("p (m t) -> p m t", m=8)
    r3f = R3.ap().bitcast(F32)
    STG = r3f[:, 0:1664]
    SQ = r3f[:, 1664:3328]
    ROPEIN = r3f[:, 3328:3328 + 416]
    R4 = sb("R4", [128, 8192], BF16)
    r4f = R4.ap().bitcast(F32)
    QN32 = r4f[:, 0:1664]
    QKB = R4.ap()[:, 3328:3328 + 1664]
    PTB = [R4.ap()[:, 4992 + s * 512:4992 + (s + 1) * 512] for s in range(2)]
    O32 = r4f[:, 3008:3008 + 1024]
    PT4 = [PTB[0], PTB[1], R4.ap()[:, 0:512], R4.ap()[:, 512:1024], R4.ap()[:, 4112:4624]]
    QZ2 = [None, R4.ap()[:, 1024:3072].rearrange("p (m v t) -> p m v t", m=8, v=2)]
    WO = R1.ap()[:, 8192:16384].rearrange("p (k n) -> p k n", k=8)
    UT = [R4.ap()[:, s * 2048:(s + 1) * 2048].rearrange("p (c t) -> p c t", c=4) for s in range(2)]
    R32 = [r4f[:, 2048 + s * 512:2048 + (s + 1) * 512] for s in range(2)]

    QZ = sb("qz", [128, 8, 2, 128], BF16).ap()
    QZ2[0] = QZ
    XT = [sb("xt%d" % s, [128, D], F32).ap() for s in range(2)]
    XS = sb("xsb", [128, D], BF16).ap()
    JUNK = sb("junk", [128, D], BF16).ap()
    XNT = sb("xnT", [128, 8, 128], BF16).ap()
    xnt_flat = XNT.rearrange("p k t -> p (k t)")
    PT4 += [xnt_flat[:, 0:512], xnt_flat[:, 512:1024]]
    MASKS = sb("masks", [128, NM, 128], BF16).ap()
    COS = sb("cos", [128, 32, 8], F32).ap()
    SIN = sb("sin", [128, 32, 8], F32).ap()
    IDENT = sb("ident", [128, 128], BF16).ap()
    GIN = sb("gin", [128, 8], F32).ap()
    GMLP = sb("gmlp", [128, 8], F32).ap()
    GOUT = sb("gout", [128, 8], F32).ap()
    QKG = sb("qkg", [128, 4, 64], F32).ap()
    SINK = sb("sink", [128, 8], F32).ap()
    ESINK = sb("esink", [128, 8], F32).ap()
    EPST = sb("epst", [128, 1], F32).ap()
    SS = sb("ss", [128, 2], F32).ap()
    LNV = sb("lnv", [128, 2], F32).ap()
    RSTD = sb("rstd", [128, 2], F32).ap()
    SSQ = sb("ssq", [128, 26], F32).ap()
    LNQ = sb("lnq", [128, 26], F32).ap()
    RQ = sb("rq", [128, 26], F32).ap()
    RT = [r3f[:, 3744 + j * 208:3744 + (j + 1) * 208].rearrange("p (h d) -> p h d", d=8) for j in range(4)]
    ODS = r4f[:, 1536:1536 + 520]
    VDS = [XT[s_].bitcast(BF16)[:, 0:1040].rearrange("p (c f) -> p c f", c=2) for s_ in range(2)]
    ODT = [XT[s_][:, 0:520] for s_ in range(2)]
    vd_dram = nc.dram_tensor("vd_scratch", [24 * 128, 520], BF16, kind="Internal").ap()
    od_dram = nc.dram_tensor("od_scratch", [NQ * 128, 520], F32, kind="Internal").ap()
    kt_scr = nc.dram_tensor("kt_scratch", [128, 5 * 1024], BF16, kind="Internal").ap()
    v_scr = nc.dram_tensor("v_scratch", [128, 8 * 650], BF16, kind="Internal").ap()
    DEN = sb("den", [128, 16], F32).ap()
    RDEN = sb("rden", [128, 16], F32).ap()

    PB = [nc.alloc_psum_tensor("pb%d" % j, [128, 512], F32).ap() for j in range(8)]
    PBh = [p.bitcast(BF16) for p in PB]

    WIN_COLS = [(0, 512), (512, 1024), (1024, 1536), (1536, 2048), (2048, 2304)]
    bWIN = [P.buf("win%d" % k) for k in range(5)]
    bXT = [P.buf("xt%d" % s) for s in range(2)]
    bXS, bJUNK, bXNT = P.buf("xs"), P.buf("junk"), P.buf("xnt")
    bMASKS, bCOS, bSIN, bIDENT = P.buf("masks"), P.buf("cos"), P.buf("sin"), P.buf("ident")
    bGIN, bGMLP, bGOUT, bQKG = P.buf("gin"), P.buf("gmlp"), P.buf("gout"), P.buf("qkg")
    bSINK, bESINK, bEPS = P.buf("sink"), P.buf("esink"), P.buf("eps")
    bSS, bLNV, bRSTD = P.buf("ss"), P.buf("lnv"), P.buf("rstd")
    bSSQ, bLNQ, bRQ = P.buf("ssq"), P.buf("lnq"), P.buf("rq")
    bRT = [P.buf("rt%d" % j) for j in range(4)]
    bDEN, bRDEN = P.buf("den"), P.buf("rden")
    bPB = [P.buf("pb%d" % j, excl=True) for j in range(8)]
    bQN32, bQKB, bO32 = P.buf("qn32"), P.buf("qkb"), P.buf("o32")
    bSTG, bSQ, bRIN = P.buf("stg"), P.buf("sq"), P.buf("rin")
    bXNTb = P.buf("xntb")
    bPT = [P.buf("pt%d" % s) for s in range(2)]
    bKT = [P.buf("kt%d" % c) for c in range(24)]
    bV = [P.buf("v%d" % c) for c in range(24)]
    bQT = [P.buf("qt%d" % i) for i in range(NQ)]
    bM = [P.buf("m%d" % i) for i in range(NQ)]
    bH = [P.buf("h%d" % i) for i in range(NQ)]
    bWO = [P.buf("wo%d" % hh) for hh in range(2)]
    bQZ = P.buf("qz")
    bQZ2 = [bQZ, P.buf("qz1")]
    bQZlo = [P.buf("qzlo0"), P.buf("qzlo1")]
    bQZhi = [P.buf("qzhi0"), P.buf("qzhi1")]
    bPT4 = [bPT[0], bPT[1], P.buf("pt2"), P.buf("pt3"), P.buf("pt4"), P.buf("pt5"), P.buf("pt6")]
    NPT = len(bPT4)
    bYST = P.buf("yst")
    bVDd, bVDW, bODd, bODS = P.buf("vdd"), P.buf("vdw"), P.buf("odd"), P.buf("ods")
    bVDS = [P.buf("vds%d" % b) for b in range(4)]
    bKTS, bVSS, bKTL = P.buf("kts"), P.buf("vss"), P.buf("ktl")
    bVL = [P.buf("vl0"), P.buf("vl1")]
    bWUP = [P.buf("wup%d" % s) for s in range(2)]
    bWDN = [[P.buf("wdn%d_%d" % (s, hh)) for hh in range(2)] for s in range(2)]
    bUT = [P.buf("ut%d" % s) for s in range(2)]
    bR32 = [P.buf("r32%d" % s) for s in range(2)]

    P.dma("sp", dict(out=IDENT, in_=d_ident), bIDENT, writes=[bIDENT])
    P.dma("sp", dict(out=MASKS, in_=d_masks.rearrange("p (m t) -> p m t", m=NM)), bMASKS, writes=[bMASKS])
    P.dma("sp", dict(out=COS, in_=d_cos.rearrange("p (t f) -> p t f", t=32)), bCOS, writes=[bCOS])
    P.dma("sp", dict(out=SIN, in_=d_sin.rearrange("p (t f) -> p t f", t=32)), bSIN, writes=[bSIN])
    P.dma("sp", dict(out=GIN, in_=d_gin), bGIN, writes=[bGIN])
    P.dma("sp", dict(out=GMLP, in_=d_gmlp), bGMLP, writes=[bGMLP])
    P.dma("sp", dict(out=GOUT, in_=d_gout), bGOUT, writes=[bGOUT])
    P.dma("sp", dict(out=QKG.rearrange("p a b -> p (a b)"), in_=d_qkg.partition_broadcast(128)), bQKG, writes=[bQKG])
    P.dma("sp", dict(out=SINK, in_=d_sink.partition_broadcast(128)), bSINK, writes=[bSINK])
    P.op("pool", "memset", dict(ap=EPST, constant=EPS), writes=[bEPS])
    P.op("pool", "memset", dict(ap=QZ, constant=0.0), writes=[bQZlo[0], bQZhi[0]])
    for j, (a0, a1) in enumerate(WIN_COLS):
        P.dma("pool", dict(out=WIN[:, :, a0:a1], in_=w_in[:, a0:a1].rearrange("(k p) f -> p k f", p=128)),
              bWIN[j], writes=[bWIN[j]])
    P.op("act", "activation", dict(out=ESINK, in_=SINK, func=AF.Exp), reads=[bSINK], writes=[bESINK])

    def rstd_chain(ss_ap, ln_ap, r_ap, n, bss, bln, br):
        P.op("act", "activation", dict(out=ln_ap, in_=ss_ap, func=AF.Ln, scale=1.0 / n, bias=EPST),
             reads=[bss, bEPS], writes=[bln])
        P.op("act", "activation", dict(out=r_ap, in_=ln_ap, func=AF.Exp, scale=-0.5), reads=[bln], writes=[br])

    import os
    CUT = int(os.environ.get("CUT", "99"))
    state = {"x": 0, "s": 0, "u": 0, "y": 0, "mm": 0}

    def load_x(xrows):
        slot = state["x"] % 2
        state["x"] += 1
        P.dma("sp", dict(out=XT[slot], in_=xrows), bXT[slot], writes=[bXT[slot]])
        return slot

    def transposes_xs(ps_idx):
        fns = [("transpose", dict(out=PBh[ps_idx][:, k * 128:(k + 1) * 128], in_=XS[:, k * 128:(k + 1) * 128],
                                  identity=IDENT)) for k in range(8)]
        P.group("pe", fns, reads=[bXS, bIDENT], writes=[bPB[ps_idx]])

    def norm_transpose(src_ap, bsrc, ps_idx):
        P.op("act", "activation", dict(out=JUNK, in_=src_ap, func=AF.Square, accum_out=SS[:, 0:1]),
             reads=[bsrc], writes=[bJUNK, bSS])
        rstd_chain(SS[:, 0:1], LNV[:, 0:1], RSTD[:, 0:1], D, bSS, bLNV, bRSTD)
        P.op("act", "activation", dict(out=XS, in_=src_ap, func=AF.Copy, scale=RSTD[:, 0:1]),
             reads=[bsrc, bRSTD], writes=[bXS])
        transposes_xs(ps_idx)

    def evac_T(ps_idx, out_ap, gain_ap, bgain, bout):
        P.op("dve", "tensor_tensor", dict(out=out_ap, in0=PBh[ps_idx].rearrange("p (k t) -> p k t", k=8),
                                          in1=gain_ap.unsqueeze(2).to_broadcast([128, 8, 128]), op=ALU.mult),
             reads=[bPB[ps_idx], bgain], writes=[bout])

    def hv(ap):
        return ap.rearrange("p (h d) -> p h d", d=64)

    for gspec in groups:
        (src, seq, q0, c0, nctx) = gspec[:5]
        halo_mode = gspec[5] if len(gspec) > 5 else None
        if stop == "setup":
            break
        xseq = xsrc[src][seq]
        yseq = ydst[src][seq]
        L = xseq.shape[0]
        nts = L // 128

        P.op("pool", "memset", dict(ap=V[:, :, :, 64:65], constant=1.0), writes=bV[:nctx])

        n_c = nctx if n1a is None else n1a
        c_start = 0
        if halo_mode == "load":
            c_start = 8
            P.dma("sp", dict(out=KT[:, :, 0:1024], in_=kt_scr.rearrange("p (m t) -> p m t", m=5)), bKTL,
                  reads=[bKTS], writes=bKT[0:8] + [bKTL])
            for half_ in range(2):
                P.dma("sp", dict(out=V[:, half_ * 4:(half_ + 1) * 4, :, :].rearrange("p c h d -> p (c h d)"),
                                 in_=v_scr[:, half_ * 2600:(half_ + 1) * 2600]), bVL[half_],
                      reads=[bVSS], writes=bV[half_ * 4:(half_ + 1) * 4] + [bVL[half_]])
            for c_ in range(8):
                P.dma("pool", dict(out=vd_dram[c_ * 128:(c_ + 1) * 128, :],
                                   in_=V[:, c_, 0:8, :].rearrange("p h d -> p (h d)")),
                      bVDW, reads=[bV[c_]], writes=[bVDd])
        BANKS = [(0, 0, 512), (1, 512, 1024), (2, 1024, 1536), (3, 1536, 2048), (4, 2048, 2304)]
        SEGS = [(0, 0, 8, 0), (1, 8, 8, 1), (2, 16, 8, 2), (4, 24, 2, 3)]
        stv = hv(STG)
        kb = hv(QKB)
        rin = ROPEIN.rearrange("p (h d) -> p h d", d=16)

        def info(c):
            ts = c0 + c
            own = q0 <= ts < q0 + NQ
            return dict(ts=ts, own=own, qi=ts - q0, banks=BANKS if own else BANKS[2:],
                        segs=SEGS if own else SEGS[2:], h_lo=0 if own else 16)

        slots1a = {}

        def st_ldx(c):
            nf = info(c)
            slots1a[c] = load_x(xseq[nf["ts"] * 128:(nf["ts"] + 1) * 128, :])

        def st_A(c):
            nf = info(c)
            slot = slots1a[c]
            P.op("act", "activation", dict(out=JUNK, in_=XT[slot], func=AF.Square, accum_out=SS[:, 0:1]),
                 reads=[bXT[slot]], writes=[bJUNK, bSS])
            rstd_chain(SS[:, 0:1], LNV[:, 0:1], RSTD[:, 0:1], D, bSS, bLNV, bRSTD)
            P.op("act", "activation", dict(out=XS, in_=XT[slot], func=AF.Copy, scale=RSTD[:, 0:1]),
                 reads=[bXT[slot], bRSTD], writes=[bXS])

        XNT2 = [XNT, R3.ap()[:, 12288:13312].rearrange("p (k t) -> p k t", k=8)]
        bXNT2 = [bXNT, bXNTb]

        def st_MM(c, which):
            nf = info(c)
            xnt, bxnt = XNT2[c % 2], bXNT2[c % 2]
            part = [b for b in nf["banks"] if (b[0] < 2) == (which == 0)]
            if not part:
                return
            fns = []
            for k in range(8):
                for (bk, a0, a1) in part:
                    fns.append(("matmul", dict(out=PB[bk][:, 0:a1 - a0], lhsT=xnt[:, k, :], rhs=WIN[:, k, a0:a1],
                                               start=(k == 0), stop=(k == 7))))
            P.group("pe", fns, reads=[bxnt] + [bWIN[bk] for (bk, _, _) in part],
                    writes=[bPB[bk] for (bk, _, _) in part])

        def st_C(c, part):
            nf = info(c)
            for (bk, h0, nh, gt) in [sg_ for sg_ in nf["segs"] if (sg_[0] < 2) == (part == 0)]:
                P.op("act", "activation", dict(out=SQ[:, h0 * 64:(h0 + nh) * 64], in_=PB[bk][:, 0:nh * 64], func=AF.Square),
                     reads=[bPB[bk]], writes=[bSQ])
                P.op("dve", "tensor_tensor", dict(
                    out=hv(STG[:, h0 * 64:(h0 + nh) * 64]), in0=hv(PB[bk][:, 0:nh * 64]),
                    in1=QKG[:, gt, :].unsqueeze(1).to_broadcast([128, nh, 64]), op=ALU.mult),
                    reads=[bPB[bk], bQKG], writes=[bSTG])
            if part == 0:
                return
            P.op("act", "activation", dict(out=V[:, c, 0:8, 0:64], in_=hv(PB[3][:, :]), func=AF.Copy),
                 reads=[bPB[3]], writes=[bV[c]])
            P.op("act", "activation", dict(out=V[:, c, 8:10, 0:64], in_=hv(PB[4][:, 128:256]), func=AF.Copy),
                 reads=[bPB[4]], writes=[bV[c]])
            P.dma("pool", dict(out=vd_dram[c * 128:(c + 1) * 128, :], in_=V[:, c, 0:8, :].rearrange("p h d -> p (h d)")),
                  bVDW, reads=[bV[c]], writes=[bVDd])

        def st_D(c):
            nf = info(c)
            h_lo, h_hi = nf["h_lo"], 26
            nh_all = h_hi - h_lo
            ts = nf["ts"]
            P.op("dve", "tensor_reduce", dict(out=SSQ[:, h_lo:h_hi], in_=hv(SQ[:, h_lo * 64:h_hi * 64]),
                                              axis=AX.X, op=ALU.add), reads=[bSQ], writes=[bSSQ])
            rstd_chain(SSQ[:, h_lo:h_hi], LNQ[:, h_lo:h_hi], RQ[:, h_lo:h_hi], HD, bSSQ, bLNQ, bRQ)
            P.op("dve", "tensor_tensor", dict(
                out=kb[:, h_lo:h_hi, 16:64], in0=stv[:, h_lo:h_hi, 16:64],
                in1=RQ[:, h_lo:h_hi].unsqueeze(2).to_broadcast([128, nh_all, 48]), op=ALU.mult),
                reads=[bSTG, bRQ], writes=[bQKB])
            P.op("dve", "tensor_tensor", dict(
                out=rin[:, h_lo:h_hi, :], in0=stv[:, h_lo:h_hi, 0:16],
                in1=RQ[:, h_lo:h_hi].unsqueeze(2).to_broadcast([128, nh_all, 16]), op=ALU.mult),
                reads=[bSTG, bRQ], writes=[bRIN])
            x1 = rin[:, h_lo:h_hi, 0:8]
            x2 = rin[:, h_lo:h_hi, 8:16]
            cosb = COS[:, ts, :].unsqueeze(1).to_broadcast([128, nh_all, 8])
            sinb = SIN[:, ts, :].unsqueeze(1).to_broadcast([128, nh_all, 8])
            for j, (xa, tb, bt) in enumerate([(x1, cosb, bCOS), (x2, sinb, bSIN), (x2, cosb, bCOS), (x1, sinb, bSIN)]):
                P.op("pool", "tensor_tensor", dict(out=RT[j][:, h_lo:h_hi, :], in0=xa, in1=tb, op=ALU.mult),
                     reads=[bRIN, bt], writes=[bRT[j]])
            P.op("dve", "tensor_tensor", dict(out=kb[:, h_lo:h_hi, 0:8], in0=RT[0][:, h_lo:h_hi, :],
                                              in1=RT[1][:, h_lo:h_hi, :], op=ALU.subtract),
                 reads=[bRT[0], bRT[1]], writes=[bQKB])
            P.op("dve", "tensor_tensor", dict(out=kb[:, h_lo:h_hi, 8:16], in0=RT[2][:, h_lo:h_hi, :],
                                              in1=RT[3][:, h_lo:h_hi, :], op=ALU.add),
                 reads=[bRT[2], bRT[3]], writes=[bQKB])

        def st_T2(c):
            nf = info(c)
            if nf["own"]:
                fns = [("transpose", dict(out=PBh[6][:, m * 128:(m + 1) * 128], in_=QKB[:, m * 128:(m + 1) * 128],
                                          identity=IDENT)) for m in range(8)]
                P.group("pe", fns, reads=[bQKB, bIDENT], writes=[bPB[6]])
            fns = [("transpose", dict(out=PBh[7][:, m * 128:(m + 1) * 128],
                                      in_=QKB[:, 1024 + m * 128:1024 + (m + 1) * 128], identity=IDENT)) for m in range(5)]
            P.group("pe", fns, reads=[bQKB, bIDENT], writes=[bPB[7]])

        def st_E(c):
            nf = info(c)
            if nf["own"]:
                qi = nf["qi"]
                P.op("dve", "tensor_copy", dict(out=QT[:, :, qi * 128:(qi + 1) * 128],
                                                in_=PBh[6].rearrange("p (m t) -> p m t", m=8)),
                     reads=[bPB[6]], writes=[bQT[qi]])
            P.op("act", "activation", dict(out=KT[:, :, c * 128:(c + 1) * 128],
                                           in_=PBh[7][:, 0:640].rearrange("p (m t) -> p m t", m=5), func=AF.Copy),
                 reads=[bPB[7]], writes=[bKT[c]])

        st_ldx(c_start)
        if n_c > c_start + 1:
            st_ldx(c_start + 1)
        st_A(c_start)
        for c in range(c_start, n_c + 2):
            if c_start <= c - 2 < n_c:
                st_D(c - 2)
            if c_start <= c - 1 < n_c:
                st_MM(c - 1, 0)
            if c < n_c:
                transposes_xs(5)
            if c_start <= c - 1 < n_c:
                st_C(c - 1, 0)
            if c < n_c:
                evac_T(5, XNT2[c % 2], GIN, bGIN, bXNT2[c % 2])
            if c_start <= c - 1 < n_c:
                st_MM(c - 1, 1)
            if c_start <= c - 2 < n_c:
                st_T2(c - 2)
            if c_start <= c - 1 < n_c:
                st_C(c - 1, 1)
            if c_start <= c - 2 < n_c:
                st_E(c - 2)
            if c + 2 < n_c:
                st_ldx(c + 2)
            if c + 1 < n_c:
                st_A(c + 1)

        if halo_mode == "save":
            P.dma("pool", dict(out=kt_scr.rearrange("p (m t) -> p m t", m=5), in_=KT[:, :, 1024:2048]), bKTS,
                  reads=bKT[8:16], writes=[bKTS])
            for half_ in range(2):
                P.dma("pool", dict(out=v_scr[:, half_ * 2600:(half_ + 1) * 2600],
                                   in_=V[:, 8 + half_ * 4:8 + (half_ + 1) * 4, :, :].rearrange("p c h d -> p (c h d)")),
                      bVSS, reads=bV[8 + half_ * 4:8 + (half_ + 1) * 4], writes=[bVSS])

        if stop == "1a":
            break
        def alias_after(dst_bufs, src_bufs):
            for d_ in dst_bufs:
                for s_ in src_bufs:
                    for sem_, v_ in s_.r.items():
                        d_.r[sem_] = max(d_.r.get(sem_, 0), v_)
                    if s_.w is not None:
                        d_.r[s_.w[0]] = max(d_.r.get(s_.w[0], 0), s_.w[1])

        alias_after([bODS, bPT4[4]], [bQKB])
        alias_after([bPT4[5], bPT4[6]], [bXNT])
        alias_after(bM, [bSTG, bSQ, bRIN, bXNTb] + bRT)
        P.op("pool", "memset", dict(ap=QZ2[1], constant=0.0), writes=[bQZlo[1], bQZhi[1]])
        n_q = NQ if n1b is None else n1b
        LA = 5
        SBANK = [0, 1, 7]

        def sl(st_, n_, step):
            return slice(st_, st_ + step * (n_ - 1) + 1, step)

        n_kpos = nctx * 8
        kchunks = [(kc, min(128, n_kpos - 128 * kc)) for kc in range((n_kpos + 127) // 128)]
        vd_cls = vd_dram.rearrange("(a s) f -> s a f", s=16)
        od_cls = od_dram.rearrange("(j s) f -> s j f", s=16)

        NVB = 4 if len(kchunks) == 1 else 2
        nkc = len(kchunks)
        VDSn = [XT[b % 2].bitcast(BF16)[:, (b // 2) * 520 * nkc:(b // 2 + 1) * 520 * nkc].rearrange("p (c f) -> p c f", c=nkc)
                for b in range(NVB)]

        def load_vd(r):
            b = r % NVB
            for kc, na in kchunks:
                P.dma("sp", dict(out=VDSn[b][0:na, kc, :], in_=vd_cls[r][128 * kc:128 * kc + na]),
                      bVDS[b], reads=[bVDd], writes=[bVDS[b], bXT[b % 2]])

        def fill_qz(kind, j):
            b = j % 2
            qz = QZ2[b]
            if kind == "D":
                P.op("act", "activation", dict(out=qz[0:64, 0:4, 0, :], in_=QT[0:64, 0:4, sl(j, 128, 16)], func=AF.Copy),
                     reads=bQT, writes=[bQZlo[b]])
                P.op("dve", "tensor_copy", dict(out=qz[64:128, 0:4, 1, :], in_=QT[64:128, 0:4, sl(j, 128, 16)]),
                     reads=bQT, writes=[bQZhi[b]])
            else:
                P.op("act", "activation", dict(out=qz[0:64, :, 0, :], in_=QT[0:64, :, j * 128:(j + 1) * 128], func=AF.Copy),
                     reads=[bQT[j]], writes=[bQZlo[b]])
                P.op("dve", "tensor_copy", dict(out=qz[64:128, :, 1, :], in_=QT[64:128, :, j * 128:(j + 1) * 128]),
                     reads=[bQT[j]], writes=[bQZhi[b]])

        groups_order = ([("D", r) for r in range(16)] if n_q == NQ else []) + [("N", i) for i in range(n_q)]

        def load_wo():
            for hh in range(2):
                P.dma("pool", dict(out=WO[:, :, hh * 512:(hh + 1) * 512],
                                   in_=w_o[:, hh * 512:(hh + 1) * 512].rearrange("(k p) n -> p k n", p=128)),
                      bWO[hh], writes=[bWO[hh]] + bQT + [bWUP[1]] + bWDN[1])

        def first_any(gidx):
            if gidx + 1 < len(groups_order):
                fill_qz(*groups_order[gidx + 1])
                if gidx + 2 == len(groups_order):
                    load_wo()

        def first_dil(r):
            first_any(r)

        def last_dil(r):
            if r + NVB < 16:
                load_vd(r + NVB)
            b0 = 2 + 2 * (r % 2)
            P.op("act", "activation", dict(out=ODS[:, 0:260], in_=PB[b0][:, 0:260], func=AF.Copy),
                 reads=[bPB[b0]], writes=[bODS])
            P.op("dve", "tensor_copy", dict(out=ODS[:, 260:520], in_=PB[b0 + 1][:, 0:260]),
                 reads=[bPB[b0 + 1]], writes=[bODS])
            P.dma("sp", dict(out=od_cls[r], in_=ODS), bODS, reads=[bODS], writes=[bODd])

        def first_nat(i):
            first_any((16 if n_q == NQ else 0) + i)

        def first_back_nat(i):
            P.dma("sp", dict(out=ODT[i % 2], in_=od_dram[i * 128:(i + 1) * 128, :]), bXT[i % 2],
                  reads=[bODd], writes=[bXT[i % 2]] + [bVDS[b] for b in range(4) if b % 2 == i % 2])

        iters = []
        fill_qz(*groups_order[0])
        if len(groups_order) == 1:
            load_wo()
        if n_q == NQ:
            for r in range(NVB):
                load_vd(r)
            for r in range(16):
                cls = []
                for hg in range(2):
                    for idx, (kc, na) in enumerate(kchunks):
                        off = (c0 * 8 + 128 * kc) - q0 * 8
                        cls.append(dict(
                            kind="A", g=hg, idx=idx, n=len(kchunks), nk=na, mi=_IDXD[off], ob=2 + hg + 2 * (r % 2),
                            qz=(QZ2[r % 2], bQZ2[r % 2]), qb=r % 2,
                            kt=(lambda ch, kc=kc, na=na, r=r: KT[:, ch, sl(2048 * kc + r, na, 16)]),
                            ktb=bKT[:nctx],
                            vfn=(lambda h, kc=kc, na=na, r=r: VDSn[r % NVB][0:na, kc, h * 65:(h + 1) * 65]),
                            vb=[bVDS[r % NVB]]))
                cls[0]["first"] = (lambda r=r: first_dil(r))
                cls[-1]["last"] = (lambda n_now, r=r: last_dil(r))
                iters += cls
        for i in range(n_q):
            tq = q0 + i
            deltas = [dl for dl in range(-2, 3) if 0 <= tq + dl < nts]
            deltas_b = [dl for dl in (-1, 0, 1) if 0 <= tq + dl < nts]
            tl = []
            for hg in range(2):
                for idx, dl in enumerate(deltas):
                    ck = tq + dl - c0
                    assert 0 <= ck < nctx
                    tl.append(dict(kind="A", g=hg, idx=idx, n=len(deltas), nk=128, mi=_IDXA[dl],
                                   qz=(QZ2[i % 2], bQZ2[i % 2]), qb=i % 2,
                                   kt=(lambda ch, ck=ck: KT[:, ch, ck * 128:(ck + 1) * 128]), ktb=[bKT[ck]],
                                   vfn=(lambda h, ck=ck: V[:, ck, h, :]), vb=[bV[ck]]))
            for e_kv in range(2):
                for idx, dl in enumerate(deltas_b):
                    ck = tq + dl - c0
                    assert 0 <= ck < nctx
                    tl.append(dict(kind="B", g=e_kv, idx=idx, n=len(deltas_b), nk=128,
                                   mi=(_IDXB[dl] if dl != 0 else None),
                                   qz=(QZ2[i % 2], bQZ2[i % 2]), qb=i % 2,
                                   kt=(lambda ch, ck=ck: KT[:, ch, ck * 128:(ck + 1) * 128]), ktb=[bKT[ck]],
                                   vfn=(lambda h, ck=ck, e_kv=e_kv: V[:, ck, 8 + e_kv, :]), vb=[bV[ck]]))
            tl[0]["first"] = (lambda i=i: first_nat(i))
            if n_q == NQ:
                tl[0]["first_back"] = (lambda i=i: first_back_nat(i))
            tl[-1]["last"] = (lambda n_now, i=i: epilogue(i, n_now))
            iters += tl

        def front(n, it):
            g, nk = it["g"], it["nk"]
            qz, bqz = it["qz"]
            if it.get("first"):
                it["first"]()
            sb_ = SBANK[n % 3]
            pt, bpt = PT4[n % NPT], bPT4[n % NPT]
            if it["kind"] == "A":
                fns = []
                for j in range(2):
                    ch = g * 2 + j
                    fns.append(("matmul", dict(out=PB[sb_][0:nk, j * 256:(j + 1) * 256], lhsT=it["kt"](ch),
                                               rhs=qz[:, ch, :, :], start=True, stop=True)))
            else:
                fns = [("matmul", dict(out=PB[sb_][0:nk, :], lhsT=it["kt"](4), rhs=qz[:, 4:8, g, :],
                                       start=True, stop=True))]
            P.group("pe", fns, reads=list(it["ktb"]) + [bQZlo[it["qb"]], bQZhi[it["qb"]]], writes=[bPB[sb_]])
            P.op("act", "activation", dict(out=pt[0:nk, :], in_=PB[sb_][0:nk, :], func=AF.Exp, scale=0.125),
                 reads=[bPB[sb_]], writes=[bpt])
            if it["mi"] is not None:
                ptv = pt[0:nk, :].rearrange("p (h t) -> p h t", h=4)
                P.op("dve", "tensor_tensor", dict(out=ptv, in0=ptv,
                                                  in1=MASKS[0:nk, it["mi"], :].unsqueeze(1).to_broadcast([nk, 4, 128]),
                                                  op=ALU.mult),
                     reads=[bpt, bMASKS], writes=[bpt])

        def back(n, it):
            g, nk, idx = it["g"], it["nk"], it["idx"]
            if it.get("first_back"):
                it["first_back"]()
            pt, bpt = PT4[n % NPT], bPT4[n % NPT]
            ob = it.get("ob", (2 + g) if it["kind"] == "A" else (4 + g))
            fns = []
            for hl in range(4):
                fns.append(("matmul", dict(out=PB[ob][:, hl * 65:(hl + 1) * 65],
                                           lhsT=pt[0:nk, hl * 128:(hl + 1) * 128], rhs=it["vfn"](g * 4 + hl),
                                           start=(idx == 0 and hl == 0), stop=(idx == it["n"] - 1),
                                           skip_group_check=True)))
            P.group("pe", fns, reads=[bpt] + list(it["vb"]), writes=[bPB[ob]])
            if it.get("last"):
                it["last"](n + LA)

        deferred = []

        def epilogue(i, n_now):
            xb = bXT[i % 2]
            obv = O32[:, 512:1024].rearrange("p (m e d) -> p m e d", e=2, d=64)

            def e1():
                for hg in range(2):
                    odt = ODT[i % 2][:, hg * 260:(hg + 1) * 260]
                    if n_q == NQ:
                        P.op("dve", "tensor_tensor", dict(out=odt, in0=PB[2 + hg][:, 0:260], in1=odt, op=ALU.add),
                             reads=[bPB[2 + hg], xb], writes=[xb])
                    else:
                        P.op("dve", "tensor_copy", dict(out=odt, in_=PB[2 + hg][:, 0:260]), reads=[bPB[2 + hg]], writes=[xb])
                for e_kv in range(2):
                    ov = PB[4 + e_kv][:, 0:260].rearrange("p (h d) -> p h d", d=65)
                    esv = ESINK.rearrange("p (m e) -> p m e", e=2)[:, :, e_kv]
                    dsl = slice(8 + e_kv * 4, 8 + (e_kv + 1) * 4)
                    P.op("dve", "tensor_copy", dict(out=obv[:, :, e_kv, :], in_=ov[:, :, 0:64]),
                         reads=[bPB[4 + e_kv]], writes=[bO32])
                    P.op("dve", "tensor_tensor", dict(out=DEN[:, dsl], in0=ov[:, :, 64], in1=esv, op=ALU.add),
                         reads=[bPB[4 + e_kv], bESINK], writes=[bDEN])

            def e2():
                for hg in range(2):
                    ov = ODT[i % 2][:, hg * 260:(hg + 1) * 260].rearrange("p (h d) -> p h d", d=65)
                    P.op("dve", "reciprocal", dict(out=RDEN[:, hg * 4:(hg + 1) * 4], in_=ov[:, :, 64]),
                         reads=[xb], writes=[bRDEN])
                    P.op("dve", "tensor_tensor", dict(
                        out=hv(O32[:, hg * 256:(hg + 1) * 256]), in0=ov[:, :, 0:64],
                        in1=RDEN[:, hg * 4:(hg + 1) * 4].unsqueeze(2).to_broadcast([128, 4, 64]), op=ALU.mult),
                        reads=[xb, bRDEN], writes=[bO32])
                P.op("dve", "reciprocal", dict(out=RDEN[:, 8:16], in_=DEN[:, 8:16]), reads=[bDEN], writes=[bRDEN])
                for e_kv in range(2):
                    dsl = slice(8 + e_kv * 4, 8 + (e_kv + 1) * 4)
                    P.op("dve", "tensor_tensor", dict(
                        out=obv[:, :, e_kv, :], in0=obv[:, :, e_kv, :],
                        in1=RDEN[:, dsl].unsqueeze(2).to_broadcast([128, 4, 64]), op=ALU.mult),
                        reads=[bO32, bRDEN], writes=[bO32])
                if os.environ.get("DUMP_O32"):
                    P.dma("sp", dict(out=yseq[(q0 + i) * 128:(q0 + i + 1) * 128, :], in_=O32), bO32, reads=[bO32])

            def e3():
                for half in range(2):
                    P.op("act", "activation", dict(out=JUNK[:, half * 512:(half + 1) * 512],
                                                   in_=O32[:, half * 512:(half + 1) * 512],
                                                   func=AF.Square, accum_out=SS[:, half:half + 1]),
                         reads=[bO32], writes=[bJUNK, bSS])
                rstd_chain(SS[:, 0:2], LNV[:, 0:2], RSTD[:, 0:2], 512, bSS, bLNV, bRSTD)

            def e4():
                for half in range(2):
                    P.op("dve", "tensor_scalar", dict(out=XS[:, half * 512:(half + 1) * 512],
                                                      in0=O32[:, half * 512:(half + 1) * 512],
                                                      scalar1=RSTD[:, half:half + 1], scalar2=None, op0=ALU.mult),
                         reads=[bO32, bRSTD], writes=[bXS])
                transposes_xs(6)

            def e5():
                evac_T(6, MT[:, :, i * 128:(i + 1) * 128], GOUT, bGOUT, bM[i])

            while deferred:
                deferred.pop(0)[1]()
            e1()
            for dist, fn in zip((1, 7, 11, 14), [e2, e3, e4, e5]):
                deferred.append((n_now + dist, fn))

        n = 0
        while n < len(iters) + LA or deferred:
            if n < len(iters):
                front(n, iters[n])
            if 0 <= n - LA < len(iters):
                back(n - LA, iters[n - LA])
            due = [d for d in deferred if d[0] <= n]
            for d in due:
                deferred.remove(d)
                d[1]()
            n += 1

        P.barrier()
        if stop == "1b":
            break
        def load_w(fb):
            s = fb % 2
            extra = list(bWO) if s == 1 else []
            P.dma("pool", dict(out=WUP[s], in_=w_up[:, fb * 512:(fb + 1) * 512].rearrange("(k p) f -> p k f", p=128)),
                  bWUP[s], writes=[bWUP[s]] + extra)
            for hh in range(2):
                P.dma("pool", dict(out=WDN[s][:, :, hh * 512:(hh + 1) * 512],
                                   in_=w_dn[fb * 512:(fb + 1) * 512, hh * 512:(hh + 1) * 512].rearrange("(c p) n -> p c n", p=128)),
                      bWDN[s][hh], writes=[bWDN[s][hh]] + extra)

        load_w(0)
        slots2a = {}

        def st2a_mm(t):
            hb = (t % 3) * 2
            fns = []
            for k in range(8):
                for half in range(2):
                    fns.append(("matmul", dict(out=PB[hb + half][:, :], lhsT=MT[:, k, t * 128:(t + 1) * 128],
                                               rhs=WO[:, k, half * 512:(half + 1) * 512], start=(k == 0), stop=(k == 7))))
            P.group("pe", fns, reads=[bM[t]] + bWO, writes=[bPB[hb], bPB[hb + 1]])

        XS2 = [XS, JUNK]
        bXS2 = [bXS, bJUNK]
        junk2 = XNT.rearrange("p k t -> p (k t)")

        def st2a_adds(t):
            hb = (t % 3) * 2
            slot = slots2a[t]
            for half in range(2):
                P.op("dve", "tensor_tensor", dict(out=H[:, t, half * 512:(half + 1) * 512], in0=PB[hb + half][:, :],
                                                  in1=XT[slot][:, half * 512:(half + 1) * 512], op=ALU.add),
                     reads=[bPB[hb + half], bXT[slot]], writes=[bH[t]])

        def st2a_act(t):
            xs_, bxs_ = XS2[t % 2], bXS2[t % 2]
            P.op("act", "activation", dict(out=junk2, in_=H[:, t, :], func=AF.Square, accum_out=SS[:, 0:1]),
                 reads=[bH[t]], writes=[bXNT, bSS])
            rstd_chain(SS[:, 0:1], LNV[:, 0:1], RSTD[:, 0:1], D, bSS, bLNV, bRSTD)
            P.op("act", "activation", dict(out=xs_, in_=H[:, t, :], func=AF.Copy, scale=RSTD[:, 0:1]),
                 reads=[bH[t], bRSTD], writes=[bxs_])

        def st2a_T(t):
            xs_, bxs_ = XS2[t % 2], bXS2[t % 2]
            fns = [("transpose", dict(out=PBh[6][:, k * 128:(k + 1) * 128], in_=xs_[:, k * 128:(k + 1) * 128],
                                      identity=IDENT)) for k in range(8)]
            P.group("pe", fns, reads=[bxs_, bIDENT], writes=[bPB[6]])
            evac_T(6, MT[:, :, t * 128:(t + 1) * 128], GMLP, bGMLP, bM[t])

        def st2a_ldx(t):
            tq = q0 + t
            slots2a[t] = load_x(xseq[tq * 128:(tq + 1) * 128, :])

        st2a_ldx(0)
        st2a_ldx(1)
        st2a_mm(0)
        st2a_mm(1)
        st2a_adds(0)
        st2a_act(0)
        for t in range(NQ):
            if t + 2 < NQ:
                st2a_mm(t + 2)
            if t + 1 < NQ:
                st2a_adds(t + 1)
                st2a_act(t + 1)
            if t + 2 < NQ:
                st2a_ldx(t + 2)
            st2a_T(t)

        if stop == "2a":
            P.barrier()
        else:
            for b_ in bUT + bR32:
                for hh in range(2):
                    for sem_, v_ in bWO[hh].r.items():
                        b_.r[sem_] = max(b_.r.get(sem_, 0), v_)
        if stop == "2a":
            for t in range(NQ):
                tq = q0 + t
                P.dma("sp", dict(out=yseq[tq * 128:(tq + 1) * 128, :], in_=H[:, t, :]), bH[t], reads=[bH[t]])
            break
        NFB = DFF // 512
        pairs = [(fb, sg) for fb in range(NFB) for sg in range(4)]
        utb_of = {}

        def up_proj(fb, sg):
            s = fb % 2
            utb = state["u"] % 2
            state["u"] += 1
            utb_of[(fb, sg)] = utb
            for c in range(4):
                ub = state["y"] % 2
                state["y"] += 1
                fns = [("matmul", dict(out=PB[ub][:, :], lhsT=WUP[s][:, k, c * 128:(c + 1) * 128],
                                       rhs=MT[:, k, sg * 512:(sg + 1) * 512], start=(k == 0), stop=(k == 7)))
                       for k in range(8)]
                P.group("pe", fns, reads=[bWUP[s]] + bM[sg * 4:sg * 4 + 4], writes=[bPB[ub]])
                P.op("act", "activation", dict(out=R32[ub], in_=PB[ub][:, :], func=AF.Relu),
                     reads=[bPB[ub]], writes=[bR32[ub]])
                P.op("act", "activation", dict(out=UT[utb][:, c, :], in_=R32[ub], func=AF.Square),
                     reads=[bR32[ub]], writes=[bUT[utb]])

        def down_proj(fb, sg):
            s = fb % 2
            utb = utb_of[(fb, sg)]
            for tl in range(4):
                t = sg * 4 + tl
                for half in range(2):
                    yb = 2 + ((tl * 2 + half) % 4)
                    fns = [("matmul", dict(out=PB[yb][:, :], lhsT=UT[utb][:, c, tl * 128:(tl + 1) * 128],
                                           rhs=WDN[s][:, c, half * 512:(half + 1) * 512], start=(c == 0), stop=(c == 3)))
                           for c in range(4)]
                    P.group("pe", fns, reads=[bUT[utb], bWDN[s][half]], writes=[bPB[yb]])
                    P.op("dve", "tensor_tensor", dict(out=H[:, t, half * 512:(half + 1) * 512], in0=PB[yb][:, :],
                                                      in1=H[:, t, half * 512:(half + 1) * 512], op=ALU.add),
                         reads=[bPB[yb], bH[t]], writes=[bH[t]])
                if fb == NFB - 1:
                    tq = q0 + t
                    P.dma("sp", dict(out=yseq[tq * 128:(tq + 1) * 128, :], in_=H[:, t, :]), bYST, reads=[bH[t]])

        load_w(1)
        up_proj(*pairs[0])
        for j, (fb, sg) in enumerate(pairs):
            if j + 1 < len(pairs):
                up_proj(*pairs[j + 1])
            down_proj(fb, sg)
            if sg == 3 and fb + 2 < NFB:
                load_w(fb + 2)
        P.barrier()

    npad = int(os.environ.get("PADPE", "0"))
    if npad:
        fns = [("matmul", dict(out=PB[7][:, 0:128], lhsT=IDENT, rhs=IDENT, start=True, stop=True)) for _ in range(npad)]
        P.group("pe", fns, reads=[bIDENT], writes=[bPB[7]])
    npad = int(os.environ.get("PADPOOL", "0"))
    for _ in range(npad):
        P.op("pool", "memset", dict(ap=EPST, constant=EPS), writes=[bEPS])
    P.finish()
    P.emit()
    return nc


def prep_weights(norm_attn, w_in, q_norm_a, k_norm_a, q_norm_b, k_norm_b, sink_b,
                 out_norm_a, out_norm_b, w_o, norm_mlp, w_up, w_down):
    f = np.float32
    w_in = np.asarray(w_in, f)[0]
    w_o = np.asarray(w_o, f)[0]
    ar = np.arange
    qb_cols = np.concatenate([1536 + h * 64 + ar(64) for h in PERM_HEADS])
    cols = np.concatenate([ar(0, 512), qb_cols, 512 + ar(512), 1024 + ar(512), 2048 + ar(128), 2176 + ar(128)])
    w_in_p = np.ascontiguousarray(w_in[:, cols])
    rows_b = np.concatenate([512 + h * 64 + ar(64) for h in PERM_HEADS])
    w_o_p = np.ascontiguousarray(w_o[np.concatenate([ar(512), rows_b])])
    gout = np.concatenate([np.asarray(out_norm_a, f)[0], np.asarray(out_norm_b, f)[0][rows_b - 512]])

    def chunked(g):
        return np.ascontiguousarray(np.asarray(g, f).reshape(8, 128).T)

    cos, sin = _rope_tables()
    masks = np.ascontiguousarray(np.concatenate(_MASKS, axis=1)).astype(ml_dtypes.bfloat16)
    return {
        "w_in": w_in_p, "w_o": w_o_p,
        "w_up": np.ascontiguousarray(np.asarray(w_up, f)[0]),
        "w_down": np.ascontiguousarray(np.asarray(w_down, f)[0]),
        "gin": chunked(np.asarray(norm_attn, f)[0]), "gmlp": chunked(np.asarray(norm_mlp, f)[0]),
        "gout": chunked(gout),
        "qkg": np.ascontiguousarray(np.concatenate([np.asarray(q_norm_a, f)[0], np.asarray(q_norm_b, f)[0],
                                                    np.asarray(k_norm_a, f)[0], np.asarray(k_norm_b, f)[0]])[None, :]),
        "sinkp": np.ascontiguousarray(np.asarray(sink_b, f)[0][PERM_HEADS][None, :]),
        "cost": cos, "sint": sin, "masks": masks,
        "ident": np.eye(128, dtype=np.float32).astype(ml_dtypes.bfloat16),
    }


FULL_GROUPS = [("p", 0, 0, 0, 16), ("p", 1, 0, 0, 16), ("s", 0, 0, 0, 24, "save"), ("s", 0, 16, 8, 24, "load")]

_NC_CACHE = {}


def kernel(x_prompt, x_sample, norm_attn, w_in, q_norm_a, k_norm_a, q_norm_b, k_norm_b,
           sink_b, out_norm_a, out_norm_b, w_o, norm_mlp, w_up, w_down):
    n = 8
    x_prompt = np.asarray(x_prompt, np.float32)
    x_sample = np.asarray(x_sample, np.float32)
    shared = prep_weights(norm_attn, w_in, q_norm_a, k_norm_a, q_norm_b, k_norm_b, sink_b,
                          out_norm_a, out_norm_b, w_o, norm_mlp, w_up, w_down)
    if "full" not in _NC_CACHE:
        _NC_CACHE["full"] = build_program(FULL_GROUPS, 2, 2048, 1, 4096)
    nc = _NC_CACHE["full"]
    in_maps = []
    for c in range(n):
        m = dict(shared)
        m["xp"] = np.ascontiguousarray(x_prompt[2 * c:2 * c + 2])
        m["xs"] = np.ascontiguousarray(x_sample[c:c + 1])
        in_maps.append(m)
    res = run_bass_kernel_spmd(nc, in_maps, core_ids=list(range(n)))
    yp = np.concatenate([r["yp"] for r in res.results], axis=0).astype(np.float32)
    ys = np.concatenate([r["ys"] for r in res.results], axis=0).astype(np.float32)
    return (yp, ys)
```

```python
import numpy as np
import ml_dtypes
import concourse.bass as bass
import concourse.mybir as mybir
from concourse.bass_utils import run_bass_kernel_spmd

F32 = mybir.dt.float32
BF16 = mybir.dt.bfloat16
AF = mybir.ActivationFunctionType
ALU = mybir.AluOpType
AX = mybir.AxisListType

D = 1024
HD = 64
IN_COLS = 2304
DFF = 4096
EPS = 1e-6
NQ = 16
PERM_HEADS = [0, 4, 1, 5, 2, 6, 3, 7]


class Buf:
    __slots__ = ("name", "w", "r", "dsem", "dcount", "excl")

    def __init__(self, name, excl=False):
        self.name = name
        self.excl = excl
        self.w = None
        self.r = {}
        self.dsem = None
        self.dcount = 0


class Eng:
    def __init__(self, key, sem):
        self.key = key
        self.sem = sem
        self.count = 0
        self.prog = []
        self.waited = {}


class Prog:
    def __init__(self, nc):
        self.nc = nc
        self.eng = {}
        for key in ("pe", "act", "dve", "pool", "sp"):
            self.eng[key] = Eng(key, nc.alloc_semaphore("s_" + key))
        self.bsem = nc.alloc_semaphore("s_bar")
        self.bcount = 0
        self.bufs = []
        self.dma_bufs = []
        self.fresh_dma_sems = False
        import os
        for j in range(int(os.environ.get("DUMMYSEM", "0"))):
            nc.alloc_semaphore("dummy%d" % j)

    def buf(self, name, excl=False):
        b = Buf(name, excl)
        self.bufs.append(b)
        return b

    def _wait(self, E, tick):
        sem, val = tick
        if E.key == "pe" and sem is E.sem:
            return
        if E.waited.get(sem, 0) >= val:
            return
        E.waited[sem] = val
        E.prog.append(("wait", sem, val))

    @staticmethod
    def _split(reads, writes):
        writes = list(writes)
        r2 = []
        for b in reads:
            if b.excl:
                if b not in writes:
                    writes.append(b)
            else:
                r2.append(b)
        return r2, writes

    def _deps(self, E, reads, writes):
        for b in reads:
            if b.w is not None:
                self._wait(E, b.w)
        for b in writes:
            if b.w is not None:
                self._wait(E, b.w)
            for s, v in b.r.items():
                self._wait(E, (s, v))

    def _commit(self, tick, reads, writes):
        for b in reads:
            b.r[tick[0]] = tick[1]
        for b in writes:
            b.w = tick
            b.r = {}

    def op(self, ek, name, kw, reads=(), writes=()):
        reads, writes = self._split(reads, writes)
        E = self.eng[ek]
        self._deps(E, reads, writes)
        E.count += 1
        tick = (E.sem, E.count)
        E.prog.append(("op", (name, kw), E.sem, 1))
        self._commit(tick, reads, writes)

    def group(self, ek, fns, reads=(), writes=()):
        reads, writes = self._split(reads, writes)
        E = self.eng[ek]
        self._deps(E, reads, writes)
        for fn in fns[:-1]:
            E.prog.append(("op", fn, None, 0))
        E.count += 1
        tick = (E.sem, E.count)
        E.prog.append(("op", fns[-1], E.sem, 1))
        self._commit(tick, reads, writes)

    def dma(self, ek, kw, primary, reads=(), writes=()):
        fn = ("dma_start", kw)
        E = self.eng[ek]
        self._deps(E, reads, writes)
        if primary.dsem is None:
            self.nsem = getattr(self, "nsem", 0) + 1
            primary.dsem = self.nc.alloc_semaphore("d%d_%s" % (self.nsem, primary.name))
            self.dma_bufs.append(primary)
        primary.dcount += 1
        tick = (primary.dsem, 16 * primary.dcount)
        E.prog.append(("op", fn, primary.dsem, 16))
        self._commit(tick, reads, writes)

    def barrier(self):
        sp = self.eng["sp"]
        for k in ("pe", "act", "dve", "pool"):
            E = self.eng[k]
            if E.count:
                self._wait(sp, (E.sem, E.count))
        for b in self.dma_bufs:
            self._wait(sp, (b.dsem, 16 * b.dcount))
        if self.fresh_dma_sems:
            for b in self.dma_bufs:
                b.dsem = None
                b.dcount = 0
            self.dma_bufs = []
        self.bcount += 1
        sp.prog.append(("inc", self.bsem, 1))
        for k in ("pe", "act", "dve", "pool"):
            self.eng[k].prog.append(("wait", self.bsem, self.bcount))
        for b in self.bufs:
            b.w = None
            b.r = {}

    def finish(self):
        sp = self.eng["sp"]
        for k in ("pe", "act", "dve", "pool"):
            E = self.eng[k]
            if E.count:
                self._wait(sp, (E.sem, E.count))
        for b in self.dma_bufs:
            self._wait(sp, (b.dsem, 16 * b.dcount))

    def emit(self):
        nc = self.nc

        def replay(E):
            def f(eng):
                for item in E.prog:
                    if item[0] == "wait":
                        eng.wait_ge(item[1], item[2])
                    elif item[0] == "inc":
                        eng.sem_inc(item[1], item[2])
                    elif item[0] == "clear":
                        eng.sem_clear(item[1])
                    else:
                        ins = getattr(eng, item[1][0])(**item[1][1])
                        if item[2] is not None:
                            ins.then_inc(item[2], item[3])
            return f

        with nc.Block() as block:
            block.sync(replay(self.eng["sp"]))
            block.gpsimd(replay(self.eng["pool"]))
            block.scalar(replay(self.eng["act"]))
            block.vector(replay(self.eng["dve"]))
            block.tensor(replay(self.eng["pe"]))


def _mask_tables():
    a = np.arange(128)[:, None]
    b = np.arange(128)[None, :]
    tabs = []
    idxA = {}
    for dl in range(-2, 3):
        diff = 128 * dl + a - b
        m = (np.abs(diff) <= 64).astype(np.float32)
        m += ((diff % 4 == 0) & (np.abs(diff) <= 256)).astype(np.float32)
        idxA[dl] = len(tabs)
        tabs.append(m)
    idxB = {}
    for dl in (-1, 1):
        diff = 128 * dl + a - b
        m = (np.abs(diff) <= 128).astype(np.float32)
        idxB[dl] = len(tabs)
        tabs.append(m)
    idxD = {}
    for off in (0, 128, -64, 64):
        m = (np.abs(off + a - b) <= 64).astype(np.float32)
        idxD[off] = len(tabs)
        tabs.append(m)
    return tabs, idxA, idxB, idxD


_MASKS, _IDXA, _IDXB, _IDXD = _mask_tables()
NM = len(_MASKS)


def _rope_tables():
    half = 8
    inv = 500000.0 ** (-(np.arange(half, dtype=np.float64) * 2.0 / 16.0))
    pos = np.arange(4096, dtype=np.float64)
    ang = pos[:, None] * inv[None, :]
    cos = np.cos(ang).astype(np.float32).reshape(32, 128, half).transpose(1, 0, 2).reshape(128, 32 * half)
    sin = np.sin(ang).astype(np.float32).reshape(32, 128, half).transpose(1, 0, 2).reshape(128, 32 * half)
    return np.ascontiguousarray(cos), np.ascontiguousarray(sin)


def build_program(groups, n_p, len_p, n_s, len_s, stop=None, n1a=None, n1b=None):
    nc = bass.Bass("TRN2", target_bir_lowering=False)
    P = Prog(nc)

    def din(name, shape, dt=F32):
        return nc.dram_tensor(name, list(shape), dt, kind="ExternalInput").ap()

    xsrc = {}
    ydst = {}
    if n_p:
        xsrc["p"] = din("xp", [n_p, len_p, D])
        ydst["p"] = nc.dram_tensor("yp", [n_p, len_p, D], F32, kind="ExternalOutput").ap()
    if n_s:
        xsrc["s"] = din("xs", [n_s, len_s, D])
        ydst["s"] = nc.dram_tensor("ys", [n_s, len_s, D], F32, kind="ExternalOutput").ap()
    w_in = din("w_in", [D, IN_COLS])
    w_o = din("w_o", [D, D])
    w_up = din("w_up", [D, DFF])
    w_dn = din("w_down", [DFF, D])
    d_gin = din("gin", [128, 8])
    d_gmlp = din("gmlp", [128, 8])
    d_gout = din("gout", [128, 8])
    d_qkg = din("qkg", [1, 256])
    d_sink = din("sinkp", [1, 8])
    d_cos = din("cost", [128, 256])
    d_sin = din("sint", [128, 256])
    d_masks = din("masks", [128, NM * 128], BF16)
    d_ident = din("ident", [128, 128], BF16)

    def sb(name, shape, dt):
        return nc.alloc_sbuf_tensor("sb_" + name, shape, dt)
    WIN_t = sb("WIN", [128, 8 * IN_COLS], BF16)
    WIN = WIN_t.ap().rearrange("p (k c) -> p k c", k=8)
    R1 = sb("R1", [128, 16384], BF16)
    QT = R1.ap().rearrange("p (m t) -> p m t", m=8)
    WUP = [R1.ap()[:, s * 8192:s * 8192 + 4096].rearrange("p (k f) -> p k f", k=8) for s in range(2)]
    WDN = [R1.ap()[:, s * 8192 + 4096:(s + 1) * 8192].rearrange("p (c n) -> p c n", c=4) for s in range(2)]
    R2 = sb("R2", [128, 32768], BF16)
    KT = R2.ap()[:, 0:15360].rearrange("p (m t) -> p m t", m=5)
    V = R2.ap()[:, 15360:15360 + 15600].rearrange("p (c h d) -> p c h d", c=24, h=10)
    H = R2.ap().bitcast(F32).rearrange("p (t f) -> p t f", t=16)
    R3 = sb("R3", [128, 16384], BF16)
    MT = R3.ap().rearrange("p (m t) -> p m t", m=8)
    r3f = R3.ap().bitcast(F32)
    STG = r3f[:, 0:1664]
    SQ = r3f[:, 1664:3328]
    ROPEIN = r3f[:, 3328:3328 + 416]
    R4 = sb("R4", [128, 8192], BF16)
    r4f = R4.ap().bitcast(F32)
    QN32 = r4f[:, 0:1664]
    QKB = R4.ap()[:, 3328:3328 + 1664]
    PTB = [R4.ap()[:, 4992 + s * 512:4992 + (s + 1) * 512] for s in range(2)]
    O32 = r4f[:, 3008:3008 + 1024]
    PT4 = [PTB[0], PTB[1], R4.ap()[:, 0:512], R4.ap()[:, 512:1024], R4.ap()[:, 4112:4624]]
    QZ2 = [None, R4.ap()[:, 1024:3072].rearrange("p (m v t) -> p m v t", m=8, v=2)]
    WO = R1.ap()[:, 8192:16384].rearrange("p (k n) -> p k n", k=8)
    UT = [R4.ap()[:, s * 2048:(s + 1) * 2048].rearrange("p (c t) -> p c t", c=4) for s in range(2)]
    R32 = [r4f[:, 2048 + s * 512:2048 + (s + 1) * 512] for s in range(2)]

    QZ = sb("qz", [128, 8, 2, 128], BF16).ap()
    QZ2[0] = QZ
    XT = [sb("xt%d" % s, [128, D], F32).ap() for s in range(2)]
    XS = sb("xsb", [128, D], BF16).ap()
    JUNK = sb("junk", [128, D], BF16).ap()
    XNT = sb("xnT", [128, 8, 128], BF16).ap()
    xnt_flat = XNT.rearrange("p k t -> p (k t)")
    PT4 += [xnt_flat[:, 0:512], xnt_flat[:, 512:1024]]
    MASKS = sb("masks", [128, NM, 128], BF16).ap()
    COS = sb("cos", [128, 32, 8], F32).ap()
    SIN = sb("sin", [128, 32, 8], F32).ap()
    IDENT = sb("ident", [128, 128], BF16).ap()
    GIN = sb("gin", [128, 8], F32).ap()
    GMLP = sb("gmlp", [128, 8], F32).ap()
    GOUT = sb("gout", [128, 8], F32).ap()
    QKG = sb("qkg", [128, 4, 64], F32).ap()
    SINK = sb("sink", [128, 8], F32).ap()
    ESINK = sb("esink", [128, 8], F32).ap()
    EPST = sb("epst", [128, 1], F32).ap()
    SS = sb("ss", [128, 2], F32).ap()
    LNV = sb("lnv", [128, 2], F32).ap()
    RSTD = sb("rstd", [128, 2], F32).ap()
    SSQ = sb("ssq", [128, 26], F32).ap()
    LNQ = sb("lnq", [128, 26], F32).ap()
    RQ = sb("rq", [128, 26], F32).ap()
    RT = [r3f[:, 3744 + j * 208:3744 + (j + 1) * 208].rearrange("p (h d) -> p h d", d=8) for j in range(4)]
    ODS = r4f[:, 1536:1536 + 520]
    VDS = [XT[s_].bitcast(BF16)[:, 0:1040].rearrange("p (c f) -> p c f", c=2) for s_ in range(2)]
    ODT = [XT[s_][:, 0:520] for s_ in range(2)]
    vd_dram = nc.dram_tensor("vd_scratch", [24 * 128, 520], BF16, kind="Internal").ap()
    od_dram = nc.dram_tensor("od_scratch", [NQ * 128, 520], F32, kind="Internal").ap()
    kt_scr = nc.dram_tensor("kt_scratch", [128, 5 * 1024], BF16, kind="Internal").ap()
    v_scr = nc.dram_tensor("v_scratch", [128, 8 * 650], BF16, kind="Internal").ap()
    DEN = sb("den", [128, 16], F32).ap()
    RDEN = sb("rden", [128, 16], F32).ap()

    PB = [nc.alloc_psum_tensor("pb%d" % j, [128, 512], F32).ap() for j in range(8)]
    PBh = [p.bitcast(BF16) for p in PB]

    WIN_COLS = [(0, 512), (512, 1024), (1024, 1536), (1536, 2048), (2048, 2304)]
    bWIN = [P.buf("win%d" % k) for k in range(5)]
    bXT = [P.buf("xt%d" % s) for s in range(2)]
    bXS, bJUNK, bXNT = P.buf("xs"), P.buf("junk"), P.buf("xnt")
    bMASKS, bCOS, bSIN, bIDENT = P.buf("masks"), P.buf("cos"), P.buf("sin"), P.buf("ident")
    bGIN, bGMLP, bGOUT, bQKG = P.buf("gin"), P.buf("gmlp"), P.buf("gout"), P.buf("qkg")
    bSINK, bESINK, bEPS = P.buf("sink"), P.buf("esink"), P.buf("eps")
    bSS, bLNV, bRSTD = P.buf("ss"), P.buf("lnv"), P.buf("rstd")
    bSSQ, bLNQ, bRQ = P.buf("ssq"), P.buf("lnq"), P.buf("rq")
    bRT = [P.buf("rt%d" % j) for j in range(4)]
    bDEN, bRDEN = P.buf("den"), P.buf("rden")
    bPB = [P.buf("pb%d" % j, excl=True) for j in range(8)]
    bQN32, bQKB, bO32 = P.buf("qn32"), P.buf("qkb"), P.buf("o32")
    bSTG, bSQ, bRIN = P.buf("stg"), P.buf("sq"), P.buf("rin")
    bXNTb = P.buf("xntb")
    bPT = [P.buf("pt%d" % s) for s in range(2)]
    bKT = [P.buf("kt%d" % c) for c in range(24)]
    bV = [P.buf("v%d" % c) for c in range(24)]
    bQT = [P.buf("qt%d" % i) for i in range(NQ)]
    bM = [P.buf("m%d" % i) for i in range(NQ)]
    bH = [P.buf("h%d" % i) for i in range(NQ)]
    bWO = [P.buf("wo%d" % hh) for hh in range(2)]
    bQZ = P.buf("qz")
    bQZ2 = [bQZ, P.buf("qz1")]
    bQZlo = [P.buf("qzlo0"), P.buf("qzlo1")]
    bQZhi = [P.buf("qzhi0"), P.buf("qzhi1")]
    bPT4 = [bPT[0], bPT[1], P.buf("pt2"), P.buf("pt3"), P.buf("pt4"), P.buf("pt5"), P.buf("pt6")]
    NPT = len(bPT4)
    bYST = P.buf("yst")
    bVDd, bVDW, bODd, bODS = P.buf("vdd"), P.buf("vdw"), P.buf("odd"), P.buf("ods")
    bVDS = [P.buf("vds%d" % b) for b in range(4)]
    bKTS, bVSS, bKTL = P.buf("kts"), P.buf("vss"), P.buf("ktl")
    bVL = [P.buf("vl0"), P.buf("vl1")]
    bWUP = [P.buf("wup%d" % s) for s in range(2)]
    bWDN = [[P.buf("wdn%d_%d" % (s, hh)) for hh in range(2)] for s in range(2)]
    bUT = [P.buf("ut%d" % s) for s in range(2)]
    bR32 = [P.buf("r32%d" % s) for s in range(2)]

    P.dma("sp", dict(out=IDENT, in_=d_ident), bIDENT, writes=[bIDENT])
    P.dma("sp", dict(out=MASKS, in_=d_masks.rearrange("p (m t) -> p m t", m=NM)), bMASKS, writes=[bMASKS])
    P.dma("sp", dict(out=COS, in_=d_cos.rearrange("p (t f) -> p t f", t=32)), bCOS, writes=[bCOS])
    P.dma("sp", dict(out=SIN, in_=d_sin.rearrange("p (t f) -> p t f", t=32)), bSIN, writes=[bSIN])
    P.dma("sp", dict(out=GIN, in_=d_gin), bGIN, writes=[bGIN])
    P.dma("sp", dict(out=GMLP, in_=d_gmlp), bGMLP, writes=[bGMLP])
    P.dma("sp", dict(out=GOUT, in_=d_gout), bGOUT, writes=[bGOUT])
    P.dma("sp", dict(out=QKG.rearrange("p a b -> p (a b)"), in_=d_qkg.partition_broadcast(128)), bQKG, writes=[bQKG])
    P.dma("sp", dict(out=SINK, in_=d_sink.partition_broadcast(128)), bSINK, writes=[bSINK])
    P.op("pool", "memset", dict(ap=EPST, constant=EPS), writes=[bEPS])
    P.op("pool", "memset", dict(ap=QZ, constant=0.0), writes=[bQZlo[0], bQZhi[0]])
    for j, (a0, a1) in enumerate(WIN_COLS):
        P.dma("pool", dict(out=WIN[:, :, a0:a1], in_=w_in[:, a0:a1].rearrange("(k p) f -> p k f", p=128)),
              bWIN[j], writes=[bWIN[j]])
    P.op("act", "activation", dict(out=ESINK, in_=SINK, func=AF.Exp), reads=[bSINK], writes=[bESINK])

    def rstd_chain(ss_ap, ln_ap, r_ap, n, bss, bln, br):
        P.op("act", "activation", dict(out=ln_ap, in_=ss_ap, func=AF.Ln, scale=1.0 / n, bias=EPST),
             reads=[bss, bEPS], writes=[bln])
        P.op("act", "activation", dict(out=r_ap, in_=ln_ap, func=AF.Exp, scale=-0.5), reads=[bln], writes=[br])

    import os
    CUT = int(os.environ.get("CUT", "99"))
    state = {"x": 0, "s": 0, "u": 0, "y": 0, "mm": 0}

    def load_x(xrows):
        slot = state["x"] % 2
        state["x"] += 1
        P.dma("sp", dict(out=XT[slot], in_=xrows), bXT[slot], writes=[bXT[slot]])
        return slot

    def transposes_xs(ps_idx):
        fns = [("transpose", dict(out=PBh[ps_idx][:, k * 128:(k + 1) * 128], in_=XS[:, k * 128:(k + 1) * 128],
                                  identity=IDENT)) for k in range(8)]
        P.group("pe", fns, reads=[bXS, bIDENT], writes=[bPB[ps_idx]])

    def norm_transpose(src_ap, bsrc, ps_idx):
        P.op("act", "activation", dict(out=JUNK, in_=src_ap, func=AF.Square, accum_out=SS[:, 0:1]),
             reads=[bsrc], writes=[bJUNK, bSS])
        rstd_chain(SS[:, 0:1], LNV[:, 0:1], RSTD[:, 0:1], D, bSS, bLNV, bRSTD)
        P.op("act", "activation", dict(out=XS, in_=src_ap, func=AF.Copy, scale=RSTD[:, 0:1]),
             reads=[bsrc, bRSTD], writes=[bXS])
        transposes_xs(ps_idx)

    def evac_T(ps_idx, out_ap, gain_ap, bgain, bout):
        P.op("dve", "tensor_tensor", dict(out=out_ap, in0=PBh[ps_idx].rearrange("p (k t) -> p k t", k=8),
                                          in1=gain_ap.unsqueeze(2).to_broadcast([128, 8, 128]), op=ALU.mult),
             reads=[bPB[ps_idx], bgain], writes=[bout])

    def hv(ap):
        return ap.rearrange("p (h d) -> p h d", d=64)

    preload = {}
    for gidx_, gspec in enumerate(groups):
        (src, seq, q0, c0, nctx) = gspec[:5]
        halo_mode = gspec[5] if len(gspec) > 5 else None
        if stop == "setup":
            break
        xseq = xsrc[src][seq]
        yseq = ydst[src][seq]
        L = xseq.shape[0]
        nts = L // 128

        P.op("pool", "memset", dict(ap=V[:, :, :, 64:65], constant=1.0), writes=bV[:nctx])

        n_c = nctx if n1a is None else n1a
        c_start = 0
        if halo_mode == "load":
            c_start = 8
            P.dma("sp", dict(out=KT[:, :, 0:1024], in_=kt_scr.rearrange("p (m t) -> p m t", m=5)), bKTL,
                  reads=[bKTS], writes=bKT[0:8] + [bKTL])
            for half_ in range(2):
                P.dma("sp", dict(out=V[:, half_ * 4:(half_ + 1) * 4, :, :].rearrange("p c h d -> p (c h d)"),
                                 in_=v_scr[:, half_ * 2600:(half_ + 1) * 2600]), bVL[half_],
                      reads=[bVSS], writes=bV[half_ * 4:(half_ + 1) * 4] + [bVL[half_]])
            for c_ in range(8):
                P.dma("pool", dict(out=vd_dram[c_ * 128:(c_ + 1) * 128, :],
                                   in_=V[:, c_, 0:8, :].rearrange("p h d -> p (h d)")),
                      bVDW, reads=[bV[c_]], writes=[bVDd])
        BANKS = [(0, 0, 512), (1, 512, 1024), (2, 1024, 1536), (3, 1536, 2048), (4, 2048, 2304)]
        SEGS = [(0, 0, 8, 0), (1, 8, 8, 1), (2, 16, 8, 2), (4, 24, 2, 3)]
        stv = hv(STG)
        kb = hv(QKB)
        rin = ROPEIN.rearrange("p (h d) -> p h d", d=16)

        def info(c):
            ts = c0 + c
            own = q0 <= ts < q0 + NQ
            return dict(ts=ts, own=own, qi=ts - q0, banks=BANKS if own else BANKS[2:],
                        segs=SEGS if own else SEGS[2:], h_lo=0 if own else 16)

        slots1a = {}

        def st_ldx(c):
            nf = info(c)
            if c in preload.get(gidx_, {}):
                slots1a[c] = preload[gidx_][c]
                return
            slots1a[c] = load_x(xseq[nf["ts"] * 128:(nf["ts"] + 1) * 128, :])

        def st_A(c):
            nf = info(c)
            slot = slots1a[c]
            P.op("act", "activation", dict(out=JUNK, in_=XT[slot], func=AF.Square, accum_out=SS[:, 0:1]),
                 reads=[bXT[slot]], writes=[bJUNK, bSS])
            rstd_chain(SS[:, 0:1], LNV[:, 0:1], RSTD[:, 0:1], D, bSS, bLNV, bRSTD)
            P.op("act", "activation", dict(out=XS, in_=XT[slot], func=AF.Copy, scale=RSTD[:, 0:1]),
                 reads=[bXT[slot], bRSTD], writes=[bXS])

        XNT2 = [XNT, R3.ap()[:, 12288:13312].rearrange("p (k t) -> p k t", k=8)]
        bXNT2 = [bXNT, bXNTb]

        def st_MM(c, which):
            nf = info(c)
            xnt, bxnt = XNT2[c % 2], bXNT2[c % 2]
            part = [b for b in nf["banks"] if (b[0] < 2) == (which == 0)]
            if not part:
                return
            fns = []
            for k in range(8):
                for (bk, a0, a1) in part:
                    fns.append(("matmul", dict(out=PB[bk][:, 0:a1 - a0], lhsT=xnt[:, k, :], rhs=WIN[:, k, a0:a1],
                                               start=(k == 0), stop=(k == 7))))
            P.group("pe", fns, reads=[bxnt] + [bWIN[bk] for (bk, _, _) in part],
                    writes=[bPB[bk] for (bk, _, _) in part])

        def st_C(c, part):
            nf = info(c)
            for (bk, h0, nh, gt) in [sg_ for sg_ in nf["segs"] if (sg_[0] < 2) == (part == 0)]:
                P.op("act", "activation", dict(out=SQ[:, h0 * 64:(h0 + nh) * 64], in_=PB[bk][:, 0:nh * 64], func=AF.Square),
                     reads=[bPB[bk]], writes=[bSQ])
                P.op("dve", "tensor_tensor", dict(
                    out=hv(STG[:, h0 * 64:(h0 + nh) * 64]), in0=hv(PB[bk][:, 0:nh * 64]),
                    in1=QKG[:, gt, :].unsqueeze(1).to_broadcast([128, nh, 64]), op=ALU.mult),
                    reads=[bPB[bk], bQKG], writes=[bSTG])
            if part == 0:
                return
            P.op("act", "activation", dict(out=V[:, c, 0:8, 0:64], in_=hv(PB[3][:, :]), func=AF.Copy),
                 reads=[bPB[3]], writes=[bV[c]])
            P.op("act", "activation", dict(out=V[:, c, 8:10, 0:64], in_=hv(PB[4][:, 128:256]), func=AF.Copy),
                 reads=[bPB[4]], writes=[bV[c]])
            P.dma("pool", dict(out=vd_dram[c * 128:(c + 1) * 128, :], in_=V[:, c, 0:8, :].rearrange("p h d -> p (h d)")),
                  bVDW, reads=[bV[c]], writes=[bVDd])

        def st_D(c):
            nf = info(c)
            h_lo, h_hi = nf["h_lo"], 26
            nh_all = h_hi - h_lo
            ts = nf["ts"]
            P.op("dve", "tensor_reduce", dict(out=SSQ[:, h_lo:h_hi], in_=hv(SQ[:, h_lo * 64:h_hi * 64]),
                                              axis=AX.X, op=ALU.add), reads=[bSQ], writes=[bSSQ])
            rstd_chain(SSQ[:, h_lo:h_hi], LNQ[:, h_lo:h_hi], RQ[:, h_lo:h_hi], HD, bSSQ, bLNQ, bRQ)
            P.op("dve", "tensor_tensor", dict(
                out=kb[:, h_lo:h_hi, 16:64], in0=stv[:, h_lo:h_hi, 16:64],
                in1=RQ[:, h_lo:h_hi].unsqueeze(2).to_broadcast([128, nh_all, 48]), op=ALU.mult),
                reads=[bSTG, bRQ], writes=[bQKB])
            P.op("dve", "tensor_tensor", dict(
                out=rin[:, h_lo:h_hi, :], in0=stv[:, h_lo:h_hi, 0:16],
                in1=RQ[:, h_lo:h_hi].unsqueeze(2).to_broadcast([128, nh_all, 16]), op=ALU.mult),
                reads=[bSTG, bRQ], writes=[bRIN])
            x1 = rin[:, h_lo:h_hi, 0:8]
            x2 = rin[:, h_lo:h_hi, 8:16]
            cosb = COS[:, ts, :].unsqueeze(1).to_broadcast([128, nh_all, 8])
            sinb = SIN[:, ts, :].unsqueeze(1).to_broadcast([128, nh_all, 8])
            for j, (xa, tb, bt) in enumerate([(x1, cosb, bCOS), (x2, sinb, bSIN), (x2, cosb, bCOS), (x1, sinb, bSIN)]):
                P.op("pool", "tensor_tensor", dict(out=RT[j][:, h_lo:h_hi, :], in0=xa, in1=tb, op=ALU.mult),
                     reads=[bRIN, bt], writes=[bRT[j]])
            P.op("dve", "tensor_tensor", dict(out=kb[:, h_lo:h_hi, 0:8], in0=RT[0][:, h_lo:h_hi, :],
                                              in1=RT[1][:, h_lo:h_hi, :], op=ALU.subtract),
                 reads=[bRT[0], bRT[1]], writes=[bQKB])
            P.op("dve", "tensor_tensor", dict(out=kb[:, h_lo:h_hi, 8:16], in0=RT[2][:, h_lo:h_hi, :],
                                              in1=RT[3][:, h_lo:h_hi, :], op=ALU.add),
                 reads=[bRT[2], bRT[3]], writes=[bQKB])

        def st_T2(c):
            nf = info(c)
            if nf["own"]:
                fns = [("transpose", dict(out=PBh[6][:, m * 128:(m + 1) * 128], in_=QKB[:, m * 128:(m + 1) * 128],
                                          identity=IDENT)) for m in range(8)]
                P.group("pe", fns, reads=[bQKB, bIDENT], writes=[bPB[6]])
            fns = [("transpose", dict(out=PBh[7][:, m * 128:(m + 1) * 128],
                                      in_=QKB[:, 1024 + m * 128:1024 + (m + 1) * 128], identity=IDENT)) for m in range(5)]
            P.group("pe", fns, reads=[bQKB, bIDENT], writes=[bPB[7]])

        def st_E(c):
            nf = info(c)
            if nf["own"]:
                qi = nf["qi"]
                P.op("dve", "tensor_copy", dict(out=QT[:, :, qi * 128:(qi + 1) * 128],
                                                in_=PBh[6].rearrange("p (m t) -> p m t", m=8)),
                     reads=[bPB[6]], writes=[bQT[qi]])
            P.op("act", "activation", dict(out=KT[:, :, c * 128:(c + 1) * 128],
                                           in_=PBh[7][:, 0:640].rearrange("p (m t) -> p m t", m=5), func=AF.Copy),
                 reads=[bPB[7]], writes=[bKT[c]])

        st_ldx(c_start)
        if n_c > c_start + 1:
            st_ldx(c_start + 1)
        st_A(c_start)
        for c in range(c_start, n_c + 2):
            if c_start <= c - 2 < n_c:
                st_D(c - 2)
            if c_start <= c - 1 < n_c:
                st_MM(c - 1, 0)
            if c < n_c:
                transposes_xs(5)
            if c_start <= c - 1 < n_c:
                st_C(c - 1, 0)
            if c < n_c:
                evac_T(5, XNT2[c % 2], GIN, bGIN, bXNT2[c % 2])
            if c_start <= c - 1 < n_c:
                st_MM(c - 1, 1)
            if c_start <= c - 2 < n_c:
                st_T2(c - 2)
            if c_start <= c - 1 < n_c:
                st_C(c - 1, 1)
            if c_start <= c - 2 < n_c:
                st_E(c - 2)
            if c + 2 < n_c:
                st_ldx(c + 2)
            if c + 1 < n_c:
                st_A(c + 1)

        if halo_mode == "save":
            P.dma("pool", dict(out=kt_scr.rearrange("p (m t) -> p m t", m=5), in_=KT[:, :, 1024:2048]), bKTS,
                  reads=bKT[8:16], writes=[bKTS])
            for half_ in range(2):
                P.dma("pool", dict(out=v_scr[:, half_ * 2600:(half_ + 1) * 2600],
                                   in_=V[:, 8 + half_ * 4:8 + (half_ + 1) * 4, :, :].rearrange("p c h d -> p (c h d)")),
                      bVSS, reads=bV[8 + half_ * 4:8 + (half_ + 1) * 4], writes=[bVSS])

        if stop == "1a":
            break
        def alias_after(dst_bufs, src_bufs):
            for d_ in dst_bufs:
                for s_ in src_bufs:
                    for sem_, v_ in s_.r.items():
                        d_.r[sem_] = max(d_.r.get(sem_, 0), v_)
                    if s_.w is not None:
                        d_.r[s_.w[0]] = max(d_.r.get(s_.w[0], 0), s_.w[1])

        alias_after([bODS, bPT4[4]], [bQKB])
        alias_after([bPT4[5], bPT4[6]], [bXNT])
        alias_after(bM, [bSTG, bSQ, bRIN, bXNTb] + bRT)
        P.op("pool", "memset", dict(ap=QZ2[1], constant=0.0), writes=[bQZlo[1], bQZhi[1]])
        n_q = NQ if n1b is None else n1b
        LA = 5
        SBANK = [0, 1, 7]

        def sl(st_, n_, step):
            return slice(st_, st_ + step * (n_ - 1) + 1, step)

        n_kpos = nctx * 8
        kchunks = [(kc, min(128, n_kpos - 128 * kc)) for kc in range((n_kpos + 127) // 128)]
        vd_cls = vd_dram.rearrange("(a s) f -> s a f", s=16)
        od_cls = od_dram.rearrange("(j s) f -> s j f", s=16)

        NVB = 4 if len(kchunks) == 1 else 2
        nkc = len(kchunks)
        VDSn = [XT[b % 2].bitcast(BF16)[:, (b // 2) * 520 * nkc:(b // 2 + 1) * 520 * nkc].rearrange("p (c f) -> p c f", c=nkc)
                for b in range(NVB)]

        def load_vd(r):
            b = r % NVB
            for kc, na in kchunks:
                P.dma("sp", dict(out=VDSn[b][0:na, kc, :], in_=vd_cls[r][128 * kc:128 * kc + na]),
                      bVDS[b], reads=[bVDd], writes=[bVDS[b], bXT[b % 2]])

        def fill_qz(kind, j):
            b = j % 2
            qz = QZ2[b]
            if kind == "D":
                P.op("act", "activation", dict(out=qz[0:64, 0:4, 0, :], in_=QT[0:64, 0:4, sl(j, 128, 16)], func=AF.Copy),
                     reads=bQT, writes=[bQZlo[b]])
                P.op("dve", "tensor_copy", dict(out=qz[64:128, 0:4, 1, :], in_=QT[64:128, 0:4, sl(j, 128, 16)]),
                     reads=bQT, writes=[bQZhi[b]])
            else:
                P.op("act", "activation", dict(out=qz[0:64, :, 0, :], in_=QT[0:64, :, j * 128:(j + 1) * 128], func=AF.Copy),
                     reads=[bQT[j]], writes=[bQZlo[b]])
                P.op("dve", "tensor_copy", dict(out=qz[64:128, :, 1, :], in_=QT[64:128, :, j * 128:(j + 1) * 128]),
                     reads=[bQT[j]], writes=[bQZhi[b]])

        groups_order = ([("D", r) for r in range(16)] if n_q == NQ else []) + [("N", i) for i in range(n_q)]

        def load_wo():
            for hh in range(2):
                P.dma("pool", dict(out=WO[:, :, hh * 512:(hh + 1) * 512],
                                   in_=w_o[:, hh * 512:(hh + 1) * 512].rearrange("(k p) n -> p k n", p=128)),
                      bWO[hh], writes=[bWO[hh]] + bQT + [bWUP[1]] + bWDN[1])

        def first_any(gidx):
            if gidx + 1 < len(groups_order):
                fill_qz(*groups_order[gidx + 1])
                if gidx + 2 == len(groups_order):
                    load_wo()

        def first_dil(r):
            first_any(r)

        def last_dil(r):
            if r + NVB < 16:
                load_vd(r + NVB)
            b0 = 2 + 2 * (r % 2)
            P.op("act", "activation", dict(out=ODS[:, 0:260], in_=PB[b0][:, 0:260], func=AF.Copy),
                 reads=[bPB[b0]], writes=[bODS])
            P.op("dve", "tensor_copy", dict(out=ODS[:, 260:520], in_=PB[b0 + 1][:, 0:260]),
                 reads=[bPB[b0 + 1]], writes=[bODS])
            P.dma("sp", dict(out=od_cls[r], in_=ODS), bODS, reads=[bODS], writes=[bODd])

        def first_nat(i):
            first_any((16 if n_q == NQ else 0) + i)

        def first_back_nat(i):
            P.dma("sp", dict(out=ODT[i % 2], in_=od_dram[i * 128:(i + 1) * 128, :]), bXT[i % 2],
                  reads=[bODd], writes=[bXT[i % 2]] + [bVDS[b] for b in range(4) if b % 2 == i % 2])

        iters = []
        fill_qz(*groups_order[0])
        if len(groups_order) == 1:
            load_wo()
        if n_q == NQ:
            for r in range(NVB):
                load_vd(r)
            for r in range(16):
                cls = []
                for hg in range(2):
                    for idx, (kc, na) in enumerate(kchunks):
                        off = (c0 * 8 + 128 * kc) - q0 * 8
                        cls.append(dict(
                            kind="A", g=hg, idx=idx, n=len(kchunks), nk=na, mi=_IDXD[off], ob=2 + hg + 2 * (r % 2),
                            qz=(QZ2[r % 2], bQZ2[r % 2]), qb=r % 2,
                            kt=(lambda ch, kc=kc, na=na, r=r: KT[:, ch, sl(2048 * kc + r, na, 16)]),
                            ktb=bKT[:nctx],
                            vfn=(lambda h, kc=kc, na=na, r=r: VDSn[r % NVB][0:na, kc, h * 65:(h + 1) * 65]),
                            vb=[bVDS[r % NVB]]))
                cls[0]["first"] = (lambda r=r: first_dil(r))
                cls[-1]["last"] = (lambda n_now, r=r: last_dil(r))
                iters += cls
        for i in range(n_q):
            tq = q0 + i
            deltas = [dl for dl in range(-2, 3) if 0 <= tq + dl < nts]
            deltas_b = [dl for dl in (-1, 0, 1) if 0 <= tq + dl < nts]
            tl = []
            for hg in range(2):
                for idx, dl in enumerate(deltas):
                    ck = tq + dl - c0
                    assert 0 <= ck < nctx
                    tl.append(dict(kind="A", g=hg, idx=idx, n=len(deltas), nk=128, mi=_IDXA[dl],
                                   qz=(QZ2[i % 2], bQZ2[i % 2]), qb=i % 2,
                                   kt=(lambda ch, ck=ck: KT[:, ch, ck * 128:(ck + 1) * 128]), ktb=[bKT[ck]],
                                   vfn=(lambda h, ck=ck: V[:, ck, h, :]), vb=[bV[ck]]))
            for e_kv in range(2):
                for idx, dl in enumerate(deltas_b):
                    ck = tq + dl - c0
                    assert 0 <= ck < nctx
                    tl.append(dict(kind="B", g=e_kv, idx=idx, n=len(deltas_b), nk=128,
                                   mi=(_IDXB[dl] if dl != 0 else None),
                                   qz=(QZ2[i % 2], bQZ2[i % 2]), qb=i % 2,
                                   kt=(lambda ch, ck=ck: KT[:, ch, ck * 128:(ck + 1) * 128]), ktb=[bKT[ck]],
                                   vfn=(lambda h, ck=ck, e_kv=e_kv: V[:, ck, 8 + e_kv, :]), vb=[bV[ck]]))
            tl[0]["first"] = (lambda i=i: first_nat(i))
            if n_q == NQ:
                tl[0]["first_back"] = (lambda i=i: first_back_nat(i))
            tl[-1]["last"] = (lambda n_now, i=i: epilogue(i, n_now))
            iters += tl

        def front(n, it):
            g, nk = it["g"], it["nk"]
            qz, bqz = it["qz"]
            if it.get("first"):
                it["first"]()
            sb_ = SBANK[n % 3]
            pt, bpt = PT4[n % NPT], bPT4[n % NPT]
            if it["kind"] == "A":
                fns = []
                for j in range(2):
                    ch = g * 2 + j
                    fns.append(("matmul", dict(out=PB[sb_][0:nk, j * 256:(j + 1) * 256], lhsT=it["kt"](ch),
                                               rhs=qz[:, ch, :, :], start=True, stop=True)))
            else:
                fns = [("matmul", dict(out=PB[sb_][0:nk, :], lhsT=it["kt"](4), rhs=qz[:, 4:8, g, :],
                                       start=True, stop=True))]
            P.group("pe", fns, reads=list(it["ktb"]) + [bQZlo[it["qb"]], bQZhi[it["qb"]]], writes=[bPB[sb_]])
            P.op("act", "activation", dict(out=pt[0:nk, :], in_=PB[sb_][0:nk, :], func=AF.Exp, scale=0.125),
                 reads=[bPB[sb_]], writes=[bpt])
            if it["mi"] is not None:
                ptv = pt[0:nk, :].rearrange("p (h t) -> p h t", h=4)
                P.op("dve", "tensor_tensor", dict(out=ptv, in0=ptv,
                                                  in1=MASKS[0:nk, it["mi"], :].unsqueeze(1).to_broadcast([nk, 4, 128]),
                                                  op=ALU.mult),
                     reads=[bpt, bMASKS], writes=[bpt])

        def back(n, it):
            g, nk, idx = it["g"], it["nk"], it["idx"]
            if it.get("first_back"):
                it["first_back"]()
            pt, bpt = PT4[n % NPT], bPT4[n % NPT]
            ob = it.get("ob", (2 + g) if it["kind"] == "A" else (4 + g))
            fns = []
            for hl in range(4):
                fns.append(("matmul", dict(out=PB[ob][:, hl * 65:(hl + 1) * 65],
                                           lhsT=pt[0:nk, hl * 128:(hl + 1) * 128], rhs=it["vfn"](g * 4 + hl),
                                           start=(idx == 0 and hl == 0), stop=(idx == it["n"] - 1),
                                           skip_group_check=True)))
            P.group("pe", fns, reads=[bpt] + list(it["vb"]), writes=[bPB[ob]])
            if it.get("last"):
                it["last"](n + LA)

        deferred = []

        def epilogue(i, n_now):
            xb = bXT[i % 2]
            obv = O32[:, 512:1024].rearrange("p (m e d) -> p m e d", e=2, d=64)

            def e1():
                for hg in range(2):
                    odt = ODT[i % 2][:, hg * 260:(hg + 1) * 260]
                    if n_q == NQ:
                        P.op("dve", "tensor_tensor", dict(out=odt, in0=PB[2 + hg][:, 0:260], in1=odt, op=ALU.add),
                             reads=[bPB[2 + hg], xb], writes=[xb])
                    else:
                        P.op("dve", "tensor_copy", dict(out=odt, in_=PB[2 + hg][:, 0:260]), reads=[bPB[2 + hg]], writes=[xb])
                for e_kv in range(2):
                    ov = PB[4 + e_kv][:, 0:260].rearrange("p (h d) -> p h d", d=65)
                    esv = ESINK.rearrange("p (m e) -> p m e", e=2)[:, :, e_kv]
                    dsl = slice(8 + e_kv * 4, 8 + (e_kv + 1) * 4)
                    P.op("dve", "tensor_copy", dict(out=obv[:, :, e_kv, :], in_=ov[:, :, 0:64]),
                         reads=[bPB[4 + e_kv]], writes=[bO32])
                    P.op("dve", "tensor_tensor", dict(out=DEN[:, dsl], in0=ov[:, :, 64], in1=esv, op=ALU.add),
                         reads=[bPB[4 + e_kv], bESINK], writes=[bDEN])

            def e2():
                for hg in range(2):
                    ov = ODT[i % 2][:, hg * 260:(hg + 1) * 260].rearrange("p (h d) -> p h d", d=65)
                    P.op("dve", "reciprocal", dict(out=RDEN[:, hg * 4:(hg + 1) * 4], in_=ov[:, :, 64]),
                         reads=[xb], writes=[bRDEN])
                    P.op("dve", "tensor_tensor", dict(
                        out=hv(O32[:, hg * 256:(hg + 1) * 256]), in0=ov[:, :, 0:64],
                        in1=RDEN[:, hg * 4:(hg + 1) * 4].unsqueeze(2).to_broadcast([128, 4, 64]), op=ALU.mult),
                        reads=[xb, bRDEN], writes=[bO32])
                P.op("dve", "reciprocal", dict(out=RDEN[:, 8:16], in_=DEN[:, 8:16]), reads=[bDEN], writes=[bRDEN])
                for e_kv in range(2):
                    dsl = slice(8 + e_kv * 4, 8 + (e_kv + 1) * 4)
                    P.op("dve", "tensor_tensor", dict(
                        out=obv[:, :, e_kv, :], in0=obv[:, :, e_kv, :],
                        in1=RDEN[:, dsl].unsqueeze(2).to_broadcast([128, 4, 64]), op=ALU.mult),
                        reads=[bO32, bRDEN], writes=[bO32])
                if os.environ.get("DUMP_O32"):
                    P.dma("sp", dict(out=yseq[(q0 + i) * 128:(q0 + i + 1) * 128, :], in_=O32), bO32, reads=[bO32])

            def e3():
                for half in range(2):
                    P.op("act", "activation", dict(out=JUNK[:, half * 512:(half + 1) * 512],
                                                   in_=O32[:, half * 512:(half + 1) * 512],
                                                   func=AF.Square, accum_out=SS[:, half:half + 1]),
                         reads=[bO32], writes=[bJUNK, bSS])
                rstd_chain(SS[:, 0:2], LNV[:, 0:2], RSTD[:, 0:2], 512, bSS, bLNV, bRSTD)

            def e4():
                for half in range(2):
                    P.op("dve", "tensor_scalar", dict(out=XS[:, half * 512:(half + 1) * 512],
                                                      in0=O32[:, half * 512:(half + 1) * 512],
                                                      scalar1=RSTD[:, half:half + 1], scalar2=None, op0=ALU.mult),
                         reads=[bO32, bRSTD], writes=[bXS])
                transposes_xs(6)

            def e5():
                evac_T(6, MT[:, :, i * 128:(i + 1) * 128], GOUT, bGOUT, bM[i])

            while deferred:
                deferred.pop(0)[1]()
            e1()
            for dist, fn in zip((1, 7, 11, 14), [e2, e3, e4, e5]):
                deferred.append((n_now + dist, fn))

        n = 0
        while n < len(iters) + LA or deferred:
            if n < len(iters):
                front(n, iters[n])
            if 0 <= n - LA < len(iters):
                back(n - LA, iters[n - LA])
            due = [d for d in deferred if d[0] <= n]
            for d in due:
                deferred.remove(d)
                d[1]()
            n += 1

        P.barrier()
        if stop == "1b":
            break
        def load_w(fb):
            s = fb % 2
            extra = list(bWO) if s == 1 else []
            P.dma("pool", dict(out=WUP[s], in_=w_up[:, fb * 512:(fb + 1) * 512].rearrange("(k p) f -> p k f", p=128)),
                  bWUP[s], writes=[bWUP[s]] + extra)
            for hh in range(2):
                P.dma("pool", dict(out=WDN[s][:, :, hh * 512:(hh + 1) * 512],
                                   in_=w_dn[fb * 512:(fb + 1) * 512, hh * 512:(hh + 1) * 512].rearrange("(c p) n -> p c n", p=128)),
                      bWDN[s][hh], writes=[bWDN[s][hh]] + extra)

        load_w(0)
        slots2a = {}

        def st2a_mm(t):
            hb = (t % 3) * 2
            fns = []
            for k in range(8):
                for half in range(2):
                    fns.append(("matmul", dict(out=PB[hb + half][:, :], lhsT=MT[:, k, t * 128:(t + 1) * 128],
                                               rhs=WO[:, k, half * 512:(half + 1) * 512], start=(k == 0), stop=(k == 7))))
            P.group("pe", fns, reads=[bM[t]] + bWO, writes=[bPB[hb], bPB[hb + 1]])

        XS2 = [XS, JUNK]
        bXS2 = [bXS, bJUNK]
        junk2 = XNT.rearrange("p k t -> p (k t)")

        def st2a_adds(t):
            hb = (t % 3) * 2
            slot = slots2a[t]
            for half in range(2):
                P.op("dve", "tensor_tensor", dict(out=H[:, t, half * 512:(half + 1) * 512], in0=PB[hb + half][:, :],
                                                  in1=XT[slot][:, half * 512:(half + 1) * 512], op=ALU.add),
                     reads=[bPB[hb + half], bXT[slot]], writes=[bH[t]])

        def st2a_act(t):
            xs_, bxs_ = XS2[t % 2], bXS2[t % 2]
            P.op("act", "activation", dict(out=junk2, in_=H[:, t, :], func=AF.Square, accum_out=SS[:, 0:1]),
                 reads=[bH[t]], writes=[bXNT, bSS])
            rstd_chain(SS[:, 0:1], LNV[:, 0:1], RSTD[:, 0:1], D, bSS, bLNV, bRSTD)
            P.op("act", "activation", dict(out=xs_, in_=H[:, t, :], func=AF.Copy, scale=RSTD[:, 0:1]),
                 reads=[bH[t], bRSTD], writes=[bxs_])

        def st2a_T(t):
            xs_, bxs_ = XS2[t % 2], bXS2[t % 2]
            fns = [("transpose", dict(out=PBh[6][:, k * 128:(k + 1) * 128], in_=xs_[:, k * 128:(k + 1) * 128],
                                      identity=IDENT)) for k in range(8)]
            P.group("pe", fns, reads=[bxs_, bIDENT], writes=[bPB[6]])
            evac_T(6, MT[:, :, t * 128:(t + 1) * 128], GMLP, bGMLP, bM[t])

        def st2a_ldx(t):
            tq = q0 + t
            slots2a[t] = load_x(xseq[tq * 128:(tq + 1) * 128, :])

        st2a_ldx(0)
        st2a_ldx(1)
        st2a_mm(0)
        st2a_mm(1)
        st2a_adds(0)
        st2a_act(0)
        for t in range(NQ):
            if t + 2 < NQ:
                st2a_mm(t + 2)
            if t + 1 < NQ:
                st2a_adds(t + 1)
                st2a_act(t + 1)
            if t + 2 < NQ:
                st2a_ldx(t + 2)
            st2a_T(t)

        if stop == "2a":
            P.barrier()
        else:
            for b_ in bUT + bR32:
                for hh in range(2):
                    for sem_, v_ in bWO[hh].r.items():
                        b_.r[sem_] = max(b_.r.get(sem_, 0), v_)
        if stop == "2a":
            for t in range(NQ):
                tq = q0 + t
                P.dma("sp", dict(out=yseq[tq * 128:(tq + 1) * 128, :], in_=H[:, t, :]), bH[t], reads=[bH[t]])
            break
        NFB = DFF // 512
        pairs = [(fb, sg) for fb in range(NFB) for sg in range(4)]
        utb_of = {}

        def up_proj(fb, sg):
            s = fb % 2
            utb = state["u"] % 2
            state["u"] += 1
            utb_of[(fb, sg)] = utb
            for c in range(4):
                ub = state["y"] % 2
                state["y"] += 1
                fns = [("matmul", dict(out=PB[ub][:, :], lhsT=WUP[s][:, k, c * 128:(c + 1) * 128],
                                       rhs=MT[:, k, sg * 512:(sg + 1) * 512], start=(k == 0), stop=(k == 7)))
                       for k in range(8)]
                P.group("pe", fns, reads=[bWUP[s]] + bM[sg * 4:sg * 4 + 4], writes=[bPB[ub]])
                P.op("act", "activation", dict(out=R32[ub], in_=PB[ub][:, :], func=AF.Relu),
                     reads=[bPB[ub]], writes=[bR32[ub]])
                P.op("act", "activation", dict(out=UT[utb][:, c, :], in_=R32[ub], func=AF.Square),
                     reads=[bR32[ub]], writes=[bUT[utb]])

        def down_proj(fb, sg):
            s = fb % 2
            utb = utb_of[(fb, sg)]
            for tl in range(4):
                t = sg * 4 + tl
                for half in range(2):
                    yb = 2 + ((tl * 2 + half) % 4)
                    fns = [("matmul", dict(out=PB[yb][:, :], lhsT=UT[utb][:, c, tl * 128:(tl + 1) * 128],
                                           rhs=WDN[s][:, c, half * 512:(half + 1) * 512], start=(c == 0), stop=(c == 3)))
                           for c in range(4)]
                    P.group("pe", fns, reads=[bUT[utb], bWDN[s][half]], writes=[bPB[yb]])
                    P.op("dve", "tensor_tensor", dict(out=H[:, t, half * 512:(half + 1) * 512], in0=PB[yb][:, :],
                                                      in1=H[:, t, half * 512:(half + 1) * 512], op=ALU.add),
                         reads=[bPB[yb], bH[t]], writes=[bH[t]])
                if fb == NFB - 1:
                    tq = q0 + t
                    P.dma("sp", dict(out=yseq[tq * 128:(tq + 1) * 128, :], in_=H[:, t, :]), bYST, reads=[bH[t]])

        load_w(1)
        up_proj(*pairs[0])
        for j, (fb, sg) in enumerate(pairs):
            if j + 1 < len(pairs):
                up_proj(*pairs[j + 1])
            down_proj(fb, sg)
            if sg == 3 and fb + 2 < NFB:
                load_w(fb + 2)
        if stop is None and n1a is None and gidx_ + 1 < len(groups):
            nsrc, nseq, nq0, nc0, nnctx = groups[gidx_ + 1][:5]
            ncs = 8 if (len(groups[gidx_ + 1]) > 5 and groups[gidx_ + 1][5] == "load") else 0
            nx = xsrc[nsrc][nseq]
            preload[gidx_ + 1] = {}
            for c_ in (ncs, ncs + 1):
                ts_ = nc0 + c_
                preload[gidx_ + 1][c_] = load_x(nx[ts_ * 128:(ts_ + 1) * 128, :])
        P.barrier()

    npad = int(os.environ.get("PADPE", "0"))
    if npad:
        fns = [("matmul", dict(out=PB[7][:, 0:128], lhsT=IDENT, rhs=IDENT, start=True, stop=True)) for _ in range(npad)]
        P.group("pe", fns, reads=[bIDENT], writes=[bPB[7]])
    npad = int(os.environ.get("PADPOOL", "0"))
    for _ in range(npad):
        P.op("pool", "memset", dict(ap=EPST, constant=EPS), writes=[bEPS])
    P.finish()
    P.emit()
    return nc


def prep_weights(norm_attn, w_in, q_norm_a, k_norm_a, q_norm_b, k_norm_b, sink_b,
                 out_norm_a, out_norm_b, w_o, norm_mlp, w_up, w_down):
    f = np.float32
    w_in = np.asarray(w_in, f)[0]
    w_o = np.asarray(w_o, f)[0]
    ar = np.arange
    qb_cols = np.concatenate([1536 + h * 64 + ar(64) for h in PERM_HEADS])
    cols = np.concatenate([ar(0, 512), qb_cols, 512 + ar(512), 1024 + ar(512), 2048 + ar(128), 2176 + ar(128)])
    w_in_p = np.ascontiguousarray(w_in[:, cols])
    rows_b = np.concatenate([512 + h * 64 + ar(64) for h in PERM_HEADS])
    w_o_p = np.ascontiguousarray(w_o[np.concatenate([ar(512), rows_b])])
    gout = np.concatenate([np.asarray(out_norm_a, f)[0], np.asarray(out_norm_b, f)[0][rows_b - 512]])

    def chunked(g):
        return np.ascontiguousarray(np.asarray(g, f).reshape(8, 128).T)

    cos, sin = _rope_tables()
    masks = np.ascontiguousarray(np.concatenate(_MASKS, axis=1)).astype(ml_dtypes.bfloat16)
    return {
        "w_in": w_in_p, "w_o": w_o_p,
        "w_up": np.ascontiguousarray(np.asarray(w_up, f)[0]),
        "w_down": np.ascontiguousarray(np.asarray(w_down, f)[0]),
        "gin": chunked(np.asarray(norm_attn, f)[0]), "gmlp": chunked(np.asarray(norm_mlp, f)[0]),
        "gout": chunked(gout),
        "qkg": np.ascontiguousarray(np.concatenate([np.asarray(q_norm_a, f)[0], np.asarray(q_norm_b, f)[0],
                                                    np.asarray(k_norm_a, f)[0], np.asarray(k_norm_b, f)[0]])[None, :]),
        "sinkp": np.ascontiguousarray(np.asarray(sink_b, f)[0][PERM_HEADS][None, :]),
        "cost": cos, "sint": sin, "masks": masks,
        "ident": np.eye(128, dtype=np.float32).astype(ml_dtypes.bfloat16),
    }


FULL_GROUPS = [("p", 0, 0, 0, 16), ("p", 1, 0, 0, 16), ("s", 0, 0, 0, 24, "save"), ("s", 0, 16, 8, 24, "load")]

_NC_CACHE = {}


def kernel(x_prompt, x_sample, norm_attn, w_in, q_norm_a, k_norm_a, q_norm_b, k_norm_b,
           sink_b, out_norm_a, out_norm_b, w_o, norm_mlp, w_up, w_down):
    n = 8
    x_prompt = np.asarray(x_prompt, np.float32)
    x_sample = np.asarray(x_sample, np.float32)
    shared = prep_weights(norm_attn, w_in, q_norm_a, k_norm_a, q_norm_b, k_norm_b, sink_b,
                          out_norm_a, out_norm_b, w_o, norm_mlp, w_up, w_down)
    if "full" not in _NC_CACHE:
        _NC_CACHE["full"] = build_program(FULL_GROUPS, 2, 2048, 1, 4096)
    nc = _NC_CACHE["full"]
    in_maps = []
    for c in range(n):
        m = dict(shared)
        m["xp"] = np.ascontiguousarray(x_prompt[2 * c:2 * c + 2])
        m["xs"] = np.ascontiguousarray(x_sample[c:c + 1])
        in_maps.append(m)
    res = run_bass_kernel_spmd(nc, in_maps, core_ids=list(range(n)))
    yp = np.concatenate([r["yp"] for r in res.results], axis=0).astype(np.float32)
    ys = np.concatenate([r["ys"] for r in res.results], axis=0).astype(np.float32)
    return (yp, ys)
```

```python
import numpy as np
import ml_dtypes
import concourse.bass as bass
import concourse.mybir as mybir
from concourse.bass_utils import run_bass_kernel_spmd

F32 = mybir.dt.float32
BF16 = mybir.dt.bfloat16
AF = mybir.ActivationFunctionType
ALU = mybir.AluOpType
AX = mybir.AxisListType

D = 1024
HD = 64
IN_COLS = 2304
DFF = 4096
EPS = 1e-6
NQ = 16
PERM_HEADS = [0, 4, 1, 5, 2, 6, 3, 7]


class Buf:
    __slots__ = ("name", "w", "r", "dsem", "dcount", "excl")

    def __init__(self, name, excl=False):
        self.name = name
        self.excl = excl
        self.w = None
        self.r = {}
        self.dsem = None
        self.dcount = 0


class Eng:
    def __init__(self, key, sem):
        self.key = key
        self.sem = sem
        self.count = 0
        self.prog = []
        self.waited = {}


class Prog:
    def __init__(self, nc):
        self.nc = nc
        self.eng = {}
        for key in ("pe", "act", "dve", "pool", "sp"):
            self.eng[key] = Eng(key, nc.alloc_semaphore("s_" + key))
        self.bsem = nc.alloc_semaphore("s_bar")
        self.bcount = 0
        self.bufs = []
        self.dma_bufs = []
        self.fresh_dma_sems = False
        import os
        for j in range(int(os.environ.get("DUMMYSEM", "0"))):
            nc.alloc_semaphore("dummy%d" % j)

    def buf(self, name, excl=False):
        b = Buf(name, excl)
        self.bufs.append(b)
        return b

    def _wait(self, E, tick):
        sem, val = tick
        if E.key == "pe" and sem is E.sem:
            return
        if E.waited.get(sem, 0) >= val:
            return
        E.waited[sem] = val
        E.prog.append(("wait", sem, val))

    @staticmethod
    def _split(reads, writes):
        writes = list(writes)
        r2 = []
        for b in reads:
            if b.excl:
                if b not in writes:
                    writes.append(b)
            else:
                r2.append(b)
        return r2, writes

    def _deps(self, E, reads, writes):
        for b in reads:
            if b.w is not None:
                self._wait(E, b.w)
        for b in writes:
            if b.w is not None:
                self._wait(E, b.w)
            for s, v in b.r.items():
                self._wait(E, (s, v))

    def _commit(self, tick, reads, writes):
        for b in reads:
            b.r[tick[0]] = tick[1]
        for b in writes:
            b.w = tick
            b.r = {}

    def op(self, ek, name, kw, reads=(), writes=()):
        reads, writes = self._split(reads, writes)
        E = self.eng[ek]
        self._deps(E, reads, writes)
        E.count += 1
        tick = (E.sem, E.count)
        E.prog.append(("op", (name, kw), E.sem, 1))
        self._commit(tick, reads, writes)

    def group(self, ek, fns, reads=(), writes=()):
        reads, writes = self._split(reads, writes)
        E = self.eng[ek]
        self._deps(E, reads, writes)
        for fn in fns[:-1]:
            E.prog.append(("op", fn, None, 0))
        E.count += 1
        tick = (E.sem, E.count)
        E.prog.append(("op", fns[-1], E.sem, 1))
        self._commit(tick, reads, writes)

    def dma(self, ek, kw, primary, reads=(), writes=()):
        fn = ("dma_start", kw)
        E = self.eng[ek]
        self._deps(E, reads, writes)
        if primary.dsem is None:
            self.nsem = getattr(self, "nsem", 0) + 1
            primary.dsem = self.nc.alloc_semaphore("d%d_%s" % (self.nsem, primary.name))
            self.dma_bufs.append(primary)
        primary.dcount += 1
        tick = (primary.dsem, 16 * primary.dcount)
        E.prog.append(("op", fn, primary.dsem, 16))
        self._commit(tick, reads, writes)

    def barrier(self):
        sp = self.eng["sp"]
        for k in ("pe", "act", "dve", "pool"):
            E = self.eng[k]
            if E.count:
                self._wait(sp, (E.sem, E.count))
        for b in self.dma_bufs:
            self._wait(sp, (b.dsem, 16 * b.dcount))
        if self.fresh_dma_sems:
            for b in self.dma_bufs:
                b.dsem = None
                b.dcount = 0
            self.dma_bufs = []
        self.bcount += 1
        sp.prog.append(("inc", self.bsem, 1))
        for k in ("pe", "act", "dve", "pool"):
            self.eng[k].prog.append(("wait", self.bsem, self.bcount))
        for b in self.bufs:
            b.w = None
            b.r = {}

    def finish(self):
        sp = self.eng["sp"]
        for k in ("pe", "act", "dve", "pool"):
            E = self.eng[k]
            if E.count:
                self._wait(sp, (E.sem, E.count))
        for b in self.dma_bufs:
            self._wait(sp, (b.dsem, 16 * b.dcount))

    def emit(self):
        nc = self.nc

        def replay(E):
            def f(eng):
                for item in E.prog:
                    if item[0] == "wait":
                        eng.wait_ge(item[1], item[2])
                    elif item[0] == "inc":
                        eng.sem_inc(item[1], item[2])
                    elif item[0] == "clear":
                        eng.sem_clear(item[1])
                    else:
                        ins = getattr(eng, item[1][0])(**item[1][1])
                        if item[2] is not None:
                            ins.then_inc(item[2], item[3])
            return f

        with nc.Block() as block:
            block.sync(replay(self.eng["sp"]))
            block.gpsimd(replay(self.eng["pool"]))
            block.scalar(replay(self.eng["act"]))
            block.vector(replay(self.eng["dve"]))
            block.tensor(replay(self.eng["pe"]))


def _mask_tables():
    a = np.arange(128)[:, None]
    b = np.arange(128)[None, :]
    tabs = []
    idxA = {}
    for dl in range(-2, 3):
        diff = 128 * dl + a - b
        m = (np.abs(diff) <= 64).astype(np.float32)
        m += ((diff % 4 == 0) & (np.abs(diff) <= 256)).astype(np.float32)
        idxA[dl] = len(tabs)
        tabs.append(m)
    idxB = {}
    for dl in (-1, 1):
        diff = 128 * dl + a - b
        m = (np.abs(diff) <= 128).astype(np.float32)
        idxB[dl] = len(tabs)
        tabs.append(m)
    idxD = {}
    for off in (0, 128, -64, 64):
        m = (np.abs(off + a - b) <= 64).astype(np.float32)
        idxD[off] = len(tabs)
        tabs.append(m)
    return tabs, idxA, idxB, idxD


_MASKS, _IDXA, _IDXB, _IDXD = _mask_tables()
NM = len(_MASKS)


def _rope_tables():
    half = 8
    inv = 500000.0 ** (-(np.arange(half, dtype=np.float64) * 2.0 / 16.0))
    pos = np.arange(4096, dtype=np.float64)
    ang = pos[:, None] * inv[None, :]
    cos = np.cos(ang).astype(np.float32).reshape(32, 128, half).transpose(1, 0, 2).reshape(128, 32 * half)
    sin = np.sin(ang).astype(np.float32).reshape(32, 128, half).transpose(1, 0, 2).reshape(128, 32 * half)
    return np.ascontiguousarray(cos), np.ascontiguousarray(sin)


def build_program(groups, n_p, len_p, n_s, len_s, stop=None, n1a=None, n1b=None):
    nc = bass.Bass("TRN2", target_bir_lowering=False)
    P = Prog(nc)

    def din(name, shape, dt=F32):
        return nc.dram_tensor(name, list(shape), dt, kind="ExternalInput").ap()

    xsrc = {}
    ydst = {}
    if n_p:
        xsrc["p"] = din("xp", [n_p, len_p, D])
        ydst["p"] = nc.dram_tensor("yp", [n_p, len_p, D], F32, kind="ExternalOutput").ap()
    if n_s:
        xsrc["s"] = din("xs", [n_s, len_s, D])
        ydst["s"] = nc.dram_tensor("ys", [n_s, len_s, D], F32, kind="ExternalOutput").ap()
    w_in = din("w_in", [D, IN_COLS])
    w_o = din("w_o", [D, D])
    w_up = din("w_up", [D, DFF])
    w_dn = din("w_down", [DFF, D])
    d_gin = din("gin", [128, 8])
    d_gmlp = din("gmlp", [128, 8])
    d_gout = din("gout", [128, 8])
    d_qkg = din("qkg", [1, 256])
    d_sink = din("sinkp", [1, 8])
    d_cos = din("cost", [128, 256])
    d_sin = din("sint", [128, 256])
    d_masks = din("masks", [128, NM * 128], BF16)
    d_ident = din("ident", [128, 128], BF16)

    def sb(name, shape, dt):
        return nc.alloc_sbuf_tensor("sb_" + name, shape, dt)
    WIN_t = sb("WIN", [128, 8 * IN_COLS], BF16)
    WIN = WIN_t.ap().rearrange("p (k c) -> p k c", k=8)
    R1 = sb("R1", [128, 16384], BF16)
    QT = R1.ap().rearrange("p (m t) -> p m t", m=8)
    WUP = [R1.ap()[:, s * 8192:s * 8192 + 4096].rearrange("p (k f) -> p k f", k=8) for s in range(2)]
    WDN = [R1.ap()[:, s * 8192 + 4096:(s + 1) * 8192].rearrange("p (c n) -> p c n", c=4) for s in range(2)]
    R2 = sb("R2", [128, 32768], BF16)
    KT = R2.ap()[:, 0:15360].rearrange("p (m t) -> p m t", m=5)
    V = R2.ap()[:, 15360:15360 + 15600].rearrange("p (c h d) -> p c h d", c=24, h=10)
    H = R2.ap().bitcast(F32).rearrange("p (t f) -> p t f", t=16)
    R3 = sb("R3", [128, 16384], BF16)
    MT = R3.ap().rearrange("p (m t) -> p m t", m=8)
    r3f = R3.ap().bitcast(F32)
    STG = r3f[:, 0:1664]
    SQ = r3f[:, 1664:3328]
    ROPEIN = r3f[:, 3328:3328 + 416]
    R4 = sb("R4", [128, 8192], BF16)
    r4f = R4.ap().bitcast(F32)
    QN32 = r4f[:, 0:1664]
    QKB = R4.ap()[:, 3328:3328 + 1664]
    PTB = [R4.ap()[:, 4992 + s * 512:4992 + (s + 1) * 512] for s in range(2)]
    O32 = r4f[:, 3008:3008 + 1024]
    PT4 = [PTB[0], PTB[1], R4.ap()[:, 0:512], R4.ap()[:, 512:1024], R4.ap()[:, 4112:4624]]
    QZ2 = [None, R4.ap()[:, 1024:3072].rearrange("p (m v t) -> p m v t", m=8, v=2)]
    WO = R1.ap()[:, 8192:16384].rearrange("p (k n) -> p k n", k=8)
    UT = [R4.ap()[:, s * 2048:(s + 1) * 2048].rearrange("p (c t) -> p c t", c=4) for s in range(2)]
    R32 = [r4f[:, 2048 + s * 512:2048 + (s + 1) * 512] for s in range(2)]

    QZ = sb("qz", [128, 8, 2, 128], BF16).ap()
    QZ2[0] = QZ
    XT = [sb("xt%d" % s, [128, D], F32).ap() for s in range(2)]
    XS = sb("xsb", [128, D], BF16).ap()
    JUNK = sb("junk", [128, D], BF16).ap()
    XNT = sb("xnT", [128, 8, 128], BF16).ap()
    xnt_flat = XNT.rearrange("p k t -> p (k t)")
    PT4 += [xnt_flat[:, 0:512], xnt_flat[:, 512:1024]]
    MASKS = sb("masks", [128, NM, 128], BF16).ap()
    COS = sb("cos", [128, 32, 8], F32).ap()
    SIN = sb("sin", [128, 32, 8], F32).ap()
    IDENT = sb("ident", [128, 128], BF16).ap()
    GIN = sb("gin", [128, 8], F32).ap()
    GMLP = sb("gmlp", [128, 8], F32).ap()
    GOUT = sb("gout", [128, 8], F32).ap()
    QKG = sb("qkg", [128, 4, 64], F32).ap()
    SINK = sb("sink", [128, 8], F32).ap()
    ESINK = sb("esink", [128, 8], F32).ap()
    EPST = sb("epst", [128, 1], F32).ap()
    SS = sb("ss", [128, 2], F32).ap()
    LNV = sb("lnv", [128, 2], F32).ap()
    RSTD = sb("rstd", [128, 2], F32).ap()
    SSQ = sb("ssq", [128, 26], F32).ap()
    LNQ = sb("lnq", [128, 26], F32).ap()
    RQ = sb("rq", [128, 26], F32).ap()
    RT = [r3f[:, 3744 + j * 208:3744 + (j + 1) * 208].rearrange("p (h d) -> p h d", d=8) for j in range(4)]
    ODS = r4f[:, 1536:1536 + 520]
    VDS = [XT[s_].bitcast(BF16)[:, 0:1040].rearrange("p (c f) -> p c f", c=2) for s_ in range(2)]
    ODT = [XT[s_][:, 0:520] for s_ in range(2)]
    vd_dram = nc.dram_tensor("vd_scratch", [24 * 128, 520], BF16, kind="Internal").ap()
    od_dram = nc.dram_tensor("od_scratch", [NQ * 128, 520], F32, kind="Internal").ap()
    kt_scr = nc.dram_tensor("kt_scratch", [128, 5 * 1024], BF16, kind="Internal").ap()
    v_scr = nc.dram_tensor("v_scratch", [128, 8 * 650], BF16, kind="Internal").ap()
    DEN = sb("den", [128, 16], F32).ap()
    RDEN = sb("rden", [128, 16], F32).ap()

    PB = [nc.alloc_psum_tensor("pb%d" % j, [128, 512], F32).ap() for j in range(8)]
    PBh = [p.bitcast(BF16) for p in PB]

    WIN_COLS = [(0, 512), (512, 1024), (1024, 1536), (1536, 2048), (2048, 2304)]
    bWIN = [P.buf("win%d" % k) for k in range(5)]
    bXT = [P.buf("xt%d" % s) for s in range(2)]
    bXS, bJUNK, bXNT = P.buf("xs"), P.buf("junk"), P.buf("xnt")
    bMASKS, bCOS, bSIN, bIDENT = P.buf("masks"), P.buf("cos"), P.buf("sin"), P.buf("ident")
    bGIN, bGMLP, bGOUT, bQKG = P.buf("gin"), P.buf("gmlp"), P.buf("gout"), P.buf("qkg")
    bSINK, bESINK, bEPS = P.buf("sink"), P.buf("esink"), P.buf("eps")
    bSS, bLNV, bRSTD = P.buf("ss"), P.buf("lnv"), P.buf("rstd")
    bSSQ, bLNQ, bRQ = P.buf("ssq"), P.buf("lnq"), P.buf("rq")
    bRT = [P.buf("rt%d" % j) for j in range(4)]
    bDEN, bRDEN = P.buf("den"), P.buf("rden")
    bPB = [P.buf("pb%d" % j, excl=True) for j in range(8)]
    bQN32, bQKB, bO32 = P.buf("qn32"), P.buf("qkb"), P.buf("o32")
    bSTG, bSQ, bRIN = P.buf("stg"), P.buf("sq"), P.buf("rin")
    bXNTb = P.buf("xntb")
    bPT = [P.buf("pt%d" % s) for s in range(2)]
    bKT = [P.buf("kt%d" % c) for c in range(24)]
    bV = [P.buf("v%d" % c) for c in range(24)]
    bQT = [P.buf("qt%d" % i) for i in range(NQ)]
    bM = [P.buf("m%d" % i) for i in range(NQ)]
    bH = [P.buf("h%d" % i) for i in range(NQ)]
    bWO = [P.buf("wo%d" % hh) for hh in range(2)]
    bQZ = P.buf("qz")
    bQZ2 = [bQZ, P.buf("qz1")]
    bQZlo = [P.buf("qzlo0"), P.buf("qzlo1")]
    bQZhi = [P.buf("qzhi0"), P.buf("qzhi1")]
    bPT4 = [bPT[0], bPT[1], P.buf("pt2"), P.buf("pt3"), P.buf("pt4"), P.buf("pt5"), P.buf("pt6")]
    NPT = len(bPT4)
    bYST = P.buf("yst")
    bVDd, bVDW, bODd, bODS = P.buf("vdd"), P.buf("vdw"), P.buf("odd"), P.buf("ods")
    bVDS = [P.buf("vds%d" % b) for b in range(4)]
    bKTS, bVSS, bKTL = P.buf("kts"), P.buf("vss"), P.buf("ktl")
    bVL = [P.buf("vl0"), P.buf("vl1")]
    bWUP = [P.buf("wup%d" % s) for s in range(2)]
    bWDN = [[P.buf("wdn%d_%d" % (s, hh)) for hh in range(2)] for s in range(2)]
    bUT = [P.buf("ut%d" % s) for s in range(2)]
    bR32 = [P.buf("r32%d" % s) for s in range(2)]

    P.dma("sp", dict(out=IDENT, in_=d_ident), bIDENT, writes=[bIDENT])
    P.dma("sp", dict(out=MASKS, in_=d_masks.rearrange("p (m t) -> p m t", m=NM)), bMASKS, writes=[bMASKS])
    P.dma("sp", dict(out=COS, in_=d_cos.rearrange("p (t f) -> p t f", t=32)), bCOS, writes=[bCOS])
    P.dma("sp", dict(out=SIN, in_=d_sin.rearrange("p (t f) -> p t f", t=32)), bSIN, writes=[bSIN])
    P.dma("sp", dict(out=GIN, in_=d_gin), bGIN, writes=[bGIN])
    P.dma("sp", dict(out=GMLP, in_=d_gmlp), bGMLP, writes=[bGMLP])
    P.dma("sp", dict(out=GOUT, in_=d_gout), bGOUT, writes=[bGOUT])
    P.dma("sp", dict(out=QKG.rearrange("p a b -> p (a b)"), in_=d_qkg.partition_broadcast(128)), bQKG, writes=[bQKG])
    P.dma("sp", dict(out=SINK, in_=d_sink.partition_broadcast(128)), bSINK, writes=[bSINK])
    P.op("pool", "memset", dict(ap=EPST, constant=EPS), writes=[bEPS])
    P.op("pool", "memset", dict(ap=QZ, constant=0.0), writes=[bQZlo[0], bQZhi[0]])
    for j, (a0, a1) in enumerate(WIN_COLS):
        P.dma("pool", dict(out=WIN[:, :, a0:a1], in_=w_in[:, a0:a1].rearrange("(k p) f -> p k f", p=128)),
              bWIN[j], writes=[bWIN[j]])
    P.op("act", "activation", dict(out=ESINK, in_=SINK, func=AF.Exp), reads=[bSINK], writes=[bESINK])

    def rstd_chain(ss_ap, ln_ap, r_ap, n, bss, bln, br):
        P.op("act", "activation", dict(out=ln_ap, in_=ss_ap, func=AF.Ln, scale=1.0 / n, bias=EPST),
             reads=[bss, bEPS], writes=[bln])
        P.op("act", "activation", dict(out=r_ap, in_=ln_ap, func=AF.Exp, scale=-0.5), reads=[bln], writes=[br])

    import os
    CUT = int(os.environ.get("CUT", "99"))
    state = {"x": 0, "s": 0, "u": 0, "y": 0, "mm": 0}

    def load_x(xrows):
        slot = state["x"] % 2
        state["x"] += 1
        P.dma("sp", dict(out=XT[slot], in_=xrows), bXT[slot], writes=[bXT[slot]])
        return slot

    def transposes_xs(ps_idx):
        fns = [("transpose", dict(out=PBh[ps_idx][:, k * 128:(k + 1) * 128], in_=XS[:, k * 128:(k + 1) * 128],
                                  identity=IDENT)) for k in range(8)]
        P.group("pe", fns, reads=[bXS, bIDENT], writes=[bPB[ps_idx]])

    def norm_transpose(src_ap, bsrc, ps_idx):
        P.op("act", "activation", dict(out=JUNK, in_=src_ap, func=AF.Square, accum_out=SS[:, 0:1]),
             reads=[bsrc], writes=[bJUNK, bSS])
        rstd_chain(SS[:, 0:1], LNV[:, 0:1], RSTD[:, 0:1], D, bSS, bLNV, bRSTD)
        P.op("act", "activation", dict(out=XS, in_=src_ap, func=AF.Copy, scale=RSTD[:, 0:1]),
             reads=[bsrc, bRSTD], writes=[bXS])
        transposes_xs(ps_idx)

    def evac_T(ps_idx, out_ap, gain_ap, bgain, bout):
        P.op("dve", "tensor_tensor", dict(out=out_ap, in0=PBh[ps_idx].rearrange("p (k t) -> p k t", k=8),
                                          in1=gain_ap.unsqueeze(2).to_broadcast([128, 8, 128]), op=ALU.mult),
             reads=[bPB[ps_idx], bgain], writes=[bout])

    def hv(ap):
        return ap.rearrange("p (h d) -> p h d", d=64)

    preload = {}
    for gidx_, gspec in enumerate(groups):
        (src, seq, q0, c0, nctx) = gspec[:5]
        halo_mode = gspec[5] if len(gspec) > 5 else None
        if stop == "setup":
            break
        xseq = xsrc[src][seq]
        yseq = ydst[src][seq]
        L = xseq.shape[0]
        nts = L // 128

        P.op("pool", "memset", dict(ap=V[:, :, :, 64:65], constant=1.0), writes=bV[:nctx])

        n_c = nctx if n1a is None else n1a
        c_start = 0
        if halo_mode == "load":
            c_start = 8
            P.dma("sp", dict(out=KT[:, :, 0:1024], in_=kt_scr.rearrange("p (m t) -> p m t", m=5)), bKTL,
                  reads=[bKTS], writes=bKT[0:8] + [bKTL])
            for half_ in range(2):
                P.dma("sp", dict(out=V[:, half_ * 4:(half_ + 1) * 4, :, :].rearrange("p c h d -> p (c h d)"),
                                 in_=v_scr[:, half_ * 2600:(half_ + 1) * 2600]), bVL[half_],
                      reads=[bVSS], writes=bV[half_ * 4:(half_ + 1) * 4] + [bVL[half_]])
            for c_ in range(8):
                P.dma("pool", dict(out=vd_dram[c_ * 128:(c_ + 1) * 128, :],
                                   in_=V[:, c_, 0:8, :].rearrange("p h d -> p (h d)")),
                      bVDW, reads=[bV[c_]], writes=[bVDd])
        BANKS = [(0, 0, 512), (1, 512, 1024), (2, 1024, 1536), (3, 1536, 2048), (4, 2048, 2304)]
        SEGS = [(0, 0, 8, 0), (1, 8, 8, 1), (2, 16, 8, 2), (4, 24, 2, 3)]
        stv = hv(STG)
        kb = hv(QKB)
        rin = ROPEIN.rearrange("p (h d) -> p h d", d=16)

        def info(c):
            ts = c0 + c
            own = q0 <= ts < q0 + NQ
            return dict(ts=ts, own=own, qi=ts - q0, banks=BANKS if own else BANKS[2:],
                        segs=SEGS if own else SEGS[2:], h_lo=0 if own else 16)

        slots1a = {}

        def st_ldx(c):
            nf = info(c)
            if c in preload.get(gidx_, {}):
                slots1a[c] = preload[gidx_][c]
                return
            slots1a[c] = load_x(xseq[nf["ts"] * 128:(nf["ts"] + 1) * 128, :])

        def st_A(c):
            nf = info(c)
            slot = slots1a[c]
            P.op("act", "activation", dict(out=JUNK, in_=XT[slot], func=AF.Square, accum_out=SS[:, 0:1]),
                 reads=[bXT[slot]], writes=[bJUNK, bSS])
            rstd_chain(SS[:, 0:1], LNV[:, 0:1], RSTD[:, 0:1], D, bSS, bLNV, bRSTD)
            P.op("act", "activation", dict(out=XS, in_=XT[slot], func=AF.Copy, scale=RSTD[:, 0:1]),
                 reads=[bXT[slot], bRSTD], writes=[bXS])

        XNT2 = [XNT, R3.ap()[:, 12288:13312].rearrange("p (k t) -> p k t", k=8)]
        bXNT2 = [bXNT, bXNTb]

        def st_MM(c, which):
            nf = info(c)
            xnt, bxnt = XNT2[c % 2], bXNT2[c % 2]
            part = [b for b in nf["banks"] if (b[0] < 2) == (which == 0)]
            if not part:
                return
            fns = []
            for k in range(8):
                for (bk, a0, a1) in part:
                    fns.append(("matmul", dict(out=PB[bk][:, 0:a1 - a0], lhsT=xnt[:, k, :], rhs=WIN[:, k, a0:a1],
                                               start=(k == 0), stop=(k == 7))))
            P.group("pe", fns, reads=[bxnt] + [bWIN[bk] for (bk, _, _) in part],
                    writes=[bPB[bk] for (bk, _, _) in part])

        def st_C(c, part):
            nf = info(c)
            for (bk, h0, nh, gt) in [sg_ for sg_ in nf["segs"] if (sg_[0] < 2) == (part == 0)]:
                P.op("act", "activation", dict(out=SQ[:, h0 * 64:(h0 + nh) * 64], in_=PB[bk][:, 0:nh * 64], func=AF.Square),
                     reads=[bPB[bk]], writes=[bSQ])
                P.op("dve", "tensor_tensor", dict(
                    out=hv(STG[:, h0 * 64:(h0 + nh) * 64]), in0=hv(PB[bk][:, 0:nh * 64]),
                    in1=QKG[:, gt, :].unsqueeze(1).to_broadcast([128, nh, 64]), op=ALU.mult),
                    reads=[bPB[bk], bQKG], writes=[bSTG])
            if part == 0:
                return
            P.op("act", "activation", dict(out=V[:, c, 0:8, 0:64], in_=hv(PB[3][:, :]), func=AF.Copy),
                 reads=[bPB[3]], writes=[bV[c]])
            P.op("act", "activation", dict(out=V[:, c, 8:10, 0:64], in_=hv(PB[4][:, 128:256]), func=AF.Copy),
                 reads=[bPB[4]], writes=[bV[c]])
            P.dma("pool", dict(out=vd_dram[c * 128:(c + 1) * 128, :], in_=V[:, c, 0:8, :].rearrange("p h d -> p (h d)")),
                  bVDW, reads=[bV[c]], writes=[bVDd])

        def st_D(c):
            nf = info(c)
            h_lo, h_hi = nf["h_lo"], 26
            nh_all = h_hi - h_lo
            ts = nf["ts"]
            P.op("dve", "tensor_reduce", dict(out=SSQ[:, h_lo:h_hi], in_=hv(SQ[:, h_lo * 64:h_hi * 64]),
                                              axis=AX.X, op=ALU.add), reads=[bSQ], writes=[bSSQ])
            rstd_chain(SSQ[:, h_lo:h_hi], LNQ[:, h_lo:h_hi], RQ[:, h_lo:h_hi], HD, bSSQ, bLNQ, bRQ)
            P.op("dve", "tensor_tensor", dict(
                out=kb[:, h_lo:h_hi, 16:64], in0=stv[:, h_lo:h_hi, 16:64],
                in1=RQ[:, h_lo:h_hi].unsqueeze(2).to_broadcast([128, nh_all, 48]), op=ALU.mult),
                reads=[bSTG, bRQ], writes=[bQKB])
            P.op("dve", "tensor_tensor", dict(
                out=rin[:, h_lo:h_hi, :], in0=stv[:, h_lo:h_hi, 0:16],
                in1=RQ[:, h_lo:h_hi].unsqueeze(2).to_broadcast([128, nh_all, 16]), op=ALU.mult),
                reads=[bSTG, bRQ], writes=[bRIN])
            x1 = rin[:, h_lo:h_hi, 0:8]
            x2 = rin[:, h_lo:h_hi, 8:16]
            cosb = COS[:, ts, :].unsqueeze(1).to_broadcast([128, nh_all, 8])
            sinb = SIN[:, ts, :].unsqueeze(1).to_broadcast([128, nh_all, 8])
            for j, (xa, tb, bt) in enumerate([(x1, cosb, bCOS), (x2, sinb, bSIN), (x2, cosb, bCOS), (x1, sinb, bSIN)]):
                P.op("pool", "tensor_tensor", dict(out=RT[j][:, h_lo:h_hi, :], in0=xa, in1=tb, op=ALU.mult),
                     reads=[bRIN, bt], writes=[bRT[j]])
            P.op("dve", "tensor_tensor", dict(out=kb[:, h_lo:h_hi, 0:8], in0=RT[0][:, h_lo:h_hi, :],
                                              in1=RT[1][:, h_lo:h_hi, :], op=ALU.subtract),
                 reads=[bRT[0], bRT[1]], writes=[bQKB])
            P.op("dve", "tensor_tensor", dict(out=kb[:, h_lo:h_hi, 8:16], in0=RT[2][:, h_lo:h_hi, :],
                                              in1=RT[3][:, h_lo:h_hi, :], op=ALU.add),
                 reads=[bRT[2], bRT[3]], writes=[bQKB])

        def st_T2(c):
            nf = info(c)
            if nf["own"]:
                fns = [("transpose", dict(out=PBh[6][:, m * 128:(m + 1) * 128], in_=QKB[:, m * 128:(m + 1) * 128],
                                          identity=IDENT)) for m in range(8)]
                P.group("pe", fns, reads=[bQKB, bIDENT], writes=[bPB[6]])
            fns = [("transpose", dict(out=PBh[7][:, m * 128:(m + 1) * 128],
                                      in_=QKB[:, 1024 + m * 128:1024 + (m + 1) * 128], identity=IDENT)) for m in range(5)]
            P.group("pe", fns, reads=[bQKB, bIDENT], writes=[bPB[7]])

        def st_E(c):
            nf = info(c)
            if nf["own"]:
                qi = nf["qi"]
                P.op("dve", "tensor_copy", dict(out=QT[:, :, qi * 128:(qi + 1) * 128],
                                                in_=PBh[6].rearrange("p (m t) -> p m t", m=8)),
                     reads=[bPB[6]], writes=[bQT[qi]])
            P.op("dve", "tensor_copy", dict(out=KT[:, :, c * 128:(c + 1) * 128],
                                            in_=PBh[7][:, 0:640].rearrange("p (m t) -> p m t", m=5)),
                 reads=[bPB[7]], writes=[bKT[c]])

        st_ldx(c_start)
        if n_c > c_start + 1:
            st_ldx(c_start + 1)
        st_A(c_start)
        for c in range(c_start, n_c + 2):
            if c_start <= c - 2 < n_c:
                st_D(c - 2)
            if c_start <= c - 1 < n_c:
                st_MM(c - 1, 0)
            if c < n_c:
                transposes_xs(5)
            if c_start <= c - 1 < n_c:
                st_C(c - 1, 0)
            if c < n_c:
                evac_T(5, XNT2[c % 2], GIN, bGIN, bXNT2[c % 2])
            if c_start <= c - 1 < n_c:
                st_MM(c - 1, 1)
            if c_start <= c - 2 < n_c:
                st_T2(c - 2)
            if c_start <= c - 1 < n_c:
                st_C(c - 1, 1)
            if c_start <= c - 2 < n_c:
                st_E(c - 2)
            if c + 2 < n_c:
                st_ldx(c + 2)
            if c + 1 < n_c:
                st_A(c + 1)

        if halo_mode == "save":
            P.dma("pool", dict(out=kt_scr.rearrange("p (m t) -> p m t", m=5), in_=KT[:, :, 1024:2048]), bKTS,
                  reads=bKT[8:16], writes=[bKTS])
            for half_ in range(2):
                P.dma("pool", dict(out=v_scr[:, half_ * 2600:(half_ + 1) * 2600],
                                   in_=V[:, 8 + half_ * 4:8 + (half_ + 1) * 4, :, :].rearrange("p c h d -> p (c h d)")),
                      bVSS, reads=bV[8 + half_ * 4:8 + (half_ + 1) * 4], writes=[bVSS])

        if stop == "1a":
            break
        def alias_after(dst_bufs, src_bufs):
            for d_ in dst_bufs:
                for s_ in src_bufs:
                    for sem_, v_ in s_.r.items():
                        d_.r[sem_] = max(d_.r.get(sem_, 0), v_)
                    if s_.w is not None:
                        d_.r[s_.w[0]] = max(d_.r.get(s_.w[0], 0), s_.w[1])

        alias_after([bODS, bPT4[4]], [bQKB])
        alias_after([bPT4[5], bPT4[6]], [bXNT])
        alias_after(bM, [bSTG, bSQ, bRIN, bXNTb] + bRT)
        P.op("pool", "memset", dict(ap=QZ2[1], constant=0.0), writes=[bQZlo[1], bQZhi[1]])
        n_q = NQ if n1b is None else n1b
        LA = 5
        SBANK = [0, 1, 7]

        def sl(st_, n_, step):
            return slice(st_, st_ + step * (n_ - 1) + 1, step)

        n_kpos = nctx * 8
        kchunks = [(kc, min(128, n_kpos - 128 * kc)) for kc in range((n_kpos + 127) // 128)]
        vd_cls = vd_dram.rearrange("(a s) f -> s a f", s=16)
        od_cls = od_dram.rearrange("(j s) f -> s j f", s=16)

        NVB = 4 if len(kchunks) == 1 else 2
        nkc = len(kchunks)
        VDSn = [XT[b % 2].bitcast(BF16)[:, (b // 2) * 520 * nkc:(b // 2 + 1) * 520 * nkc].rearrange("p (c f) -> p c f", c=nkc)
                for b in range(NVB)]

        def load_vd(r):
            b = r % NVB
            for kc, na in kchunks:
                P.dma("sp", dict(out=VDSn[b][0:na, kc, :], in_=vd_cls[r][128 * kc:128 * kc + na]),
                      bVDS[b], reads=[bVDd], writes=[bVDS[b], bXT[b % 2]])

        def fill_qz(kind, j):
            b = j % 2
            qz = QZ2[b]
            if kind == "D":
                P.op("act", "activation", dict(out=qz[0:64, 0:4, 0, :], in_=QT[0:64, 0:4, sl(j, 128, 16)], func=AF.Copy),
                     reads=bQT, writes=[bQZlo[b]])
                P.op("dve", "tensor_copy", dict(out=qz[64:128, 0:4, 1, :], in_=QT[64:128, 0:4, sl(j, 128, 16)]),
                     reads=bQT, writes=[bQZhi[b]])
            else:
                P.op("act", "activation", dict(out=qz[0:64, :, 0, :], in_=QT[0:64, :, j * 128:(j + 1) * 128], func=AF.Copy),
                     reads=[bQT[j]], writes=[bQZlo[b]])
                P.op("dve", "tensor_copy", dict(out=qz[64:128, :, 1, :], in_=QT[64:128, :, j * 128:(j + 1) * 128]),
                     reads=[bQT[j]], writes=[bQZhi[b]])

        groups_order = ([("D", r) for r in range(16)] if n_q == NQ else []) + [("N", i) for i in range(n_q)]

        def load_wo():
            for hh in range(2):
                P.dma("pool", dict(out=WO[:, :, hh * 512:(hh + 1) * 512],
                                   in_=w_o[:, hh * 512:(hh + 1) * 512].rearrange("(k p) n -> p k n", p=128)),
                      bWO[hh], writes=[bWO[hh]] + bQT + [bWUP[1]] + bWDN[1])

        def first_any(gidx):
            if gidx + 1 < len(groups_order):
                fill_qz(*groups_order[gidx + 1])
                if gidx + 2 == len(groups_order):
                    load_wo()

        def first_dil(r):
            first_any(r)

        def last_dil(r):
            if r + NVB < 16:
                load_vd(r + NVB)
            b0 = 2 + 2 * (r % 2)
            P.op("act", "activation", dict(out=ODS[:, 0:260], in_=PB[b0][:, 0:260], func=AF.Copy),
                 reads=[bPB[b0]], writes=[bODS])
            P.op("dve", "tensor_copy", dict(out=ODS[:, 260:520], in_=PB[b0 + 1][:, 0:260]),
                 reads=[bPB[b0 + 1]], writes=[bODS])
            P.dma("sp", dict(out=od_cls[r], in_=ODS), bODS, reads=[bODS], writes=[bODd])

        def first_nat(i):
            first_any((16 if n_q == NQ else 0) + i)

        def first_back_nat(i):
            P.dma("sp", dict(out=ODT[i % 2], in_=od_dram[i * 128:(i + 1) * 128, :]), bXT[i % 2],
                  reads=[bODd], writes=[bXT[i % 2]] + [bVDS[b] for b in range(4) if b % 2 == i % 2])

        iters = []
        fill_qz(*groups_order[0])
        if len(groups_order) == 1:
            load_wo()
        if n_q == NQ:
            for r in range(NVB):
                load_vd(r)
            for r in range(16):
                cls = []
                for hg in range(2):
                    for idx, (kc, na) in enumerate(kchunks):
                        off = (c0 * 8 + 128 * kc) - q0 * 8
                        cls.append(dict(
                            kind="A", g=hg, idx=idx, n=len(kchunks), nk=na, mi=_IDXD[off], ob=2 + hg + 2 * (r % 2),
                            qz=(QZ2[r % 2], bQZ2[r % 2]), qb=r % 2,
                            kt=(lambda ch, kc=kc, na=na, r=r: KT[:, ch, sl(2048 * kc + r, na, 16)]),
                            ktb=bKT[:nctx],
                            vfn=(lambda h, kc=kc, na=na, r=r: VDSn[r % NVB][0:na, kc, h * 65:(h + 1) * 65]),
                            vb=[bVDS[r % NVB]]))
                cls[0]["first"] = (lambda r=r: first_dil(r))
                cls[-1]["last"] = (lambda n_now, r=r: last_dil(r))
                iters += cls
        for i in range(n_q):
            tq = q0 + i
            deltas = [dl for dl in range(-2, 3) if 0 <= tq + dl < nts]
            deltas_b = [dl for dl in (-1, 0, 1) if 0 <= tq + dl < nts]
            tl = []
            for hg in range(2):
                for idx, dl in enumerate(deltas):
                    ck = tq + dl - c0
                    assert 0 <= ck < nctx
                    tl.append(dict(kind="A", g=hg, idx=idx, n=len(deltas), nk=128, mi=_IDXA[dl],
                                   qz=(QZ2[i % 2], bQZ2[i % 2]), qb=i % 2,
                                   kt=(lambda ch, ck=ck: KT[:, ch, ck * 128:(ck + 1) * 128]), ktb=[bKT[ck]],
                                   vfn=(lambda h, ck=ck: V[:, ck, h, :]), vb=[bV[ck]]))
            for e_kv in range(2):
                for idx, dl in enumerate(deltas_b):
                    ck = tq + dl - c0
                    assert 0 <= ck < nctx
                    tl.append(dict(kind="B", g=e_kv, idx=idx, n=len(deltas_b), nk=128,
                                   mi=(_IDXB[dl] if dl != 0 else None),
                                   qz=(QZ2[i % 2], bQZ2[i % 2]), qb=i % 2,
                                   kt=(lambda ch, ck=ck: KT[:, ch, ck * 128:(ck + 1) * 128]), ktb=[bKT[ck]],
                                   vfn=(lambda h, ck=ck, e_kv=e_kv: V[:, ck, 8 + e_kv, :]), vb=[bV[ck]]))
            tl[0]["first"] = (lambda i=i: first_nat(i))
            if n_q == NQ:
                tl[0]["first_back"] = (lambda i=i: first_back_nat(i))
            tl[-1]["last"] = (lambda n_now, i=i: epilogue(i, n_now))
            iters += tl

        def front(n, it):
            g, nk = it["g"], it["nk"]
            qz, bqz = it["qz"]
            if it.get("first"):
                it["first"]()
            sb_ = SBANK[n % 3]
            pt, bpt = PT4[n % NPT], bPT4[n % NPT]
            if it["kind"] == "A":
                fns = []
                for j in range(2):
                    ch = g * 2 + j
                    fns.append(("matmul", dict(out=PB[sb_][0:nk, j * 256:(j + 1) * 256], lhsT=it["kt"](ch),
                                               rhs=qz[:, ch, :, :], start=True, stop=True)))
            else:
                fns = [("matmul", dict(out=PB[sb_][0:nk, :], lhsT=it["kt"](4), rhs=qz[:, 4:8, g, :],
                                       start=True, stop=True))]
            P.group("pe", fns, reads=list(it["ktb"]) + [bQZlo[it["qb"]], bQZhi[it["qb"]]], writes=[bPB[sb_]])
            P.op("act", "activation", dict(out=pt[0:nk, :], in_=PB[sb_][0:nk, :], func=AF.Exp, scale=0.125),
                 reads=[bPB[sb_]], writes=[bpt])
            if it["mi"] is not None:
                ptv = pt[0:nk, :].rearrange("p (h t) -> p h t", h=4)
                P.op("dve", "tensor_tensor", dict(out=ptv, in0=ptv,
                                                  in1=MASKS[0:nk, it["mi"], :].unsqueeze(1).to_broadcast([nk, 4, 128]),
                                                  op=ALU.mult),
                     reads=[bpt, bMASKS], writes=[bpt])

        def back(n, it):
            g, nk, idx = it["g"], it["nk"], it["idx"]
            if it.get("first_back"):
                it["first_back"]()
            pt, bpt = PT4[n % NPT], bPT4[n % NPT]
            ob = it.get("ob", (2 + g) if it["kind"] == "A" else (4 + g))
            fns = []
            for hl in range(4):
                fns.append(("matmul", dict(out=PB[ob][:, hl * 65:(hl + 1) * 65],
                                           lhsT=pt[0:nk, hl * 128:(hl + 1) * 128], rhs=it["vfn"](g * 4 + hl),
                                           start=(idx == 0 and hl == 0), stop=(idx == it["n"] - 1),
                                           skip_group_check=True)))
            P.group("pe", fns, reads=[bpt] + list(it["vb"]), writes=[bPB[ob]])
            if it.get("last"):
                it["last"](n + LA)

        deferred = []

        def epilogue(i, n_now):
            xb = bXT[i % 2]
            obv = O32[:, 512:1024].rearrange("p (m e d) -> p m e d", e=2, d=64)

            def e1():
                for hg in range(2):
                    odt = ODT[i % 2][:, hg * 260:(hg + 1) * 260]
                    if n_q == NQ:
                        P.op("dve", "tensor_tensor", dict(out=odt, in0=PB[2 + hg][:, 0:260], in1=odt, op=ALU.add),
                             reads=[bPB[2 + hg], xb], writes=[xb])
                    else:
                        P.op("dve", "tensor_copy", dict(out=odt, in_=PB[2 + hg][:, 0:260]), reads=[bPB[2 + hg]], writes=[xb])
                for e_kv in range(2):
                    ov = PB[4 + e_kv][:, 0:260].rearrange("p (h d) -> p h d", d=65)
                    esv = ESINK.rearrange("p (m e) -> p m e", e=2)[:, :, e_kv]
                    dsl = slice(8 + e_kv * 4, 8 + (e_kv + 1) * 4)
                    P.op("dve", "tensor_copy", dict(out=obv[:, :, e_kv, :], in_=ov[:, :, 0:64]),
                         reads=[bPB[4 + e_kv]], writes=[bO32])
                    P.op("dve", "tensor_tensor", dict(out=DEN[:, dsl], in0=ov[:, :, 64], in1=esv, op=ALU.add),
                         reads=[bPB[4 + e_kv], bESINK], writes=[bDEN])

            def e2():
                for hg in range(2):
                    ov = ODT[i % 2][:, hg * 260:(hg + 1) * 260].rearrange("p (h d) -> p h d", d=65)
                    P.op("dve", "reciprocal", dict(out=RDEN[:, hg * 4:(hg + 1) * 4], in_=ov[:, :, 64]),
                         reads=[xb], writes=[bRDEN])
                    P.op("dve", "tensor_tensor", dict(
                        out=hv(O32[:, hg * 256:(hg + 1) * 256]), in0=ov[:, :, 0:64],
                        in1=RDEN[:, hg * 4:(hg + 1) * 4].unsqueeze(2).to_broadcast([128, 4, 64]), op=ALU.mult),
                        reads=[xb, bRDEN], writes=[bO32])
                P.op("dve", "reciprocal", dict(out=RDEN[:, 8:16], in_=DEN[:, 8:16]), reads=[bDEN], writes=[bRDEN])
                for e_kv in range(2):
                    dsl = slice(8 + e_kv * 4, 8 + (e_kv + 1) * 4)
                    P.op("dve", "tensor_tensor", dict(
                        out=obv[:, :, e_kv, :], in0=obv[:, :, e_kv, :],
                        in1=RDEN[:, dsl].unsqueeze(2).to_broadcast([128, 4, 64]), op=ALU.mult),
                        reads=[bO32, bRDEN], writes=[bO32])
                if os.environ.get("DUMP_O32"):
                    P.dma("sp", dict(out=yseq[(q0 + i) * 128:(q0 + i + 1) * 128, :], in_=O32), bO32, reads=[bO32])

            def e3():
                for half in range(2):
                    P.op("act", "activation", dict(out=JUNK[:, half * 512:(half + 1) * 512],
                                                   in_=O32[:, half * 512:(half + 1) * 512],
                                                   func=AF.Square, accum_out=SS[:, half:half + 1]),
                         reads=[bO32], writes=[bJUNK, bSS])
                rstd_chain(SS[:, 0:2], LNV[:, 0:2], RSTD[:, 0:2], 512, bSS, bLNV, bRSTD)

            def e4():
                for half in range(2):
                    P.op("dve", "tensor_scalar", dict(out=XS[:, half * 512:(half + 1) * 512],
                                                      in0=O32[:, half * 512:(half + 1) * 512],
                                                      scalar1=RSTD[:, half:half + 1], scalar2=None, op0=ALU.mult),
                         reads=[bO32, bRSTD], writes=[bXS])
                transposes_xs(6)

            def e5():
                evac_T(6, MT[:, :, i * 128:(i + 1) * 128], GOUT, bGOUT, bM[i])

            while deferred:
                deferred.pop(0)[1]()
            e1()
            for dist, fn in zip((1, 7, 11, 14), [e2, e3, e4, e5]):
                deferred.append((n_now + dist, fn))

        n = 0
        while n < len(iters) + LA or deferred:
            if n < len(iters):
                front(n, iters[n])
            if 0 <= n - LA < len(iters):
                back(n - LA, iters[n - LA])
            due = [d for d in deferred if d[0] <= n]
            for d in due:
                deferred.remove(d)
                d[1]()
            n += 1

        P.barrier()
        if stop == "1b":
            break
        def load_w(fb):
            s = fb % 2
            extra = list(bWO) if s == 1 else []
            P.dma("pool", dict(out=WUP[s], in_=w_up[:, fb * 512:(fb + 1) * 512].rearrange("(k p) f -> p k f", p=128)),
                  bWUP[s], writes=[bWUP[s]] + extra)
            for hh in range(2):
                P.dma("pool", dict(out=WDN[s][:, :, hh * 512:(hh + 1) * 512],
                                   in_=w_dn[fb * 512:(fb + 1) * 512, hh * 512:(hh + 1) * 512].rearrange("(c p) n -> p c n", p=128)),
                      bWDN[s][hh], writes=[bWDN[s][hh]] + extra)

        load_w(0)
        slots2a = {}

        def st2a_mm(t):
            hb = (t % 3) * 2
            fns = []
            for k in range(8):
                for half in range(2):
                    fns.append(("matmul", dict(out=PB[hb + half][:, :], lhsT=MT[:, k, t * 128:(t + 1) * 128],
                                               rhs=WO[:, k, half * 512:(half + 1) * 512], start=(k == 0), stop=(k == 7))))
            P.group("pe", fns, reads=[bM[t]] + bWO, writes=[bPB[hb], bPB[hb + 1]])

        XS2 = [XS, JUNK]
        bXS2 = [bXS, bJUNK]
        junk2 = XNT.rearrange("p k t -> p (k t)")

        def st2a_adds(t):
            hb = (t % 3) * 2
            slot = slots2a[t]
            for half in range(2):
                P.op("dve", "tensor_tensor", dict(out=H[:, t, half * 512:(half + 1) * 512], in0=PB[hb + half][:, :],
                                                  in1=XT[slot][:, half * 512:(half + 1) * 512], op=ALU.add),
                     reads=[bPB[hb + half], bXT[slot]], writes=[bH[t]])

        def st2a_act(t):
            xs_, bxs_ = XS2[t % 2], bXS2[t % 2]
            P.op("act", "activation", dict(out=junk2, in_=H[:, t, :], func=AF.Square, accum_out=SS[:, 0:1]),
                 reads=[bH[t]], writes=[bXNT, bSS])
            rstd_chain(SS[:, 0:1], LNV[:, 0:1], RSTD[:, 0:1], D, bSS, bLNV, bRSTD)
            P.op("act", "activation", dict(out=xs_, in_=H[:, t, :], func=AF.Copy, scale=RSTD[:, 0:1]),
                 reads=[bH[t], bRSTD], writes=[bxs_])

        def st2a_T(t):
            xs_, bxs_ = XS2[t % 2], bXS2[t % 2]
            fns = [("transpose", dict(out=PBh[6][:, k * 128:(k + 1) * 128], in_=xs_[:, k * 128:(k + 1) * 128],
                                      identity=IDENT)) for k in range(8)]
            P.group("pe", fns, reads=[bxs_, bIDENT], writes=[bPB[6]])
            evac_T(6, MT[:, :, t * 128:(t + 1) * 128], GMLP, bGMLP, bM[t])

        def st2a_ldx(t):
            tq = q0 + t
            slots2a[t] = load_x(xseq[tq * 128:(tq + 1) * 128, :])

        st2a_ldx(0)
        st2a_ldx(1)
        st2a_mm(0)
        st2a_mm(1)
        st2a_adds(0)
        st2a_act(0)
        for t in range(NQ):
            if t + 2 < NQ:
                st2a_mm(t + 2)
            if t + 1 < NQ:
                st2a_adds(t + 1)
                st2a_act(t + 1)
            if t + 2 < NQ:
                st2a_ldx(t + 2)
            st2a_T(t)

        if stop == "2a":
            P.barrier()
        else:
            for b_ in bUT + bR32:
                for hh in range(2):
                    for sem_, v_ in bWO[hh].r.items():
                        b_.r[sem_] = max(b_.r.get(sem_, 0), v_)
        if stop == "2a":
            for t in range(NQ):
                tq = q0 + t
                P.dma("sp", dict(out=yseq[tq * 128:(tq + 1) * 128, :], in_=H[:, t, :]), bH[t], reads=[bH[t]])
            break
        NFB = DFF // 512
        pairs = [(fb, sg) for fb in range(NFB) for sg in range(4)]
        utb_of = {}

        def up_proj(fb, sg):
            s = fb % 2
            utb = state["u"] % 2
            state["u"] += 1
            utb_of[(fb, sg)] = utb
            for c in range(4):
                ub = state["y"] % 2
                state["y"] += 1
                fns = [("matmul", dict(out=PB[ub][:, :], lhsT=WUP[s][:, k, c * 128:(c + 1) * 128],
                                       rhs=MT[:, k, sg * 512:(sg + 1) * 512], start=(k == 0), stop=(k == 7)))
                       for k in range(8)]
                P.group("pe", fns, reads=[bWUP[s]] + bM[sg * 4:sg * 4 + 4], writes=[bPB[ub]])
                P.op("act", "activation", dict(out=R32[ub], in_=PB[ub][:, :], func=AF.Relu),
                     reads=[bPB[ub]], writes=[bR32[ub]])
                P.op("act", "activation", dict(out=UT[utb][:, c, :], in_=R32[ub], func=AF.Square),
                     reads=[bR32[ub]], writes=[bUT[utb]])

        def down_proj(fb, sg):
            s = fb % 2
            utb = utb_of[(fb, sg)]
            for tl in range(4):
                t = sg * 4 + tl
                for half in range(2):
                    yb = 2 + ((tl * 2 + half) % 4)
                    fns = [("matmul", dict(out=PB[yb][:, :], lhsT=UT[utb][:, c, tl * 128:(tl + 1) * 128],
                                           rhs=WDN[s][:, c, half * 512:(half + 1) * 512], start=(c == 0), stop=(c == 3)))
                           for c in range(4)]
                    P.group("pe", fns, reads=[bUT[utb], bWDN[s][half]], writes=[bPB[yb]])
                    P.op("dve", "tensor_tensor", dict(out=H[:, t, half * 512:(half + 1) * 512], in0=PB[yb][:, :],
                                                      in1=H[:, t, half * 512:(half + 1) * 512], op=ALU.add),
                         reads=[bPB[yb], bH[t]], writes=[bH[t]])
                if fb == NFB - 1:
                    tq = q0 + t
                    P.dma("sp", dict(out=yseq[tq * 128:(tq + 1) * 128, :], in_=H[:, t, :]), bYST, reads=[bH[t]])

        load_w(1)
        up_proj(*pairs[0])
        for j, (fb, sg) in enumerate(pairs):
            if j + 1 < len(pairs):
                up_proj(*pairs[j + 1])
            down_proj(fb, sg)
            if sg == 3 and fb + 2 < NFB:
                load_w(fb + 2)
        if stop is None and n1a is None and gidx_ + 1 < len(groups):
            nsrc, nseq, nq0, nc0, nnctx = groups[gidx_ + 1][:5]
            ncs = 8 if (len(groups[gidx_ + 1]) > 5 and groups[gidx_ + 1][5] == "load") else 0
            nx = xsrc[nsrc][nseq]
            preload[gidx_ + 1] = {}
            for c_ in (ncs, ncs + 1):
                ts_ = nc0 + c_
                preload[gidx_ + 1][c_] = load_x(nx[ts_ * 128:(ts_ + 1) * 128, :])
        P.barrier()

    npad = int(os.environ.get("PADPE", "0"))
    if npad:
        fns = [("matmul", dict(out=PB[7][:, 0:128], lhsT=IDENT, rhs=IDENT, start=True, stop=True)) for _ in range(npad)]
        P.group("pe", fns, reads=[bIDENT], writes=[bPB[7]])
    npad = int(os.environ.get("PADPOOL", "0"))
    for _ in range(npad):
        P.op("pool", "memset", dict(ap=EPST, constant=EPS), writes=[bEPS])
    P.finish()
    P.emit()
    return nc


def prep_weights(norm_attn, w_in, q_norm_a, k_norm_a, q_norm_b, k_norm_b, sink_b,
                 out_norm_a, out_norm_b, w_o, norm_mlp, w_up, w_down):
    f = np.float32
    w_in = np.asarray(w_in, f)[0]
    w_o = np.asarray(w_o, f)[0]
    ar = np.arange
    qb_cols = np.concatenate([1536 + h * 64 + ar(64) for h in PERM_HEADS])
    cols = np.concatenate([ar(0, 512), qb_cols, 512 + ar(512), 1024 + ar(512), 2048 + ar(128), 2176 + ar(128)])
    w_in_p = np.ascontiguousarray(w_in[:, cols])
    rows_b = np.concatenate([512 + h * 64 + ar(64) for h in PERM_HEADS])
    w_o_p = np.ascontiguousarray(w_o[np.concatenate([ar(512), rows_b])])
    gout = np.concatenate([np.asarray(out_norm_a, f)[0], np.asarray(out_norm_b, f)[0][rows_b - 512]])

    def chunked(g):
        return np.ascontiguousarray(np.asarray(g, f).reshape(8, 128).T)

    cos, sin = _rope_tables()
    masks = np.ascontiguousarray(np.concatenate(_MASKS, axis=1)).astype(ml_dtypes.bfloat16)
    return {
        "w_in": w_in_p, "w_o": w_o_p,
        "w_up": np.ascontiguousarray(np.asarray(w_up, f)[0]),
        "w_down": np.ascontiguousarray(np.asarray(w_down, f)[0]),
        "gin": chunked(np.asarray(norm_attn, f)[0]), "gmlp": chunked(np.asarray(norm_mlp, f)[0]),
        "gout": chunked(gout),
        "qkg": np.ascontiguousarray(np.concatenate([np.asarray(q_norm_a, f)[0], np.asarray(q_norm_b, f)[0],
                                                    np.asarray(k_norm_a, f)[0], np.asarray(k_norm_b, f)[0]])[None, :]),
        "sinkp": np.ascontiguousarray(np.asarray(sink_b, f)[0][PERM_HEADS][None, :]),
        "cost": cos, "sint": sin, "masks": masks,
        "ident": np.eye(128, dtype=np.float32).astype(ml_dtypes.bfloat16),
    }


FULL_GROUPS = [("p", 0, 0, 0, 16), ("p", 1, 0, 0, 16), ("s", 0, 0, 0, 24, "save"), ("s", 0, 16, 8, 24, "load")]

_NC_CACHE = {}


def kernel(x_prompt, x_sample, norm_attn, w_in, q_norm_a, k_norm_a, q_norm_b, k_norm_b,
           sink_b, out_norm_a, out_norm_b, w_o, norm_mlp, w_up, w_down):
    n = 8
    x_prompt = np.asarray(x_prompt, np.float32)
    x_sample = np.asarray(x_sample, np.float32)
    shared = prep_weights(norm_attn, w_in, q_norm_a, k_norm_a, q_norm_b, k_norm_b, sink_b,
                          out_norm_a, out_norm_b, w_o, norm_mlp, w_up, w_down)
    if "full" not in _NC_CACHE:
        _NC_CACHE["full"] = build_program(FULL_GROUPS, 2, 2048, 1, 4096)
    nc = _NC_CACHE["full"]
    in_maps = []
    for c in range(n):
        m = dict(shared)
        m["xp"] = np.ascontiguousarray(x_prompt[2 * c:2 * c + 2])
        m["xs"] = np.ascontiguousarray(x_sample[c:c + 1])
        in_maps.append(m)
    res = run_bass_kernel_spmd(nc, in_maps, core_ids=list(range(n)))
    yp = np.concatenate([r["yp"] for r in res.results], axis=0).astype(np.float32)
    ys = np.concatenate([r["ys"] for r in res.results], axis=0).astype(np.float32)
    return (yp, ys)
```

```python
import numpy as np
import ml_dtypes
import concourse.bass as bass
import concourse.mybir as mybir
from concourse.bass_utils import run_bass_kernel_spmd

F32 = mybir.dt.float32
BF16 = mybir.dt.bfloat16
AF = mybir.ActivationFunctionType
ALU = mybir.AluOpType
AX = mybir.AxisListType

D = 1024
HD = 64
IN_COLS = 2304
DFF = 4096
EPS = 1e-6
NQ = 16
PERM_HEADS = [0, 4, 1, 5, 2, 6, 3, 7]


class Buf:
    __slots__ = ("name", "w", "r", "dsem", "dcount", "excl")

    def __init__(self, name, excl=False):
        self.name = name
        self.excl = excl
        self.w = None
        self.r = {}
        self.dsem = None
        self.dcount = 0


class Eng:
    def __init__(self, key, sem):
        self.key = key
        self.sem = sem
        self.count = 0
        self.prog = []
        self.waited = {}


class Prog:
    def __init__(self, nc):
        self.nc = nc
        self.eng = {}
        for key in ("pe", "act", "dve", "pool", "sp"):
            self.eng[key] = Eng(key, nc.alloc_semaphore("s_" + key))
        self.bsem = nc.alloc_semaphore("s_bar")
        self.bcount = 0
        self.bufs = []
        self.dma_bufs = []
        self.fresh_dma_sems = False
        import os
        for j in range(int(os.environ.get("DUMMYSEM", "0"))):
            nc.alloc_semaphore("dummy%d" % j)

    def buf(self, name, excl=False):
        b = Buf(name, excl)
        self.bufs.append(b)
        return b

    def _wait(self, E, tick):
        sem, val = tick
        if E.key == "pe" and sem is E.sem:
            return
        if E.waited.get(sem, 0) >= val:
            return
        E.waited[sem] = val
        E.prog.append(("wait", sem, val))

    @staticmethod
    def _split(reads, writes):
        writes = list(writes)
        r2 = []
        for b in reads:
            if b.excl:
                if b not in writes:
                    writes.append(b)
            else:
                r2.append(b)
        return r2, writes

    def _deps(self, E, reads, writes):
        for b in reads:
            if b.w is not None:
                self._wait(E, b.w)
        for b in writes:
            if b.w is not None:
                self._wait(E, b.w)
            for s, v in b.r.items():
                self._wait(E, (s, v))

    def _commit(self, tick, reads, writes):
        for b in reads:
            b.r[tick[0]] = tick[1]
        for b in writes:
            b.w = tick
            b.r = {}

    def op(self, ek, name, kw, reads=(), writes=()):
        reads, writes = self._split(reads, writes)
        E = self.eng[ek]
        self._deps(E, reads, writes)
        E.count += 1
        tick = (E.sem, E.count)
        E.prog.append(("op", (name, kw), E.sem, 1))
        self._commit(tick, reads, writes)

    def group(self, ek, fns, reads=(), writes=()):
        reads, writes = self._split(reads, writes)
        E = self.eng[ek]
        self._deps(E, reads, writes)
        for fn in fns[:-1]:
            E.prog.append(("op", fn, None, 0))
        E.count += 1
        tick = (E.sem, E.count)
        E.prog.append(("op", fns[-1], E.sem, 1))
        self._commit(tick, reads, writes)

    def dma(self, ek, kw, primary, reads=(), writes=()):
        fn = ("dma_start", kw)
        E = self.eng[ek]
        self._deps(E, reads, writes)
        if primary.dsem is None:
            self.nsem = getattr(self, "nsem", 0) + 1
            primary.dsem = self.nc.alloc_semaphore("d%d_%s" % (self.nsem, primary.name))
            self.dma_bufs.append(primary)
        primary.dcount += 1
        tick = (primary.dsem, 16 * primary.dcount)
        E.prog.append(("op", fn, primary.dsem, 16))
        self._commit(tick, reads, writes)

    def barrier(self):
        sp = self.eng["sp"]
        for k in ("pe", "act", "dve", "pool"):
            E = self.eng[k]
            if E.count:
                self._wait(sp, (E.sem, E.count))
        for b in self.dma_bufs:
            self._wait(sp, (b.dsem, 16 * b.dcount))
        if self.fresh_dma_sems:
            for b in self.dma_bufs:
                b.dsem = None
                b.dcount = 0
            self.dma_bufs = []
        self.bcount += 1
        sp.prog.append(("inc", self.bsem, 1))
        for k in ("pe", "act", "dve", "pool"):
            self.eng[k].prog.append(("wait", self.bsem, self.bcount))
        for b in self.bufs:
            b.w = None
            b.r = {}

    def finish(self):
        sp = self.eng["sp"]
        for k in ("pe", "act", "dve", "pool"):
            E = self.eng[k]
            if E.count:
                self._wait(sp, (E.sem, E.count))
        for b in self.dma_bufs:
            self._wait(sp, (b.dsem, 16 * b.dcount))

    def emit(self):
        nc = self.nc

        def replay(E):
            def f(eng):
                for item in E.prog:
                    if item[0] == "wait":
                        eng.wait_ge(item[1], item[2])
                    elif item[0] == "inc":
                        eng.sem_inc(item[1], item[2])
                    elif item[0] == "clear":
                        eng.sem_clear(item[1])
                    else:
                        ins = getattr(eng, item[1][0])(**item[1][1])
                        if item[2] is not None:
                            ins.then_inc(item[2], item[3])
            return f

        with nc.Block() as block:
            block.sync(replay(self.eng["sp"]))
            block.gpsimd(replay(self.eng["pool"]))
            block.scalar(replay(self.eng["act"]))
            block.vector(replay(self.eng["dve"]))
            block.tensor(replay(self.eng["pe"]))


def _mask_tables():
    a = np.arange(128)[:, None]
    b = np.arange(128)[None, :]
    tabs = []
    idxA = {}
    for dl in range(-2, 3):
        diff = 128 * dl + a - b
        m = (np.abs(diff) <= 64).astype(np.float32)
        m += ((diff % 4 == 0) & (np.abs(diff) <= 256)).astype(np.float32)
        idxA[dl] = len(tabs)
        tabs.append(m)
    idxB = {}
    for dl in (-1, 1):
        diff = 128 * dl + a - b
        m = (np.abs(diff) <= 128).astype(np.float32)
        idxB[dl] = len(tabs)
        tabs.append(m)
    idxD = {}
    for off in (0, 128, -64, 64):
        m = (np.abs(off + a - b) <= 64).astype(np.float32)
        idxD[off] = len(tabs)
        tabs.append(m)
    return tabs, idxA, idxB, idxD


_MASKS, _IDXA, _IDXB, _IDXD = _mask_tables()
NM = len(_MASKS)


def _rope_tables():
    half = 8
    inv = 500000.0 ** (-(np.arange(half, dtype=np.float64) * 2.0 / 16.0))
    pos = np.arange(4096, dtype=np.float64)
    ang = pos[:, None] * inv[None, :]
    cos = np.cos(ang).astype(np.float32).reshape(32, 128, half).transpose(1, 0, 2).reshape(128, 32 * half)
    sin = np.sin(ang).astype(np.float32).reshape(32, 128, half).transpose(1, 0, 2).reshape(128, 32 * half)
    return np.ascontiguousarray(cos), np.ascontiguousarray(sin)


def build_program(groups, n_p, len_p, n_s, len_s, stop=None, n1a=None, n1b=None):
    nc = bass.Bass("TRN2", target_bir_lowering=False)
    P = Prog(nc)

    def din(name, shape, dt=F32):
        return nc.dram_tensor(name, list(shape), dt, kind="ExternalInput").ap()

    xsrc = {}
    ydst = {}
    if n_p:
        xsrc["p"] = din("xp", [n_p, len_p, D])
        ydst["p"] = nc.dram_tensor("yp", [n_p, len_p, D], F32, kind="ExternalOutput").ap()
    if n_s:
        xsrc["s"] = din("xs", [n_s, len_s, D])
        ydst["s"] = nc.dram_tensor("ys", [n_s, len_s, D], F32, kind="ExternalOutput").ap()
    w_in = din("w_in", [D, IN_COLS])
    w_o = din("w_o", [D, D])
    w_up = din("w_up", [D, DFF])
    w_dn = din("w_down", [DFF, D])
    d_gin = din("gin", [128, 8])
    d_gmlp = din("gmlp", [128, 8])
    d_gout = din("gout", [128, 8])
    d_qkg = din("qkg", [1, 256])
    d_sink = din("sinkp", [1, 8])
    d_cos = din("cost", [128, 256])
    d_sin = din("sint", [128, 256])
    d_masks = din("masks", [128, NM * 128], BF16)
    d_ident = din("ident", [128, 128], BF16)

    def sb(name, shape, dt):
        return nc.alloc_sbuf_tensor("sb_" + name, shape, dt)
    WIN_t = sb("WIN", [128, 8 * IN_COLS], BF16)
    WIN = WIN_t.ap().rearrange("p (k c) -> p k c", k=8)
    R1 = sb("R1", [128, 16384], BF16)
    QT = R1.ap().rearrange("p (m t) -> p m t", m=8)
    WUP = [R1.ap()[:, s * 8192:s * 8192 + 4096].rearrange("p (k f) -> p k f", k=8) for s in range(2)]
    WDN = [R1.ap()[:, s * 8192 + 4096:(s + 1) * 8192].rearrange("p (c n) -> p c n", c=4) for s in range(2)]
    R2 = sb("R2", [128, 32768], BF16)
    KT = R2.ap()[:, 0:15360].rearrange("p (m t) -> p m t", m=5)
    V = R2.ap()[:, 15360:15360 + 15600].rearrange("p (c h d) -> p c h d", c=24, h=10)
    H = R2.ap().bitcast(F32).rearrange("p (t f) -> p t f", t=16)
    R3 = sb("R3", [128, 16384], BF16)
    MT = R3.ap().rearrange("p (m t) -> p m t", m=8)
    r3f = R3.ap().bitcast(F32)
    STG = r3f[:, 0:1664]
    SQ = r3f[:, 1664:3328]
    ROPEIN = r3f[:, 3328:3328 + 416]
    R4 = sb("R4", [128, 8192], BF16)
    r4f = R4.ap().bitcast(F32)
    QN32 = r4f[:, 0:1664]
    QKB = R4.ap()[:, 3328:3328 + 1664]
    PTB = [R4.ap()[:, 4992 + s * 512:4992 + (s + 1) * 512] for s in range(2)]
    O32 = r4f[:, 3008:3008 + 1024]
    PT4 = [PTB[0], PTB[1], R4.ap()[:, 0:512], R4.ap()[:, 512:1024], R4.ap()[:, 4112:4624]]
    QZ2 = [None, R4.ap()[:, 1024:3072].rearrange("p (m v t) -> p m v t", m=8, v=2)]
    WO = R1.ap()[:, 8192:16384].rearrange("p (k n) -> p k n", k=8)
    UT = [R4.ap()[:, s * 2048:(s + 1) * 2048].rearrange("p (c t) -> p c t", c=4) for s in range(2)]
    R32 = [r4f[:, 2048 + s * 512:2048 + (s + 1) * 512] for s in range(2)]

    QZ = sb("qz", [128, 8, 2, 128], BF16).ap()
    QZ2[0] = QZ
    XT = [sb("xt%d" % s, [128, D], F32).ap() for s in range(2)]
    XS = sb("xsb", [128, D], BF16).ap()
    JUNK = sb("junk", [128, D], BF16).ap()
    XNT = sb("xnT", [128, 8, 128], BF16).ap()
    xnt_flat = XNT.rearrange("p k t -> p (k t)")
    PT4 += [xnt_flat[:, 0:512], xnt_flat[:, 512:1024]]
    MASKS = sb("masks", [128, NM, 128], BF16).ap()
    COS = sb("cos", [128, 32, 8], F32).ap()
    SIN = sb("sin", [128, 32, 8], F32).ap()
    IDENT = sb("ident", [128, 128], BF16).ap()
    GIN = sb("gin", [128, 8], F32).ap()
    GMLP = sb("gmlp", [128, 8], F32).ap()
    GOUT = sb("gout", [128, 8], F32).ap()
    QKG = sb("qkg", [128, 4, 64], F32).ap()
    SINK = sb("sink", [128, 8], F32).ap()
    ESINK = sb("esink", [128, 8], F32).ap()
    EPST = sb("epst", [128, 1], F32).ap()
    SS = sb("ss", [128, 2], F32).ap()
    LNV = sb("lnv", [128, 2], F32).ap()
    RSTD = sb("rstd", [128, 2], F32).ap()
    SSQ = sb("ssq", [128, 26], F32).ap()
    LNQ = sb("lnq", [128, 26], F32).ap()
    RQ = sb("rq", [128, 26], F32).ap()
    RT = [r3f[:, 3744 + j * 208:3744 + (j + 1) * 208].rearrange("p (h d) -> p h d", d=8) for j in range(4)]
    ODS = r4f[:, 1536:1536 + 520]
    VDS = [XT[s_].bitcast(BF16)[:, 0:1040].rearrange("p (c f) -> p c f", c=2) for s_ in range(2)]
    ODT = [XT[s_][:, 0:520] for s_ in range(2)]
    vd_dram = nc.dram_tensor("vd_scratch", [24 * 128, 520], BF16, kind="Internal").ap()
    od_dram = nc.dram_tensor("od_scratch", [NQ * 128, 520], F32, kind="Internal").ap()
    kt_scr = nc.dram_tensor("kt_scratch", [128, 5 * 1024], BF16, kind="Internal").ap()
    v_scr = nc.dram_tensor("v_scratch", [128, 8 * 650], BF16, kind="Internal").ap()
    DEN = sb("den", [128, 16], F32).ap()
    RDEN = sb("rden", [128, 16], F32).ap()

    PB = [nc.alloc_psum_tensor("pb%d" % j, [128, 512], F32).ap() for j in range(8)]
    PBh = [p.bitcast(BF16) for p in PB]

    WIN_COLS = [(0, 512), (512, 1024), (1024, 1536), (1536, 2048), (2048, 2304)]
    bWIN = [P.buf("win%d" % k) for k in range(5)]
    bXT = [P.buf("xt%d" % s) for s in range(2)]
    bXS, bJUNK, bXNT = P.buf("xs"), P.buf("junk"), P.buf("xnt")
    bMASKS, bCOS, bSIN, bIDENT = P.buf("masks"), P.buf("cos"), P.buf("sin"), P.buf("ident")
    bGIN, bGMLP, bGOUT, bQKG = P.buf("gin"), P.buf("gmlp"), P.buf("gout"), P.buf("qkg")
    bSINK, bESINK, bEPS = P.buf("sink"), P.buf("esink"), P.buf("eps")
    bSS, bLNV, bRSTD = P.buf("ss"), P.buf("lnv"), P.buf("rstd")
    bSSQ, bLNQ, bRQ = P.buf("ssq"), P.buf("lnq"), P.buf("rq")
    bRT = [P.buf("rt%d" % j) for j in range(4)]
    bDEN, bRDEN = P.buf("den"), P.buf("rden")
    bPB = [P.buf("pb%d" % j, excl=True) for j in range(8)]
    bQN32, bQKB, bO32 = P.buf("qn32"), P.buf("qkb"), P.buf("o32")
    bSTG, bSQ, bRIN = P.buf("stg"), P.buf("sq"), P.buf("rin")
    bXNTb = P.buf("xntb")
    bPT = [P.buf("pt%d" % s) for s in range(2)]
    bKT = [P.buf("kt%d" % c) for c in range(24)]
    bV = [P.buf("v%d" % c) for c in range(24)]
    bQT = [P.buf("qt%d" % i) for i in range(NQ)]
    bM = [P.buf("m%d" % i) for i in range(NQ)]
    bH = [P.buf("h%d" % i) for i in range(NQ)]
    bWO = [P.buf("wo%d" % hh) for hh in range(2)]
    bQZ = P.buf("qz")
    bQZ2 = [bQZ, P.buf("qz1")]
    bQZlo = [P.buf("qzlo0"), P.buf("qzlo1")]
    bQZhi = [P.buf("qzhi0"), P.buf("qzhi1")]
    bPT4 = [bPT[0], bPT[1], P.buf("pt2"), P.buf("pt3"), P.buf("pt4"), P.buf("pt5"), P.buf("pt6")]
    NPT = len(bPT4)
    bYST = P.buf("yst")
    bVDd, bVDW, bODd, bODS = P.buf("vdd"), P.buf("vdw"), P.buf("odd"), P.buf("ods")
    bVDS = [P.buf("vds%d" % b) for b in range(4)]
    bKTS, bVSS, bKTL = P.buf("kts"), P.buf("vss"), P.buf("ktl")
    bVL = [P.buf("vl0"), P.buf("vl1")]
    bWUP = [P.buf("wup%d" % s) for s in range(2)]
    bWDN = [[P.buf("wdn%d_%d" % (s, hh)) for hh in range(2)] for s in range(2)]
    bUT = [P.buf("ut%d" % s) for s in range(2)]
    bR32 = [P.buf("r32%d" % s) for s in range(2)]

    P.dma("sp", dict(out=IDENT, in_=d_ident), bIDENT, writes=[bIDENT])
    P.dma("sp", dict(out=MASKS, in_=d_masks.rearrange("p (m t) -> p m t", m=NM)), bMASKS, writes=[bMASKS])
    P.dma("sp", dict(out=COS, in_=d_cos.rearrange("p (t f) -> p t f", t=32)), bCOS, writes=[bCOS])
    P.dma("sp", dict(out=SIN, in_=d_sin.rearrange("p (t f) -> p t f", t=32)), bSIN, writes=[bSIN])
    P.dma("sp", dict(out=GIN, in_=d_gin), bGIN, writes=[bGIN])
    P.dma("sp", dict(out=GMLP, in_=d_gmlp), bGMLP, writes=[bGMLP])
    P.dma("sp", dict(out=GOUT, in_=d_gout), bGOUT, writes=[bGOUT])
    P.dma("sp", dict(out=QKG.rearrange("p a b -> p (a b)"), in_=d_qkg.partition_broadcast(128)), bQKG, writes=[bQKG])
    P.dma("sp", dict(out=SINK, in_=d_sink.partition_broadcast(128)), bSINK, writes=[bSINK])
    P.op("pool", "memset", dict(ap=EPST, constant=EPS), writes=[bEPS])
    P.op("pool", "memset", dict(ap=QZ, constant=0.0), writes=[bQZlo[0], bQZhi[0]])
    for j, (a0, a1) in enumerate(WIN_COLS):
        P.dma("pool", dict(out=WIN[:, :, a0:a1], in_=w_in[:, a0:a1].rearrange("(k p) f -> p k f", p=128)),
              bWIN[j], writes=[bWIN[j]])
    P.op("act", "activation", dict(out=ESINK, in_=SINK, func=AF.Exp), reads=[bSINK], writes=[bESINK])

    def rstd_chain(ss_ap, ln_ap, r_ap, n, bss, bln, br):
        P.op("act", "activation", dict(out=ln_ap, in_=ss_ap, func=AF.Ln, scale=1.0 / n, bias=EPST),
             reads=[bss, bEPS], writes=[bln])
        P.op("act", "activation", dict(out=r_ap, in_=ln_ap, func=AF.Exp, scale=-0.5), reads=[bln], writes=[br])

    import os
    CUT = int(os.environ.get("CUT", "99"))
    state = {"x": 0, "s": 0, "u": 0, "y": 0, "mm": 0}

    def load_x(xrows):
        slot = state["x"] % 2
        state["x"] += 1
        P.dma("sp", dict(out=XT[slot], in_=xrows), bXT[slot], writes=[bXT[slot]])
        return slot

    def transposes_xs(ps_idx):
        fns = [("transpose", dict(out=PBh[ps_idx][:, k * 128:(k + 1) * 128], in_=XS[:, k * 128:(k + 1) * 128],
                                  identity=IDENT)) for k in range(8)]
        P.group("pe", fns, reads=[bXS, bIDENT], writes=[bPB[ps_idx]])

    def norm_transpose(src_ap, bsrc, ps_idx):
        P.op("act", "activation", dict(out=JUNK, in_=src_ap, func=AF.Square, accum_out=SS[:, 0:1]),
             reads=[bsrc], writes=[bJUNK, bSS])
        rstd_chain(SS[:, 0:1], LNV[:, 0:1], RSTD[:, 0:1], D, bSS, bLNV, bRSTD)
        P.op("act", "activation", dict(out=XS, in_=src_ap, func=AF.Copy, scale=RSTD[:, 0:1]),
             reads=[bsrc, bRSTD], writes=[bXS])
        transposes_xs(ps_idx)

    def evac_T(ps_idx, out_ap, gain_ap, bgain, bout):
        P.op("dve", "tensor_tensor", dict(out=out_ap, in0=PBh[ps_idx].rearrange("p (k t) -> p k t", k=8),
                                          in1=gain_ap.unsqueeze(2).to_broadcast([128, 8, 128]), op=ALU.mult),
             reads=[bPB[ps_idx], bgain], writes=[bout])

    def hv(ap):
        return ap.rearrange("p (h d) -> p h d", d=64)

    preload = {}
    for gidx_, gspec in enumerate(groups):
        (src, seq, q0, c0, nctx) = gspec[:5]
        halo_mode = gspec[5] if len(gspec) > 5 else None
        if stop == "setup":
            break
        xseq = xsrc[src][seq]
        yseq = ydst[src][seq]
        L = xseq.shape[0]
        nts = L // 128

        P.op("pool", "memset", dict(ap=V[:, :, :, 64:65], constant=1.0), writes=bV[:nctx])

        n_c = nctx if n1a is None else n1a
        c_start = 0
        if halo_mode == "load":
            c_start = 8
            P.dma("sp", dict(out=KT[:, :, 0:1024], in_=kt_scr.rearrange("p (m t) -> p m t", m=5)), bKTL,
                  reads=[bKTS], writes=bKT[0:8] + [bKTL])
            for half_ in range(2):
                P.dma("sp", dict(out=V[:, half_ * 4:(half_ + 1) * 4, :, :].rearrange("p c h d -> p (c h d)"),
                                 in_=v_scr[:, half_ * 2600:(half_ + 1) * 2600]), bVL[half_],
                      reads=[bVSS], writes=bV[half_ * 4:(half_ + 1) * 4] + [bVL[half_]])
            for c_ in range(8):
                P.dma("pool", dict(out=vd_dram[c_ * 128:(c_ + 1) * 128, :],
                                   in_=V[:, c_, 0:8, :].rearrange("p h d -> p (h d)")),
                      bVDW, reads=[bV[c_]], writes=[bVDd])
        BANKS = [(0, 0, 512), (1, 512, 1024), (2, 1024, 1536), (3, 1536, 2048), (4, 2048, 2304)]
        SEGS = [(0, 0, 8, 0), (1, 8, 8, 1), (2, 16, 8, 2), (4, 24, 2, 3)]
        stv = hv(STG)
        kb = hv(QKB)
        rin = ROPEIN.rearrange("p (h d) -> p h d", d=16)

        def info(c):
            ts = c0 + c
            own = q0 <= ts < q0 + NQ
            return dict(ts=ts, own=own, qi=ts - q0, banks=BANKS if own else BANKS[2:],
                        segs=SEGS if own else SEGS[2:], h_lo=0 if own else 16)

        slots1a = {}

        def st_ldx(c):
            nf = info(c)
            if c in preload.get(gidx_, {}):
                slots1a[c] = preload[gidx_][c]
                return
            slots1a[c] = load_x(xseq[nf["ts"] * 128:(nf["ts"] + 1) * 128, :])

        def st_A(c):
            nf = info(c)
            slot = slots1a[c]
            P.op("act", "activation", dict(out=JUNK, in_=XT[slot], func=AF.Square, accum_out=SS[:, 0:1]),
                 reads=[bXT[slot]], writes=[bJUNK, bSS])
            rstd_chain(SS[:, 0:1], LNV[:, 0:1], RSTD[:, 0:1], D, bSS, bLNV, bRSTD)
            P.op("act", "activation", dict(out=XS, in_=XT[slot], func=AF.Copy, scale=RSTD[:, 0:1]),
                 reads=[bXT[slot], bRSTD], writes=[bXS])

        XNT2 = [XNT, R3.ap()[:, 12288:13312].rearrange("p (k t) -> p k t", k=8)]
        bXNT2 = [bXNT, bXNTb]

        def st_MM(c, which):
            nf = info(c)
            xnt, bxnt = XNT2[c % 2], bXNT2[c % 2]
            part = [b for b in nf["banks"] if (b[0] < 2) == (which == 0)]
            if not part:
                return
            fns = []
            for k in range(8):
                for (bk, a0, a1) in part:
                    fns.append(("matmul", dict(out=PB[bk][:, 0:a1 - a0], lhsT=xnt[:, k, :], rhs=WIN[:, k, a0:a1],
                                               start=(k == 0), stop=(k == 7))))
            P.group("pe", fns, reads=[bxnt] + [bWIN[bk] for (bk, _, _) in part],
                    writes=[bPB[bk] for (bk, _, _) in part])

        def st_C(c, part):
            nf = info(c)
            for (bk, h0, nh, gt) in [sg_ for sg_ in nf["segs"] if (sg_[0] < 2) == (part == 0)]:
                P.op("act", "activation", dict(out=SQ[:, h0 * 64:(h0 + nh) * 64], in_=PB[bk][:, 0:nh * 64], func=AF.Square),
                     reads=[bPB[bk]], writes=[bSQ])
                P.op("dve", "tensor_tensor", dict(
                    out=hv(STG[:, h0 * 64:(h0 + nh) * 64]), in0=hv(PB[bk][:, 0:nh * 64]),
                    in1=QKG[:, gt, :].unsqueeze(1).to_broadcast([128, nh, 64]), op=ALU.mult),
                    reads=[bPB[bk], bQKG], writes=[bSTG])
            if part == 0:
                return
            P.op("act", "activation", dict(out=V[:, c, 0:8, 0:64], in_=hv(PB[3][:, :]), func=AF.Copy),
                 reads=[bPB[3]], writes=[bV[c]])
            P.op("act", "activation", dict(out=V[:, c, 8:10, 0:64], in_=hv(PB[4][:, 128:256]), func=AF.Copy),
                 reads=[bPB[4]], writes=[bV[c]])
            P.dma("pool", dict(out=vd_dram[c * 128:(c + 1) * 128, :], in_=V[:, c, 0:8, :].rearrange("p h d -> p (h d)")),
                  bVDW, reads=[bV[c]], writes=[bVDd])

        def st_D(c):
            nf = info(c)
            h_lo, h_hi = nf["h_lo"], 26
            nh_all = h_hi - h_lo
            ts = nf["ts"]
            P.op("dve", "tensor_reduce", dict(out=SSQ[:, h_lo:h_hi], in_=hv(SQ[:, h_lo * 64:h_hi * 64]),
                                              axis=AX.X, op=ALU.add), reads=[bSQ], writes=[bSSQ])
            rstd_chain(SSQ[:, h_lo:h_hi], LNQ[:, h_lo:h_hi], RQ[:, h_lo:h_hi], HD, bSSQ, bLNQ, bRQ)
            P.op("dve", "tensor_tensor", dict(
                out=kb[:, h_lo:h_hi, 16:64], in0=stv[:, h_lo:h_hi, 16:64],
                in1=RQ[:, h_lo:h_hi].unsqueeze(2).to_broadcast([128, nh_all, 48]), op=ALU.mult),
                reads=[bSTG, bRQ], writes=[bQKB])
            P.op("dve", "tensor_tensor", dict(
                out=rin[:, h_lo:h_hi, :], in0=stv[:, h_lo:h_hi, 0:16],
                in1=RQ[:, h_lo:h_hi].unsqueeze(2).to_broadcast([128, nh_all, 16]), op=ALU.mult),
                reads=[bSTG, bRQ], writes=[bRIN])
            x1 = rin[:, h_lo:h_hi, 0:8]
            x2 = rin[:, h_lo:h_hi, 8:16]
            cosb = COS[:, ts, :].unsqueeze(1).to_broadcast([128, nh_all, 8])
            sinb = SIN[:, ts, :].unsqueeze(1).to_broadcast([128, nh_all, 8])
            for j, (xa, tb, bt) in enumerate([(x1, cosb, bCOS), (x2, sinb, bSIN), (x2, cosb, bCOS), (x1, sinb, bSIN)]):
                P.op("pool", "tensor_tensor", dict(out=RT[j][:, h_lo:h_hi, :], in0=xa, in1=tb, op=ALU.mult),
                     reads=[bRIN, bt], writes=[bRT[j]])
            P.op("dve", "tensor_tensor", dict(out=kb[:, h_lo:h_hi, 0:8], in0=RT[0][:, h_lo:h_hi, :],
                                              in1=RT[1][:, h_lo:h_hi, :], op=ALU.subtract),
                 reads=[bRT[0], bRT[1]], writes=[bQKB])
            P.op("dve", "tensor_tensor", dict(out=kb[:, h_lo:h_hi, 8:16], in0=RT[2][:, h_lo:h_hi, :],
                                              in1=RT[3][:, h_lo:h_hi, :], op=ALU.add),
                 reads=[bRT[2], bRT[3]], writes=[bQKB])

        def st_T2(c):
            nf = info(c)
            if nf["own"]:
                fns = [("transpose", dict(out=PBh[6][:, m * 128:(m + 1) * 128], in_=QKB[:, m * 128:(m + 1) * 128],
                                          identity=IDENT)) for m in range(8)]
                P.group("pe", fns, reads=[bQKB, bIDENT], writes=[bPB[6]])
            fns = [("transpose", dict(out=PBh[7][:, m * 128:(m + 1) * 128],
                                      in_=QKB[:, 1024 + m * 128:1024 + (m + 1) * 128], identity=IDENT)) for m in range(5)]
            P.group("pe", fns, reads=[bQKB, bIDENT], writes=[bPB[7]])

        def st_E(c):
            nf = info(c)
            if nf["own"]:
                qi = nf["qi"]
                P.op("dve", "tensor_copy", dict(out=QT[:, :, qi * 128:(qi + 1) * 128],
                                                in_=PBh[6].rearrange("p (m t) -> p m t", m=8)),
                     reads=[bPB[6]], writes=[bQT[qi]])
            P.op("dve", "tensor_copy", dict(out=KT[:, :, c * 128:(c + 1) * 128],
                                            in_=PBh[7][:, 0:640].rearrange("p (m t) -> p m t", m=5)),
                 reads=[bPB[7]], writes=[bKT[c]])

        st_ldx(c_start)
        if n_c > c_start + 1:
            st_ldx(c_start + 1)
        st_A(c_start)
        for c in range(c_start, n_c + 2):
            if c_start <= c - 2 < n_c:
                st_D(c - 2)
            if c_start <= c - 1 < n_c:
                st_MM(c - 1, 0)
            if c < n_c:
                transposes_xs(5)
            if c_start <= c - 1 < n_c:
                st_C(c - 1, 0)
            if c < n_c:
                evac_T(5, XNT2[c % 2], GIN, bGIN, bXNT2[c % 2])
            if c_start <= c - 1 < n_c:
                st_MM(c - 1, 1)
            if c_start <= c - 2 < n_c:
                st_T2(c - 2)
            if c_start <= c - 1 < n_c:
                st_C(c - 1, 1)
            if c_start <= c - 2 < n_c:
                st_E(c - 2)
            if c + 2 < n_c:
                st_ldx(c + 2)
            if c + 1 < n_c:
                st_A(c + 1)

        if halo_mode == "save":
            P.dma("pool", dict(out=kt_scr.rearrange("p (m t) -> p m t", m=5), in_=KT[:, :, 1024:2048]), bKTS,
                  reads=bKT[8:16], writes=[bKTS])
            for half_ in range(2):
                P.dma("pool", dict(out=v_scr[:, half_ * 2600:(half_ + 1) * 2600],
                                   in_=V[:, 8 + half_ * 4:8 + (half_ + 1) * 4, :, :].rearrange("p c h d -> p (c h d)")),
                      bVSS, reads=bV[8 + half_ * 4:8 + (half_ + 1) * 4], writes=[bVSS])

        if stop == "1a":
            break
        def alias_after(dst_bufs, src_bufs):
            for d_ in dst_bufs:
                for s_ in src_bufs:
                    for sem_, v_ in s_.r.items():
                        d_.r[sem_] = max(d_.r.get(sem_, 0), v_)
                    if s_.w is not None:
                        d_.r[s_.w[0]] = max(d_.r.get(s_.w[0], 0), s_.w[1])

        alias_after([bODS, bPT4[4]], [bQKB])
        alias_after([bPT4[5], bPT4[6]], [bXNT])
        alias_after(bM, [bSTG, bSQ, bRIN, bXNTb] + bRT)
        P.op("pool", "memset", dict(ap=QZ2[1], constant=0.0), writes=[bQZlo[1], bQZhi[1]])
        n_q = NQ if n1b is None else n1b
        LA = 5
        SBANK = [0, 1, 7]

        def sl(st_, n_, step):
            return slice(st_, st_ + step * (n_ - 1) + 1, step)

        n_kpos = nctx * 8
        kchunks = [(kc, min(128, n_kpos - 128 * kc)) for kc in range((n_kpos + 127) // 128)]
        vd_cls = vd_dram.rearrange("(a s) f -> s a f", s=16)
        od_cls = od_dram.rearrange("(j s) f -> s j f", s=16)

        NVB = 4 if len(kchunks) == 1 else 2
        nkc = len(kchunks)
        VDSn = [XT[b % 2].bitcast(BF16)[:, (b // 2) * 520 * nkc:(b // 2 + 1) * 520 * nkc].rearrange("p (c f) -> p c f", c=nkc)
                for b in range(NVB)]

        def load_vd(r):
            b = r % NVB
            for kc, na in kchunks:
                P.dma("sp", dict(out=VDSn[b][0:na, kc, :], in_=vd_cls[r][128 * kc:128 * kc + na]),
                      bVDS[b], reads=[bVDd], writes=[bVDS[b], bXT[b % 2]])

        def fill_qz(kind, j):
            b = j % 2
            qz = QZ2[b]
            if kind == "D":
                P.op("act", "activation", dict(out=qz[0:64, 0:4, 0, :], in_=QT[0:64, 0:4, sl(j, 128, 16)], func=AF.Copy),
                     reads=bQT, writes=[bQZlo[b]])
                P.op("dve", "tensor_copy", dict(out=qz[64:128, 0:4, 1, :], in_=QT[64:128, 0:4, sl(j, 128, 16)]),
                     reads=bQT, writes=[bQZhi[b]])
            else:
                P.op("act", "activation", dict(out=qz[0:64, :, 0, :], in_=QT[0:64, :, j * 128:(j + 1) * 128], func=AF.Copy),
                     reads=[bQT[j]], writes=[bQZlo[b]])
                P.op("dve", "tensor_copy", dict(out=qz[64:128, :, 1, :], in_=QT[64:128, :, j * 128:(j + 1) * 128]),
                     reads=[bQT[j]], writes=[bQZhi[b]])

        groups_order = ([("D", r) for r in range(16)] if n_q == NQ else []) + [("N", i) for i in range(n_q)]

        def load_wo():
            for hh in range(2):
                P.dma("pool", dict(out=WO[:, :, hh * 512:(hh + 1) * 512],
                                   in_=w_o[:, hh * 512:(hh + 1) * 512].rearrange("(k p) n -> p k n", p=128)),
                      bWO[hh], writes=[bWO[hh]] + bQT + [bWUP[1]] + bWDN[1])

        def first_any(gidx):
            if gidx + 1 < len(groups_order):
                fill_qz(*groups_order[gidx + 1])
                if gidx + 2 == len(groups_order):
                    load_wo()

        def first_dil(r):
            first_any(r)

        def last_dil(r):
            if r + NVB < 16:
                load_vd(r + NVB)
            b0 = 2 + 2 * (r % 2)
            P.op("dve", "tensor_copy", dict(out=ODS[:, 0:260], in_=PB[b0][:, 0:260]),
                 reads=[bPB[b0]], writes=[bODS])
            P.op("dve", "tensor_copy", dict(out=ODS[:, 260:520], in_=PB[b0 + 1][:, 0:260]),
                 reads=[bPB[b0 + 1]], writes=[bODS])
            P.dma("sp", dict(out=od_cls[r], in_=ODS), bODS, reads=[bODS], writes=[bODd])

        def first_nat(i):
            first_any((16 if n_q == NQ else 0) + i)

        def first_back_nat(i):
            P.dma("sp", dict(out=ODT[i % 2], in_=od_dram[i * 128:(i + 1) * 128, :]), bXT[i % 2],
                  reads=[bODd], writes=[bXT[i % 2]] + [bVDS[b] for b in range(4) if b % 2 == i % 2])

        iters = []
        fill_qz(*groups_order[0])
        if len(groups_order) == 1:
            load_wo()
        if n_q == NQ:
            for r in range(NVB):
                load_vd(r)
            for r in range(16):
                cls = []
                for hg in range(2):
                    for idx, (kc, na) in enumerate(kchunks):
                        off = (c0 * 8 + 128 * kc) - q0 * 8
                        cls.append(dict(
                            kind="A", g=hg, idx=idx, n=len(kchunks), nk=na, mi=_IDXD[off], ob=2 + hg + 2 * (r % 2),
                            qz=(QZ2[r % 2], bQZ2[r % 2]), qb=r % 2,
                            kt=(lambda ch, kc=kc, na=na, r=r: KT[:, ch, sl(2048 * kc + r, na, 16)]),
                            ktb=bKT[:nctx],
                            vfn=(lambda h, kc=kc, na=na, r=r: VDSn[r % NVB][0:na, kc, h * 65:(h + 1) * 65]),
                            vb=[bVDS[r % NVB]]))
                cls[0]["first"] = (lambda r=r: first_dil(r))
                cls[-1]["last"] = (lambda n_now, r=r: last_dil(r))
                iters += cls
        for i in range(n_q):
            tq = q0 + i
            deltas = [dl for dl in range(-2, 3) if 0 <= tq + dl < nts]
            deltas_b = [dl for dl in (-1, 0, 1) if 0 <= tq + dl < nts]
            tl = []
            for hg in range(2):
                for idx, dl in enumerate(deltas):
                    ck = tq + dl - c0
                    assert 0 <= ck < nctx
                    tl.append(dict(kind="A", g=hg, idx=idx, n=len(deltas), nk=128, mi=_IDXA[dl],
                                   qz=(QZ2[i % 2], bQZ2[i % 2]), qb=i % 2,
                                   kt=(lambda ch, ck=ck: KT[:, ch, ck * 128:(ck + 1) * 128]), ktb=[bKT[ck]],
                                   vfn=(lambda h, ck=ck: V[:, ck, h, :]), vb=[bV[ck]]))
            for e_kv in range(2):
                for idx, dl in enumerate(deltas_b):
                    ck = tq + dl - c0
                    assert 0 <= ck < nctx
                    tl.append(dict(kind="B", g=e_kv, idx=idx, n=len(deltas_b), nk=128,
                                   mi=(_IDXB[dl] if dl != 0 else None),
                                   qz=(QZ2[i % 2], bQZ2[i % 2]), qb=i % 2,
                                   kt=(lambda ch, ck=ck: KT[:, ch, ck * 128:(ck + 1) * 128]), ktb=[bKT[ck]],
                                   vfn=(lambda h, ck=ck, e_kv=e_kv: V[:, ck, 8 + e_kv, :]), vb=[bV[ck]]))
            tl[0]["first"] = (lambda i=i: first_nat(i))
            if n_q == NQ:
                tl[0]["first_back"] = (lambda i=i: first_back_nat(i))
            tl[-1]["last"] = (lambda n_now, i=i: epilogue(i, n_now))
            iters += tl

        def front(n, it):
            g, nk = it["g"], it["nk"]
            qz, bqz = it["qz"]
            if it.get("first"):
                it["first"]()
            sb_ = SBANK[n % 3]
            pt, bpt = PT4[n % NPT], bPT4[n % NPT]
            if it["kind"] == "A":
                fns = []
                for j in range(2):
                    ch = g * 2 + j
                    fns.append(("matmul", dict(out=PB[sb_][0:nk, j * 256:(j + 1) * 256], lhsT=it["kt"](ch),
                                               rhs=qz[:, ch, :, :], start=True, stop=True)))
            else:
                fns = [("matmul", dict(out=PB[sb_][0:nk, :], lhsT=it["kt"](4), rhs=qz[:, 4:8, g, :],
                                       start=True, stop=True))]
            P.group("pe", fns, reads=list(it["ktb"]) + [bQZlo[it["qb"]], bQZhi[it["qb"]]], writes=[bPB[sb_]])
            P.op("act", "activation", dict(out=pt[0:nk, :], in_=PB[sb_][0:nk, :], func=AF.Exp, scale=0.125),
                 reads=[bPB[sb_]], writes=[bpt])
            if it["mi"] is not None:
                ptv = pt[0:nk, :].rearrange("p (h t) -> p h t", h=4)
                P.op("dve", "tensor_tensor", dict(out=ptv, in0=ptv,
                                                  in1=MASKS[0:nk, it["mi"], :].unsqueeze(1).to_broadcast([nk, 4, 128]),
                                                  op=ALU.mult),
                     reads=[bpt, bMASKS], writes=[bpt])

        def back(n, it):
            g, nk, idx = it["g"], it["nk"], it["idx"]
            if it.get("first_back"):
                it["first_back"]()
            pt, bpt = PT4[n % NPT], bPT4[n % NPT]
            ob = it.get("ob", (2 + g) if it["kind"] == "A" else (4 + g))
            fns = []
            for hl in range(4):
                fns.append(("matmul", dict(out=PB[ob][:, hl * 65:(hl + 1) * 65],
                                           lhsT=pt[0:nk, hl * 128:(hl + 1) * 128], rhs=it["vfn"](g * 4 + hl),
                                           start=(idx == 0 and hl == 0), stop=(idx == it["n"] - 1),
                                           skip_group_check=True)))
            P.group("pe", fns, reads=[bpt] + list(it["vb"]), writes=[bPB[ob]])
            if it.get("last"):
                it["last"](n + LA)

        deferred = []

        def epilogue(i, n_now):
            xb = bXT[i % 2]
            obv = O32[:, 512:1024].rearrange("p (m e d) -> p m e d", e=2, d=64)

            def e1():
                for hg in range(2):
                    odt = ODT[i % 2][:, hg * 260:(hg + 1) * 260]
                    if n_q == NQ:
                        P.op("dve", "tensor_tensor", dict(out=odt, in0=PB[2 + hg][:, 0:260], in1=odt, op=ALU.add),
                             reads=[bPB[2 + hg], xb], writes=[xb])
                    else:
                        P.op("dve", "tensor_copy", dict(out=odt, in_=PB[2 + hg][:, 0:260]), reads=[bPB[2 + hg]], writes=[xb])
                for e_kv in range(2):
                    ov = PB[4 + e_kv][:, 0:260].rearrange("p (h d) -> p h d", d=65)
                    esv = ESINK.rearrange("p (m e) -> p m e", e=2)[:, :, e_kv]
                    dsl = slice(8 + e_kv * 4, 8 + (e_kv + 1) * 4)
                    P.op("dve", "tensor_copy", dict(out=obv[:, :, e_kv, :], in_=ov[:, :, 0:64]),
                         reads=[bPB[4 + e_kv]], writes=[bO32])
                    P.op("dve", "tensor_tensor", dict(out=DEN[:, dsl], in0=ov[:, :, 64], in1=esv, op=ALU.add),
                         reads=[bPB[4 + e_kv], bESINK], writes=[bDEN])

            def e2():
                for hg in range(2):
                    ov = ODT[i % 2][:, hg * 260:(hg + 1) * 260].rearrange("p (h d) -> p h d", d=65)
                    P.op("dve", "reciprocal", dict(out=RDEN[:, hg * 4:(hg + 1) * 4], in_=ov[:, :, 64]),
                         reads=[xb], writes=[bRDEN])
                    P.op("dve", "tensor_tensor", dict(
                        out=hv(O32[:, hg * 256:(hg + 1) * 256]), in0=ov[:, :, 0:64],
                        in1=RDEN[:, hg * 4:(hg + 1) * 4].unsqueeze(2).to_broadcast([128, 4, 64]), op=ALU.mult),
                        reads=[xb, bRDEN], writes=[bO32])
                P.op("dve", "reciprocal", dict(out=RDEN[:, 8:16], in_=DEN[:, 8:16]), reads=[bDEN], writes=[bRDEN])
                for e_kv in range(2):
                    dsl = slice(8 + e_kv * 4, 8 + (e_kv + 1) * 4)
                    P.op("dve", "tensor_tensor", dict(
                        out=obv[:, :, e_kv, :], in0=obv[:, :, e_kv, :],
                        in1=RDEN[:, dsl].unsqueeze(2).to_broadcast([128, 4, 64]), op=ALU.mult),
                        reads=[bO32, bRDEN], writes=[bO32])
                if os.environ.get("DUMP_O32"):
                    P.dma("sp", dict(out=yseq[(q0 + i) * 128:(q0 + i + 1) * 128, :], in_=O32), bO32, reads=[bO32])

            def e3():
                for half in range(2):
                    P.op("act", "activation", dict(out=JUNK[:, half * 512:(half + 1) * 512],
                                                   in_=O32[:, half * 512:(half + 1) * 512],
                                                   func=AF.Square, accum_out=SS[:, half:half + 1]),
                         reads=[bO32], writes=[bJUNK, bSS])
                rstd_chain(SS[:, 0:2], LNV[:, 0:2], RSTD[:, 0:2], 512, bSS, bLNV, bRSTD)

            def e4():
                for half in range(2):
                    P.op("dve", "tensor_scalar", dict(out=XS[:, half * 512:(half + 1) * 512],
                                                      in0=O32[:, half * 512:(half + 1) * 512],
                                                      scalar1=RSTD[:, half:half + 1], scalar2=None, op0=ALU.mult),
                         reads=[bO32, bRSTD], writes=[bXS])
                transposes_xs(6)

            def e5():
                evac_T(6, MT[:, :, i * 128:(i + 1) * 128], GOUT, bGOUT, bM[i])

            while deferred:
                deferred.pop(0)[1]()
            e1()
            for dist, fn in zip((1, 7, 11, 14), [e2, e3, e4, e5]):
                deferred.append((n_now + dist, fn))

        n = 0
        while n < len(iters) + LA or deferred:
            if n < len(iters):
                front(n, iters[n])
            if 0 <= n - LA < len(iters):
                back(n - LA, iters[n - LA])
            due = [d for d in deferred if d[0] <= n]
            for d in due:
                deferred.remove(d)
                d[1]()
            n += 1

        P.barrier()
        if stop == "1b":
            break
        def load_w(fb):
            s = fb % 2
            extra = list(bWO) if s == 1 else []
            P.dma("pool", dict(out=WUP[s], in_=w_up[:, fb * 512:(fb + 1) * 512].rearrange("(k p) f -> p k f", p=128)),
                  bWUP[s], writes=[bWUP[s]] + extra)
            for hh in range(2):
                P.dma("pool", dict(out=WDN[s][:, :, hh * 512:(hh + 1) * 512],
                                   in_=w_dn[fb * 512:(fb + 1) * 512, hh * 512:(hh + 1) * 512].rearrange("(c p) n -> p c n", p=128)),
                      bWDN[s][hh], writes=[bWDN[s][hh]] + extra)

        load_w(0)
        slots2a = {}

        def st2a_mm(t):
            hb = (t % 3) * 2
            fns = []
            for k in range(8):
                for half in range(2):
                    fns.append(("matmul", dict(out=PB[hb + half][:, :], lhsT=MT[:, k, t * 128:(t + 1) * 128],
                                               rhs=WO[:, k, half * 512:(half + 1) * 512], start=(k == 0), stop=(k == 7))))
            P.group("pe", fns, reads=[bM[t]] + bWO, writes=[bPB[hb], bPB[hb + 1]])

        XS2 = [XS, JUNK]
        bXS2 = [bXS, bJUNK]
        junk2 = XNT.rearrange("p k t -> p (k t)")

        def st2a_adds(t):
            hb = (t % 3) * 2
            slot = slots2a[t]
            for half in range(2):
                P.op("dve", "tensor_tensor", dict(out=H[:, t, half * 512:(half + 1) * 512], in0=PB[hb + half][:, :],
                                                  in1=XT[slot][:, half * 512:(half + 1) * 512], op=ALU.add),
                     reads=[bPB[hb + half], bXT[slot]], writes=[bH[t]])

        def st2a_act(t):
            xs_, bxs_ = XS2[t % 2], bXS2[t % 2]
            P.op("act", "activation", dict(out=junk2, in_=H[:, t, :], func=AF.Square, accum_out=SS[:, 0:1]),
                 reads=[bH[t]], writes=[bXNT, bSS])
            rstd_chain(SS[:, 0:1], LNV[:, 0:1], RSTD[:, 0:1], D, bSS, bLNV, bRSTD)
            P.op("act", "activation", dict(out=xs_, in_=H[:, t, :], func=AF.Copy, scale=RSTD[:, 0:1]),
                 reads=[bH[t], bRSTD], writes=[bxs_])

        def st2a_T(t):
            xs_, bxs_ = XS2[t % 2], bXS2[t % 2]
            fns = [("transpose", dict(out=PBh[6][:, k * 128:(k + 1) * 128], in_=xs_[:, k * 128:(k + 1) * 128],
                                      identity=IDENT)) for k in range(8)]
            P.group("pe", fns, reads=[bxs_, bIDENT], writes=[bPB[6]])
            evac_T(6, MT[:, :, t * 128:(t + 1) * 128], GMLP, bGMLP, bM[t])

        def st2a_ldx(t):
            tq = q0 + t
            slots2a[t] = load_x(xseq[tq * 128:(tq + 1) * 128, :])

        st2a_ldx(0)
        st2a_ldx(1)
        st2a_mm(0)
        st2a_mm(1)
        st2a_adds(0)
        st2a_act(0)
        for t in range(NQ):
            if t + 2 < NQ:
                st2a_mm(t + 2)
            if t + 1 < NQ:
                st2a_adds(t + 1)
                st2a_act(t + 1)
            if t + 2 < NQ:
                st2a_ldx(t + 2)
            st2a_T(t)

        if stop == "2a":
            P.barrier()
        else:
            for b_ in bUT + bR32:
                for hh in range(2):
                    for sem_, v_ in bWO[hh].r.items():
                        b_.r[sem_] = max(b_.r.get(sem_, 0), v_)
        if stop == "2a":
            for t in range(NQ):
                tq = q0 + t
                P.dma("sp", dict(out=yseq[tq * 128:(tq + 1) * 128, :], in_=H[:, t, :]), bH[t], reads=[bH[t]])
            break
        NFB = DFF // 512
        pairs = [(fb, sg) for fb in range(NFB) for sg in range(4)]
        utb_of = {}

        def up_proj(fb, sg):
            s = fb % 2
            utb = state["u"] % 2
            state["u"] += 1
            utb_of[(fb, sg)] = utb
            for c in range(4):
                ub = state["y"] % 2
                state["y"] += 1
                fns = [("matmul", dict(out=PB[ub][:, :], lhsT=WUP[s][:, k, c * 128:(c + 1) * 128],
                                       rhs=MT[:, k, sg * 512:(sg + 1) * 512], start=(k == 0), stop=(k == 7)))
                       for k in range(8)]
                P.group("pe", fns, reads=[bWUP[s]] + bM[sg * 4:sg * 4 + 4], writes=[bPB[ub]])
                P.op("act", "activation", dict(out=R32[ub], in_=PB[ub][:, :], func=AF.Relu),
                     reads=[bPB[ub]], writes=[bR32[ub]])
                P.op("act", "activation", dict(out=UT[utb][:, c, :], in_=R32[ub], func=AF.Square),
                     reads=[bR32[ub]], writes=[bUT[utb]])

        def down_proj(fb, sg):
            s = fb % 2
            utb = utb_of[(fb, sg)]
            for tl in range(4):
                t = sg * 4 + tl
                for half in range(2):
                    yb = 2 + ((tl * 2 + half) % 4)
                    fns = [("matmul", dict(out=PB[yb][:, :], lhsT=UT[utb][:, c, tl * 128:(tl + 1) * 128],
                                           rhs=WDN[s][:, c, half * 512:(half + 1) * 512], start=(c == 0), stop=(c == 3)))
                           for c in range(4)]
                    P.group("pe", fns, reads=[bUT[utb], bWDN[s][half]], writes=[bPB[yb]])
                    P.op("dve", "tensor_tensor", dict(out=H[:, t, half * 512:(half + 1) * 512], in0=PB[yb][:, :],
                                                      in1=H[:, t, half * 512:(half + 1) * 512], op=ALU.add),
                         reads=[bPB[yb], bH[t]], writes=[bH[t]])
                if fb == NFB - 1:
                    tq = q0 + t
                    P.dma("sp", dict(out=yseq[tq * 128:(tq + 1) * 128, :], in_=H[:, t, :]), bYST, reads=[bH[t]])

        load_w(1)
        up_proj(*pairs[0])
        for j, (fb, sg) in enumerate(pairs):
            if j + 1 < len(pairs):
                up_proj(*pairs[j + 1])
            down_proj(fb, sg)
            if sg == 3 and fb + 2 < NFB:
                load_w(fb + 2)
        if stop is None and n1a is None and gidx_ + 1 < len(groups):
            nsrc, nseq, nq0, nc0, nnctx = groups[gidx_ + 1][:5]
            ncs = 8 if (len(groups[gidx_ + 1]) > 5 and groups[gidx_ + 1][5] == "load") else 0
            nx = xsrc[nsrc][nseq]
            preload[gidx_ + 1] = {}
            for c_ in (ncs, ncs + 1):
                ts_ = nc0 + c_
                preload[gidx_ + 1][c_] = load_x(nx[ts_ * 128:(ts_ + 1) * 128, :])
        P.barrier()

    npad = int(os.environ.get("PADPE", "0"))
    if npad:
        fns = [("matmul", dict(out=PB[7][:, 0:128], lhsT=IDENT, rhs=IDENT, start=True, stop=True)) for _ in range(npad)]
        P.group("pe", fns, reads=[bIDENT], writes=[bPB[7]])
    npad = int(os.environ.get("PADPOOL", "0"))
    for _ in range(npad):
        P.op("pool", "memset", dict(ap=EPST, constant=EPS), writes=[bEPS])
    P.finish()
    P.emit()
    return nc


def prep_weights(norm_attn, w_in, q_norm_a, k_norm_a, q_norm_b, k_norm_b, sink_b,
                 out_norm_a, out_norm_b, w_o, norm_mlp, w_up, w_down):
    f = np.float32
    w_in = np.asarray(w_in, f)[0]
    w_o = np.asarray(w_o, f)[0]
    ar = np.arange
    qb_cols = np.concatenate([1536 + h * 64 + ar(64) for h in PERM_HEADS])
    cols = np.concatenate([ar(0, 512), qb_cols, 512 + ar(512), 1024 + ar(512), 2048 + ar(128), 2176 + ar(128)])
    w_in_p = np.ascontiguousarray(w_in[:, cols])
    rows_b = np.concatenate([512 + h * 64 + ar(64) for h in PERM_HEADS])
    w_o_p = np.ascontiguousarray(w_o[np.concatenate([ar(512), rows_b])])
    gout = np.concatenate([np.asarray(out_norm_a, f)[0], np.asarray(out_norm_b, f)[0][rows_b - 512]])

    def chunked(g):
        return np.ascontiguousarray(np.asarray(g, f).reshape(8, 128).T)

    cos, sin = _rope_tables()
    masks = np.ascontiguousarray(np.concatenate(_MASKS, axis=1)).astype(ml_dtypes.bfloat16)
    return {
        "w_in": w_in_p, "w_o": w_o_p,
        "w_up": np.ascontiguousarray(np.asarray(w_up, f)[0]),
        "w_down": np.ascontiguousarray(np.asarray(w_down, f)[0]),
        "gin": chunked(np.asarray(norm_attn, f)[0]), "gmlp": chunked(np.asarray(norm_mlp, f)[0]),
        "gout": chunked(gout),
        "qkg": np.ascontiguousarray(np.concatenate([np.asarray(q_norm_a, f)[0], np.asarray(q_norm_b, f)[0],
                                                    np.asarray(k_norm_a, f)[0], np.asarray(k_norm_b, f)[0]])[None, :]),
        "sinkp": np.ascontiguousarray(np.asarray(sink_b, f)[0][PERM_HEADS][None, :]),
        "cost": cos, "sint": sin, "masks": masks,
        "ident": np.eye(128, dtype=np.float32).astype(ml_dtypes.bfloat16),
    }


FULL_GROUPS = [("p", 0, 0, 0, 16), ("p", 1, 0, 0, 16), ("s", 0, 0, 0, 24, "save"), ("s", 0, 16, 8, 24, "load")]

_NC_CACHE = {}


def kernel(x_prompt, x_sample, norm_attn, w_in, q_norm_a, k_norm_a, q_norm_b, k_norm_b,
           sink_b, out_norm_a, out_norm_b, w_o, norm_mlp, w_up, w_down):
    n = 8
    x_prompt = np.asarray(x_prompt, np.float32)
    x_sample = np.asarray(x_sample, np.float32)
    shared = prep_weights(norm_attn, w_in, q_norm_a, k_norm_a, q_norm_b, k_norm_b, sink_b,
                          out_norm_a, out_norm_b, w_o, norm_mlp, w_up, w_down)
    if "full" not in _NC_CACHE:
        _NC_CACHE["full"] = build_program(FULL_GROUPS, 2, 2048, 1, 4096)
    nc = _NC_CACHE["full"]
    in_maps = []
    for c in range(n):
        m = dict(shared)
        m["xp"] = np.ascontiguousarray(x_prompt[2 * c:2 * c + 2])
        m["xs"] = np.ascontiguousarray(x_sample[c:c + 1])
        in_maps.append(m)
    res = run_bass_kernel_spmd(nc, in_maps, core_ids=list(range(n)))
    yp = np.concatenate([r["yp"] for r in res.results], axis=0).astype(np.float32)
    ys = np.concatenate([r["ys"] for r in res.results], axis=0).astype(np.float32)
    return (yp, ys)
```
